# Optimizing a Trainium2 kernel written in Bass

```python
import jax, jax.numpy as jnp
from jax import lax
import numpy as np

D_MODEL = 1024
BATCH = 4
SEQ = 4096
DEPTH = 1

PLE_DIM = 256
HEAD_DIM = 64
DILATED_CFG = ((128, 1), (512, 4), (2048, 16))
N_GROUPS = 3
HEADS_PER_GROUP = 8
ATT_HEADS = N_GROUPS * HEADS_PER_GROUP
ATT_QKV = ATT_HEADS * HEAD_DIM
ATT_OUT = HEADS_PER_GROUP * HEAD_DIM
ROT_DIM = HEAD_DIM // 4
ROPE_THETA = 500000.0
GLA_HEADS = 4
GLA_DK = 128
GLA_DV = 256
GLA_KEY = GLA_HEADS * GLA_DK
GLA_VAL = GLA_HEADS * GLA_DV
GLA_GATE_RANK = 16
GLA_TAU = 16.0
GLA_CHUNK = 64
EPS = 1e-6
IN_SPLITS = (ATT_QKV, ATT_QKV, ATT_QKV, ATT_OUT,
             GLA_KEY, GLA_KEY, GLA_VAL, GLA_GATE_RANK, GLA_VAL,
             D_MODEL, D_MODEL)
IN_COLS = sum(IN_SPLITS)

kernel_name = "hybrid_dilated_swa_gla_gated_merge"


def _rmsnorm(x, g):
    xf = x.astype(jnp.float32)
    y = xf * lax.rsqrt(jnp.mean(xf * xf, axis=-1, keepdims=True) + EPS)
    return (y * g.astype(jnp.float32)).astype(x.dtype)


def _head_rmsnorm(t, g):
    return t * lax.rsqrt(jnp.mean(t * t, axis=-1, keepdims=True) + EPS) * g.astype(jnp.float32)


def _rotary_partial(t, pos):
    half = ROT_DIM // 2
    inv = jnp.power(jnp.float32(ROPE_THETA), -jnp.arange(half, dtype=jnp.float32) * 2.0 / ROT_DIM)
    ang = pos[..., None] * inv
    cos = jnp.cos(ang)[:, :, None, :]
    sin = jnp.sin(ang)[:, :, None, :]
    t1 = t[..., :half]
    t2 = t[..., half:ROT_DIM]
    return jnp.concatenate([t1 * cos - t2 * sin, t2 * cos + t1 * sin, t[..., ROT_DIM:]], axis=-1)


def _strided_window_attention(q, k, v, n_steps):
    N, L, H, hd = q.shape
    blk = n_steps
    nb = -(-L // blk)
    pad = nb * blk - L
    if pad:
        pw = ((0, 0), (0, pad), (0, 0), (0, 0))
        q, k, v = jnp.pad(q, pw), jnp.pad(k, pw), jnp.pad(v, pw)
    qb = q.reshape(N, nb, blk, H, hd)
    kb = k.reshape(N, nb, blk, H, hd)
    vb = v.reshape(N, nb, blk, H, hd)
    shift = ((0, 0), (1, 0), (0, 0), (0, 0), (0, 0))
    k2 = jnp.concatenate([jnp.pad(kb, shift)[:, :-1], kb], axis=2)
    v2 = jnp.concatenate([jnp.pad(vb, shift)[:, :-1], vb], axis=2)
    s = jnp.einsum('nbqhd,nbkhd->nbhqk', qb, k2) * (hd ** -0.5)
    a = jnp.arange(blk)[:, None]
    c = jnp.arange(2 * blk)[None, :]
    dist = blk + a - c
    in_band = (dist >= 0) & (dist <= n_steps)
    exists = (jnp.arange(nb)[:, None, None] > 0) | (c >= blk)[None]
    mask = in_band[None] & exists
    s = jnp.where(mask[None, :, None], s, -jnp.inf)
    m = jnp.max(s, axis=-1)
    pr = jnp.exp(s - m[..., None])
    l = jnp.sum(pr, axis=-1)
    o = jnp.einsum('nbhqk,nbkhd->nbqhd', pr, v2) / jnp.swapaxes(l, 2, 3)[..., None]
    o = o.reshape(N, nb * blk, H, hd)[:, :L]
    m = jnp.swapaxes(m, 2, 3).reshape(N, nb * blk, H)[:, :L]
    l = jnp.swapaxes(l, 2, 3).reshape(N, nb * blk, H)[:, :L]
    return o, m, l


def _dilated_group(q, k, v, window, dilation):
    B, S, H, hd = q.shape
    L = S // dilation

    def to_res(t):
        return t.reshape(B, L, dilation, H, hd).transpose(0, 2, 1, 3, 4).reshape(B * dilation, L, H, hd)

    o, m, l = _strided_window_attention(to_res(q), to_res(k), to_res(v), window // dilation)
    o = o.reshape(B, dilation, L, H, hd).transpose(0, 2, 1, 3, 4).reshape(B, S, H, hd)
    m = m.reshape(B, dilation, L, H).transpose(0, 2, 1, 3).reshape(B, S, H)
    l = l.reshape(B, dilation, L, H).transpose(0, 2, 1, 3).reshape(B, S, H)
    return o, m, l


def _dilated_attention(qa, ka, va, pos, gq, gk):
    B, S, _ = qa.shape
    q = _head_rmsnorm(qa.astype(jnp.float32).reshape(B, S, ATT_HEADS, HEAD_DIM), gq)
    k = _head_rmsnorm(ka.astype(jnp.float32).reshape(B, S, ATT_HEADS, HEAD_DIM), gk)
    v = va.astype(jnp.float32).reshape(B, S, ATT_HEADS, HEAD_DIM)
    q = _rotary_partial(q, pos)
    k = _rotary_partial(k, pos)
    q = q.reshape(B, S, N_GROUPS, HEADS_PER_GROUP, HEAD_DIM)
    k = k.reshape(B, S, N_GROUPS, HEADS_PER_GROUP, HEAD_DIM)
    v = v.reshape(B, S, N_GROUPS, HEADS_PER_GROUP, HEAD_DIM)
    outs, maxs, dens = [], [], []
    for g, (window, dilation) in enumerate(DILATED_CFG):
        o, m, l = _dilated_group(q[:, :, g], k[:, :, g], v[:, :, g], window, dilation)
        outs.append(o); maxs.append(m); dens.append(l)
    o = jnp.stack(outs)
    m = jnp.stack(maxs)
    l = jnp.stack(dens)
    w = l * jnp.exp(m - jnp.max(m, axis=0, keepdims=True))
    out = jnp.sum(w[..., None] * o, axis=0) / jnp.sum(w, axis=0)[..., None]
    return out.reshape(B, S, ATT_OUT)


def _gla_chunk_step(state, inp):
    q, k, v, lg = inp
    cum = jnp.cumsum(lg, axis=2)
    o_inter = jnp.einsum('bhid,bhde->bhie', q * jnp.exp(cum), state)
    C = q.shape[2]
    causal = jnp.tril(jnp.ones((C, C), dtype=bool))
    diff = cum[:, :, :, None, :] - cum[:, :, None, :, :]
    decay = jnp.exp(jnp.where(causal[:, :, None], diff, -jnp.inf))
    att = jnp.einsum('bhid,bhjd,bhijd->bhij', q, k, decay)
    o_intra = jnp.einsum('bhij,bhje->bhie', att, v)
    last = cum[:, :, -1:, :]
    new_state = jnp.exp(last[:, :, 0, :])[..., None] * state + \
        jnp.einsum('bhjd,bhje->bhde', k * jnp.exp(last - cum), v)
    return new_state, o_intra + o_inter


def _gla(qg, kg, vg, glr, w_g2, b_g, g_norm):
    B, S, _ = qg.shape
    nc = S // GLA_CHUNK
    q = qg.astype(jnp.float32).reshape(B, S, GLA_HEADS, GLA_DK) * (GLA_DK ** -0.5)
    k = kg.astype(jnp.float32).reshape(B, S, GLA_HEADS, GLA_DK)
    v = vg.astype(jnp.float32).reshape(B, S, GLA_HEADS, GLA_DV)
    logit = (glr @ w_g2 + b_g).astype(jnp.float32)
    lg = (jax.nn.log_sigmoid(logit) / GLA_TAU).reshape(B, S, GLA_HEADS, GLA_DK)

    def chunks(t):
        return t.reshape(B, nc, GLA_CHUNK, GLA_HEADS, t.shape[-1]).transpose(1, 0, 3, 2, 4)

    s0 = jnp.zeros((B, GLA_HEADS, GLA_DK, GLA_DV), jnp.float32)
    _, o = lax.scan(_gla_chunk_step, s0, (chunks(q), chunks(k), chunks(v), chunks(lg)))
    o = o.transpose(1, 0, 3, 2, 4).reshape(B, S, GLA_HEADS, GLA_DV)
    o = _head_rmsnorm(o, g_norm)
    return o.reshape(B, S, GLA_VAL)


def setup_inputs(seed: int = 0) -> dict:
    key = jax.random.key(seed)
    ks = jax.random.split(key, 18)
    f32 = jnp.float32

    def nrm(k, shape, fan_in):
        return jax.random.normal(k, shape, f32) * (fan_in ** -0.5)

    def gain(k, shape):
        return 1.0 + 0.1 * jax.random.normal(k, shape, f32)

    return {
        "x": jax.random.normal(ks[0], (BATCH, SEQ, D_MODEL), f32),
        "p": jax.random.normal(ks[1], (DEPTH, BATCH, SEQ, PLE_DIM), f32),
        "positions": jnp.broadcast_to(jnp.arange(SEQ, dtype=jnp.int32), (BATCH, SEQ)),
        "norm_g": gain(ks[2], (DEPTH, D_MODEL)),
        "w_in": nrm(ks[3], (DEPTH, D_MODEL, IN_COLS), D_MODEL),
        "qk_norm_q": gain(ks[4], (DEPTH, HEAD_DIM)),
        "qk_norm_k": gain(ks[5], (DEPTH, HEAD_DIM)),
        "gla_gate_w2": nrm(ks[6], (DEPTH, GLA_GATE_RANK, GLA_KEY), GLA_GATE_RANK),
        "gla_gate_b": 0.1 * jax.random.normal(ks[7], (DEPTH, GLA_KEY), f32),
        "gla_norm_g": gain(ks[8], (DEPTH, GLA_DV)),
        "w_att_proj": nrm(ks[9], (DEPTH, ATT_OUT, D_MODEL), ATT_OUT),
        "w_gla_proj": nrm(ks[10], (DEPTH, GLA_VAL, D_MODEL), GLA_VAL),
        "w_out": nrm(ks[11], (DEPTH, D_MODEL, D_MODEL), D_MODEL),
        "ple_norm_g": gain(ks[12], (DEPTH, D_MODEL)),
        "w_ple_gate": nrm(ks[13], (DEPTH, D_MODEL, D_MODEL), D_MODEL),
        "w_ple": nrm(ks[14], (DEPTH, PLE_DIM, D_MODEL), PLE_DIM),
    }


def reference(x, p, positions, norm_g, w_in, qk_norm_q, qk_norm_k, gla_gate_w2, gla_gate_b,
              gla_norm_g, w_att_proj, w_gla_proj, w_out, ple_norm_g, w_ple_gate, w_ple):
    pos = positions.astype(jnp.float32)
    split_idx = tuple(int(s) for s in np.cumsum(IN_SPLITS)[:-1])
    for i in range(DEPTH):
        h = _rmsnorm(x, norm_g[i])
        proj = h @ w_in[i]
        qa, ka, va, za, qg, kg, vg, glr, zg, gate_a, gate_b = jnp.split(proj, split_idx, axis=-1)
        att = _dilated_attention(qa, ka, va, pos, qk_norm_q[i], qk_norm_k[i]).astype(x.dtype)
        y_a = (att * jax.nn.silu(za)) @ w_att_proj[i]
        lin = _gla(qg, kg, vg, glr, gla_gate_w2[i], gla_gate_b[i], gla_norm_g[i]).astype(x.dtype)
        y_b = (lin * jax.nn.silu(zg)) @ w_gla_proj[i]
        y = jax.nn.sigmoid(gate_a) * y_a + jax.nn.sigmoid(gate_b) * y_b
        x = x + y @ w_out[i]
        ple_gate = jax.nn.sigmoid(_rmsnorm(x, ple_norm_g[i]) @ w_ple_gate[i])
        x = x + (p[i] @ w_ple[i]) * ple_gate
    return x
```

```python
import numpy as np
import ml_dtypes
import concourse.bass as bass
import concourse.mybir as mybir
from concourse.bass_utils import run_bass_kernel_spmd

F32 = mybir.dt.float32
BF16 = mybir.dt.bfloat16
I32 = mybir.dt.int32
AF = mybir.ActivationFunctionType
ALU = mybir.AluOpType
AX = mybir.AxisListType
NPBF = ml_dtypes.bfloat16

D = 1024
T = 2048
NT = T // 128
EPS = 1e-6
C_QA, C_KA, C_VA, C_ZA, C_QG, C_KG, C_VG, C_GLR, C_ZG, C_GA, C_GB = (
    0, 1536, 3072, 4608, 5120, 5632, 6144, 7168, 7184, 8208, 9232)
GROUPS = ((16, 2), (4, 1), (1, 0))
NEG = -30000.0


class Sched:
    def __init__(self, nc):
        self.nc = nc
        self.E = {"pe": nc.tensor, "dve": nc.vector, "act": nc.scalar,
                  "pool": nc.gpsimd, "sp": nc.sync}
        self.sems, self.cnt, self.waited = {}, {}, {}
        self.lastw, self.readers = {}, {}
        self.ninst = 0
        self.nwait = 0
        for e in ("pe", "dve", "act", "pool"):
            self._sem("E_" + e)

    def _sem(self, key):
        if key not in self.sems:
            self.sems[key] = self.nc.alloc_semaphore("s_" + key)
            self.cnt[key] = 0
        return self.sems[key]

    def _need(self, eng, toks):
        best = {}
        for t in toks:
            if t is None:
                continue
            k, v = t
            if v > best.get(k, 0):
                best[k] = v
        for k, v in best.items():
            if eng == "pe" and k == "E_pe":
                continue
            if self.waited.get((eng, k), 0) >= v:
                continue
            self.E[eng].wait_ge(self.sems[k], v)
            self.waited[(eng, k)] = v
            self.nwait += 1

    def deps(self, eng, reads, writes):
        toks = []
        for r in reads:
            toks.append(self.lastw.get(r))
        for w in writes:
            toks.append(self.lastw.get(w))
            toks.extend(self.readers.get(w, ()))
        self._need(eng, toks)

    def commit(self, tok, reads, writes):
        for r in reads:
            self.readers.setdefault(r, []).append(tok)
        for w in writes:
            self.lastw[w] = tok
            self.readers[w] = []

    def op(self, eng, reads, writes, fn):
        self.deps(eng, reads, writes)
        inst = fn()
        k = "E_" + eng
        self.cnt[k] += 1
        inst.then_inc(self.sems[k], 1)
        self.commit((k, self.cnt[k]), reads, writes)
        self.ninst += 1

    def dma(self, q, slot, reads, writes, out, in_):
        self.deps(q, reads, writes)
        k = "D_" + slot
        self._sem(k)
        inst = self.E[q].dma_start(out=out, in_=in_)
        self.cnt[k] += 16
        inst.then_inc(self.sems[k], 16)
        self.commit((k, self.cnt[k]), reads, writes)
        self.ninst += 1

    def pe_drain(self):
        v = self.cnt["E_pe"]
        if v > self.waited.get(("pe", "E_pe"), 0):
            self.E["pe"].wait_ge(self.sems["E_pe"], v)
            self.waited[("pe", "E_pe")] = v
            self.nwait += 1

    def barrier(self):
        toks = [(k, v) for k, v in self.cnt.items() if v > 0]
        for e in ("pe", "dve", "act", "pool", "sp"):
            self._need(e, toks)
        self.lastw.clear()
        self.readers.clear()


class Ring:
    def __init__(self, name, aps):
        self.name, self.aps, self.i = name, aps, 0

    def next(self):
        j = self.i % len(self.aps)
        self.i += 1
        return "%s%d" % (self.name, j), self.aps[j]


def _blocks():
    out = {}
    bid = 0
    for d, gi in GROUPS:
        lst = []
        for r in range(d):
            for B in range(16 // d - 1, 32 // d):
                lst.append((r, B, B == 16 // d - 1, bid))
                bid += 1
        out[d] = lst
    return out, bid


def build_program(debug=False, stop_after=99):
    nc = bass.Bass("TRN2", target_bir_lowering=False)
    S = Sched(nc)
    blocks, NB = _blocks()

    def din(name, shape, dt):
        return nc.dram_tensor(name, list(shape), dt, kind="ExternalInput").ap()

    xs = din("xs", [4096, D], F32)
    posb = din("posb", [128, NB], I32)
    invf = din("invf", [128, 8], F32)
    pT_d = din("pT", [256, T], F32)
    consts_d = din("cbf", [128, 13 * 128], BF16)
    trif_d = din("trif", [128, 128], F32)
    ng_d = din("ng", [128, 8], F32)
    png_d = din("png", [128, 8], F32)
    gq_d = din("gqb", [128, 64], F32)
    gk_d = din("gkb", [128, 64], F32)
    gn_d = din("gnb", [128, 256], F32)
    w2_d = din("w2aug", [32, 512], F32)
    w_in = din("w_in", [D, 10256], F32)
    w_pa = din("w_pa", [512, D], F32)
    w_pb = din("w_pb", [D, D], F32)
    w_out = din("w_out", [D, D], F32)
    w_pg = din("w_pg", [D, D], F32)
    w_ple = din("w_ple", [256, D], F32)
    out_d = nc.dram_tensor("out", [T, D], F32, kind="ExternalOutput").ap()
    scr = {4: nc.dram_tensor("scr4", [T, 520], BF16, kind="Internal").ap(),
           16: nc.dram_tensor("scr16", [T, 520], BF16, kind="Internal").ap()}
    dbg = {}
    if debug:
        dbg["ua"] = nc.dram_tensor("dbg_ua", [128, 4, T], BF16, kind="ExternalOutput").ap()
        dbg["ub"] = nc.dram_tensor("dbg_ub", [128, 8, T], BF16, kind="ExternalOutput").ap()
        dbg["y"] = nc.dram_tensor("dbg_y", [128, 8, T], BF16, kind="ExternalOutput").ap()

    _CNT = [0]

    class Arena:
        def __init__(self, base):
            self.top = base
            self.n = 0

        def alloc(self, shape, dt):
            nbytes = int(np.prod(shape[1:])) * (4 if dt in (F32, I32) else 2)
            off = (self.top + 63) // 64 * 64
            self.top = off + nbytes
            self.n += 1
            assert self.top <= 229344, ("SBUF overflow", self.top)
            _CNT[0] += 1
            return nc.alloc_sbuf_tensor_at("sb%d" % _CNT[0], list(shape), dt, offset=off).ap()

    P = Arena(16512)
    cbf = P.alloc([128, 13 * 128], BF16)
    ident, m_prev, m_cur, m_halo, tri = (cbf[:, i * 128:(i + 1) * 128] for i in range(5))
    mask4 = cbf[:, 5 * 128:9 * 128]
    mask4h = cbf[:, 9 * 128:13 * 128]
    trif = P.alloc([128, 128], F32)
    ngT = P.alloc([128, 8], F32)
    pngT = P.alloc([128, 8], F32)
    gqb = P.alloc([128, 64], F32)
    gkb = P.alloc([128, 64], F32)
    gnb = P.alloc([128, 256], F32)
    w2aug = P.alloc([32, 512], F32)
    hTo = P.alloc([128, 8, T], BF16)
    hTh = P.alloc([128, 8, T], BF16)
    uaT = P.alloc([128, 4, T], BF16)
    XBASE = P.top

    banks = [nc.alloc_psum_tensor("bank%d" % i, [128, 512], F32).ap() for i in range(8)]

    def bf_view(bank):
        return bank.bitcast(BF16)

    for i, (dst, src) in enumerate(((cbf, consts_d), (trif, trif_d), (ngT, ng_d), (pngT, png_d), (gqb, gq_d),
                                    (gkb, gk_d), (gnb, gn_d), (w2aug, w2_d))):
        S.dma("sp", "c%d" % i, [], ["const"], dst, src)
    S.barrier()

    def make_loader(arena, width=1024, nslot=3, name="stg"):
        stg = Ring(name, [arena.alloc([128, width], F32) for _ in range(nslot)])

        def load_w(dst, key, src, ncols, nchunk, scale=None, eng="dve"):
            for c in range(nchunk):
                for c0 in range(0, ncols, width):
                    cw = min(width, ncols - c0)
                    sk, st = stg.next()
                    S.dma("sp", sk, [], [sk], st[:, 0:cw], src[c * 128:(c + 1) * 128, c0:c0 + cw])
                    o = dst[:, c, c0:c0 + cw]
                    if scale is None:
                        S.op(eng, [sk], [key], lambda: S.E[eng].tensor_copy(out=o, in_=st[:, 0:cw]))
                    else:
                        S.op(eng, [sk, "const"], [key], lambda: S.E[eng].tensor_scalar(
                            out=o, in0=st[:, 0:cw], scalar1=scale[:, c:c + 1], scalar2=None, op0=ALU.mult))
        def load_steps(dst, key, src, ncols, nchunk, scale=None, eng="dve", q="sp"):
            steps = []
            for c in range(nchunk):
                for c0 in range(0, ncols, width):
                    cw = min(width, ncols - c0)

                    def dma(c=c, c0=c0, cw=cw):
                        sk, st = stg.next()
                        S.dma(q, sk, [], [sk], st[:, 0:cw], src[c * 128:(c + 1) * 128, c0:c0 + cw])
                        return sk, st

                    def cast(h, c=c, c0=c0, cw=cw):
                        sk, st = h
                        o = dst[:, c, c0:c0 + cw]
                        if scale is None:
                            if eng == "act":
                                S.op("act", [sk], [key], lambda: nc.scalar.activation(out=o, in_=st[:, 0:cw], func=AF.Copy))
                            else:
                                S.op(eng, [sk], [key], lambda: S.E[eng].tensor_copy(out=o, in_=st[:, 0:cw]))
                        elif eng == "act":
                            S.op("act", [sk, "const"], [key], lambda: nc.scalar.activation(
                                out=o, in_=st[:, 0:cw], func=AF.Copy, scale=scale[:, c:c + 1]))
                        else:
                            S.op(eng, [sk, "const"], [key], lambda: S.E[eng].tensor_scalar(
                                out=o, in0=st[:, 0:cw], scalar1=scale[:, c:c + 1], scalar2=None, op0=ALU.mult))
                    steps.append((dma, cast))
            return steps
        load_w.steps = load_steps
        load_w.nslot = nslot
        return load_w

    class Prefetch:
        def __init__(self):
            self.q = []
            self.inflight = []

        def add(self, steps):
            self.q.extend(steps)

        def _casts(self):
            for cast, h in self.inflight:
                cast(h)
            self.inflight = []

        def pump(self, n=1):
            self._casts()
            for _ in range(min(n, len(self.q))):
                dma, cast = self.q.pop(0)
                self.inflight.append((cast, dma()))

        def flush(self, ring=2):
            while self.q or self.inflight:
                self.pump(ring)

    pf = Prefetch()
    AW = Arena(XBASE)
    wbuf = [AW.alloc([128, 8, 1536], BF16) for _ in range(2)]
    XW = AW.top

    def group_w_steps(loader, gidx, eng, q="sp"):
        d_, gi_ = GROUPS[gidx]
        b = wbuf[gidx % 2]
        st = []
        for (o0, c0) in ((0, C_QA + gi_ * 512), (512, C_KA + gi_ * 512), (1024, C_VA + gi_ * 512)):
            st += loader.steps(b[:, :, o0:o0 + 512], "W%d" % (gidx % 2), w_in[:, c0:c0 + 512], 512, 8, scale=ngT, eng=eng, q=q)
        return st

    AT = Arena(XW)
    cos2T = AT.alloc([128, NB, 16], F32)
    sin2T = AT.alloc([128, NB, 16], F32)
    XT = AT.top
    A1 = Arena(XT)
    load_w1 = make_loader(A1, 512, 3, name="stgp")
    pf.add(group_w_steps(load_w1, 0, "dve", q="pool"))
    xt_r = Ring("xt", [A1.alloc([128, D], F32) for _ in range(8)])
    xn_r = Ring("xn", [A1.alloc([128, D], BF16) for _ in range(4)])
    junk = A1.alloc([128, D], BF16)
    ss1 = A1.alloc([128, 32], F32)
    posi = A1.alloc([128, NB], I32)
    posf = A1.alloc([128, NB], F32)
    ang = A1.alloc([128, NB, 8], F32)
    ang2 = A1.alloc([128, NB, 8], F32)
    invt = A1.alloc([128, 8], F32)
    pring1 = Ring("pb", [(banks[i]) for i in range(4)])

    S.dma("sp", "posi", [], ["posi"], posi, posb)
    S.dma("sp", "invt", [], ["invt"], invt, invf)
    S.op("dve", ["posi"], ["posf"], lambda: nc.vector.tensor_copy(out=posf, in_=posi))
    S.op("dve", ["posf", "invt"], ["ang"], lambda: nc.vector.tensor_tensor(
        out=ang, in0=posf.to_broadcast([128, NB, 8]), in1=invt.unsqueeze(1).to_broadcast([128, NB, 8]), op=ALU.mult))
    PI = float(np.pi)
    MAGIC = 12582912.0
    C1 = 6.28125
    C2 = 2.0 * PI - C1
    kf = A1.alloc([128, NB, 8], F32)
    for (dst, shift, sgn) in ((sin2T, 0.0, -1.0), (cos2T, 0.5 * PI, 1.0)):
        S.op("dve", ["ang"], ["ang2"], lambda: nc.vector.tensor_scalar(
            out=ang2, in0=ang, scalar1=shift, scalar2=None, op0=ALU.add))
        S.op("dve", ["ang2"], ["kf"], lambda: nc.vector.tensor_scalar(
            out=kf, in0=ang2, scalar1=1.0 / (2.0 * PI), scalar2=MAGIC, op0=ALU.mult, op1=ALU.add))
        S.op("dve", ["kf"], ["kf"], lambda: nc.vector.tensor_scalar(
            out=kf, in0=kf, scalar1=-MAGIC, scalar2=None, op0=ALU.add))
        S.op("dve", ["kf", "ang2"], ["ang2"], lambda: nc.vector.scalar_tensor_tensor(
            out=ang2.rearrange("p a b -> p (a b)"), in0=kf.rearrange("p a b -> p (a b)"), scalar=-C1,
            in1=ang2.rearrange("p a b -> p (a b)"), op0=ALU.mult, op1=ALU.add))
        S.op("dve", ["kf", "ang2"], ["ang2"], lambda: nc.vector.scalar_tensor_tensor(
            out=ang2.rearrange("p a b -> p (a b)"), in0=kf.rearrange("p a b -> p (a b)"), scalar=-C2,
            in1=ang2.rearrange("p a b -> p (a b)"), op0=ALU.mult, op1=ALU.add))
        S.op("dve", ["ang2"], ["ang2"], lambda: nc.vector.tensor_scalar(
            out=ang2, in0=ang2, scalar1=-PI, scalar2=PI, op0=ALU.max, op1=ALU.min))
        S.op("act", ["ang2"], ["rot"], lambda: nc.scalar.activation(out=dst[:, :, 0:8], in_=ang2, func=AF.Sin, scale=sgn))
        S.op("act", ["ang2"], ["rot"], lambda: nc.scalar.activation(out=dst[:, :, 8:16], in_=ang2, func=AF.Sin))

    def stageA1(j):
        xk, xt = xt_r.next()
        S.dma("sp", xk, [], [xk], xt, xs[j * 128:(j + 1) * 128, :])
        S.op("act", [xk], ["junk", "ss1_%d" % j], lambda: nc.scalar.activation(
            out=junk, in_=xt, func=AF.Square, accum_out=ss1[:, j:j + 1]))
        S.op("act", ["ss1_%d" % j], ["ss1_%d" % j], lambda: nc.scalar.activation(
            out=ss1[:, j:j + 1], in_=ss1[:, j:j + 1], func=AF.Sqrt, scale=1.0 / D, bias=EPS))
        S.op("dve", ["ss1_%d" % j], ["ss1_%d" % j], lambda: nc.vector.reciprocal(out=ss1[:, j:j + 1], in_=ss1[:, j:j + 1]))
        nk, xn = xn_r.next()
        S.op("dve", [xk, "ss1_%d" % j], [nk], lambda: nc.vector.tensor_scalar(
            out=xn, in0=xt, scalar1=ss1[:, j:j + 1], scalar2=None, op0=ALU.mult))
        return nk, xn

    def stageB1(j, nk, xn):
        bk, bank = pring1.next()
        pv = bf_view(bank)[:, 0:1024].rearrange("p (c t) -> p c t", t=128)
        for c in range(8):
            S.op("pe", [nk, "const"], [bk], lambda: nc.tensor.transpose(
                out=pv[:, c, :], in_=xn[:, c * 128:(c + 1) * 128], identity=ident))
        dst = (hTh if j < 16 else hTo)[:, :, (j % 16) * 128:(j % 16 + 1) * 128]
        if j % 2 == 0:
            S.op("act", [bk], ["hT%d" % j], lambda: nc.scalar.activation(out=dst, in_=pv, func=AF.Copy))
        else:
            S.op("dve", [bk], ["hT%d" % j], lambda: nc.vector.tensor_copy(out=dst, in_=pv))

    st1 = {0: stageA1(0), 1: stageA1(1)}
    for j in range(32):
        if j + 2 < 32:
            st1[j + 2] = stageA1(j + 2)
        stageB1(j, *st1.pop(j))
        pf.pump(1)
    pf.flush()
    S.barrier()

    if stop_after <= 1:
        return nc, S

    def hsrc(L0, n, step, c):
        src, s = (hTh, L0) if L0 < T else (hTo, L0 - T)
        return src[:, c, s:s + (n - 1) * step + 1:step]

    A2 = Arena(XT)
    load_w = make_loader(A2, 512, 3)
    Wsel = {}
    ost_r = Ring("ost", [A2.alloc([128, 2, 260], BF16) for _ in range(3)])
    nat_r = {4: Ring("nat4_", [A2.alloc([128, 2, 260], BF16) for _ in range(4)]),
             16: Ring("nat16_", [A2.alloc([128, 2, 260], BF16) for _ in range(4)])}
    sqs_r = Ring("sqs", [A2.alloc([128, 512], F32) for _ in range(2)])
    ssq_r = Ring("ssq", [A2.alloc([128, 8], F32) for _ in range(4)])
    tmpn_r = Ring("tmpn", [A2.alloc([128, 8, 64], F32) for _ in range(4)])
    t16_r = Ring("t16", [A2.alloc([128, 8, 16], F32) for _ in range(2)])
    rtmp_r = Ring("rtmp", [A2.alloc([128, 2, 8, 16], F32) for _ in range(2)])
    qn_r = Ring("qn", [A2.alloc([128, 8, 64], BF16) for _ in range(3)])
    kn_r = Ring("kn", [A2.alloc([128, 8, 64], BF16) for _ in range(3)])
    qT_r = Ring("qT", [A2.alloc([128, 4, 128], BF16) for _ in range(4)])
    kT_r = Ring("kT", [A2.alloc([128, 4, 128], BF16) for _ in range(5)])
    V_r = Ring("V", [A2.alloc([128, 8, 65], BF16) for _ in range(7)])
    PT_r = Ring("PT", [A2.alloc([128, 512], BF16) for _ in range(4)])
    pmA = A2.alloc([128, 2, 260], F32)
    rl = A2.alloc([128, 8], F32)
    uab_r = Ring("uab", [A2.alloc([128, 8, 64], BF16) for _ in range(2)])
    pending_ua = []
    pring = Ring("pb", [banks[i] for i in range(3)])
    trbank = ("pb3", banks[3])
    sring = {16: Ring("sb", [banks[4], banks[5]]), 4: Ring("sb", [banks[4], banks[5]]), 1: Ring("sb", [banks[4], banks[5]])}
    PVb = [banks[6], banks[7]]

    for Vt0 in V_r.aps:
        S.op("pool", [], ["Vinit"], lambda: nc.gpsimd.memset(Vt0[:, :, 64:65], 1.0))
    S.barrier()

    def norm_early(ps_key, ps):
        sk, sqs = sqs_r.next()
        tk, raw = tmpn_r.next()
        S.op("act", [ps_key], [sk], lambda: nc.scalar.activation(out=sqs, in_=ps, func=AF.Square))
        S.op("act", [ps_key], [tk], lambda: nc.scalar.activation(
            out=raw.rearrange("p h e -> p (h e)"), in_=ps, func=AF.Copy))
        return (sk, sqs, tk, raw)

    def norm_steps(early, gb, bid, out_ring):
        sk, sqs, tk, raw = early
        qk_, ssq = ssq_r.next()
        ok, on = out_ring.next()
        k16, t16 = t16_r.next()
        rk, rt = rtmp_r.next()
        c2 = cos2T[:, bid, :].unsqueeze(1).to_broadcast([128, 8, 16])
        s2 = sin2T[:, bid, :].rearrange("p (two e) -> p two e", two=2).unsqueeze(1).to_broadcast([128, 8, 2, 8])
        sw = t16.rearrange("p h (two e) -> p h two e", two=2)[:, :, ::-1, :]
        steps = [
            lambda: S.op("dve", [sk], [qk_], lambda: nc.vector.tensor_reduce(
                out=ssq, in_=sqs.rearrange("p (h e) -> p h e", e=64), axis=AX.X, op=ALU.add)),
            lambda: S.op("act", [qk_], [qk_], lambda: nc.scalar.activation(
                out=ssq, in_=ssq, func=AF.Ln, scale=1.0 / 64, bias=EPS)),
            lambda: S.op("act", [qk_], [qk_], lambda: nc.scalar.activation(out=ssq, in_=ssq, func=AF.Exp, scale=-0.5)),
            lambda: S.op("dve", [tk, qk_], [tk], lambda: nc.vector.tensor_tensor(
                out=raw, in0=raw, in1=ssq.to_broadcast([128, 8, 64]), op=ALU.mult)),
            lambda: S.op("dve", [tk], [k16], lambda: nc.vector.tensor_tensor(
                out=t16, in0=raw[:, :, 0:16], in1=gb[:, 0:16].unsqueeze(1).to_broadcast([128, 8, 16]), op=ALU.mult)),
            lambda: S.op("dve", [tk], [ok], lambda: nc.vector.tensor_tensor(
                out=on[:, :, 16:64], in0=raw[:, :, 16:64], in1=gb[:, 16:64].unsqueeze(1).to_broadcast([128, 8, 48]), op=ALU.mult)),
        ]
        steps.append(lambda: S.op("dve", [k16, "rot"], [rk], lambda: nc.vector.tensor_tensor(
            out=rt[:, 0], in0=t16, in1=c2, op=ALU.mult)))
        steps.append(lambda: S.op("dve", [k16, "rot"], [rk], lambda: nc.vector.tensor_tensor(
            out=rt[:, 1].rearrange("p h (two e) -> p h two e", two=2), in0=sw, in1=s2, op=ALU.mult)))
        steps.append(lambda: S.op("dve", [rk], [ok], lambda: nc.vector.tensor_tensor(
            out=on[:, :, 0:16], in0=rt[:, 0], in1=rt[:, 1], op=ALU.add)))
        return ok, on, steps

    def stageA(d, blk):
        r, B, halo, bid = blk
        L0 = r + d * 128 * B
        info = {"blk": blk, "halo": halo}
        kb, kbank = pring.next()
        for c in range(8):
            S.op("pe", [Wsel["key"]], [kb], lambda: nc.tensor.matmul(
                kbank, lhsT=hsrc(L0, 128, d, c), rhs=Wsel["k"][:, c, :], start=(c == 0), stop=(c == 7)))
        info["kearly"] = norm_early(kb, kbank)
        vb, vbank = pring.next()
        for c in range(8):
            S.op("pe", [Wsel["key"]], [vb], lambda: nc.tensor.matmul(
                vbank, lhsT=hsrc(L0, 128, d, c), rhs=Wsel["v"][:, c, :], start=(c == 0), stop=(c == 7)))
        Vk, Vt = V_r.next()
        S.op("act", [vb, "Vinit"], [Vk], lambda: nc.scalar.activation(
            out=Vt[:, :, 0:64], in_=vbank.rearrange("p (h e) -> p h e", e=64), func=AF.Copy))
        info["V"] = (Vk, Vt)
        if not halo:
            qb, qbank = pring.next()
            for c in range(8):
                S.op("pe", [Wsel["key"]], [qb], lambda: nc.tensor.matmul(
                    qbank, lhsT=hsrc(L0, 128, d, c), rhs=Wsel["q"][:, c, :], start=(c == 0), stop=(c == 7)))
            info["qearly"] = norm_early(qb, qbank)
            if d == 1:
                J0 = B - 16
                for dd in (4, 16):
                    nk_, nt_ = nat_r[dd].next()
                    S.dma("sp", nk_, [], [nk_], nt_.rearrange("p b e -> p (b e)"), scr[dd][J0 * 128:(J0 + 1) * 128, :])
                    info["nat%d" % dd] = (nk_, nt_)
        return info

    def stageA2(info):
        bid = info["blk"][3]
        knk, kn, ksteps = norm_steps(info["kearly"], gkb, bid, kn_r)
        info["kn"] = (knk, kn)
        qsteps = []
        if "qearly" in info:
            qnk, qn, qsteps = norm_steps(info["qearly"], gqb, bid, qn_r)
            info["qn"] = (qnk, qn)
        for i in range(max(len(ksteps), len(qsteps))):
            if i < len(ksteps):
                ksteps[i]()
            if i < len(qsteps):
                qsteps[i]()

    def stageB(info):
        bk, bank = trbank
        pvw = bf_view(bank).rearrange("p (w c t) -> p w c t", w=2, t=128)
        todo = []
        for w, name, ring, eng in ((0, "kn", kT_r, "dve"), (1, "qn", qT_r, "act")):
            if name not in info:
                continue
            sk, src = info[name]
            flat = src.rearrange("p h e -> p (h e)")
            for c in range(4):
                S.op("pe", [sk, "const"], [bk], lambda: nc.tensor.transpose(
                    out=pvw[:, w, c, :], in_=flat[:, c * 128:(c + 1) * 128], identity=ident))
            todo.append((w, ring, eng))
        info["evac_todo"] = (bk, pvw, todo)

    def stageB_evac(info):
        bk, pvw, todo = info.pop("evac_todo")
        for w, ring, eng in todo:
            ok, o = ring.next()
            if eng == "act":
                S.op("act", [bk], [ok, bk], lambda: nc.scalar.activation(out=o, in_=pvw[:, w], func=AF.Copy))
            else:
                S.op("dve", [bk], [ok, bk], lambda: nc.vector.tensor_copy(out=o, in_=pvw[:, w]))
            info["kT" if w == 0 else "qT"] = (ok, o)

    def stageC(d, info, prev, mid=None, late=None):
        r, B, halo, bid = info["blk"]
        Bo = B - 16 // d
        J = Bo
        kTk, kT = info["kT"]
        qTk, qT = info["qT"]
        Vk, Vt = info["V"]
        pkTk, pkT = prev["kT"]
        pVk, pV = prev["V"]
        msk = mask4h if prev["halo"] else mask4

        def scores2(st):
            xb, xbank = sring[d].next()
            yb, ybank = sring[d].next()
            S.op("pe", ["const"], [xb], lambda: nc.tensor.matmul(
                xbank, lhsT=ident, rhs=msk, start=True, stop=False, skip_group_check=True))
            S.op("pe", ["const"], [yb], lambda: nc.tensor.matmul(
                ybank, lhsT=ident, rhs=msk, start=True, stop=False, skip_group_check=True))
            for pi in range(2):
                hp = 2 * st + pi
                for (kk, ktile, kkey) in ((0, pkT, pkTk), (1, kT, kTk)):
                    reg = slice((2 * pi + kk) * 128, (2 * pi + kk + 1) * 128)
                    last = (pi == 1 and kk == 1)
                    S.op("pe", [kkey, qTk], [xb], lambda: nc.tensor.matmul(
                        xbank[:, reg], lhsT=ktile[0:64, hp, :], rhs=qT[0:64, hp, :], start=False, stop=last,
                        skip_group_check=True))
                    S.op("pe", [kkey, qTk], [yb], lambda: nc.tensor.matmul(
                        ybank[:, reg], lhsT=ktile[64:128, hp, :], rhs=qT[64:128, hp, :], start=False, stop=last,
                        skip_group_check=True))
            res = []
            for (bk_, bank_) in ((xb, xbank), (yb, ybank)):
                Pk, PT = PT_r.next()
                S.op("act", [bk_], [Pk], lambda: nc.scalar.activation(out=PT, in_=bank_, func=AF.Exp, scale=0.125))
                res.append((Pk, PT))
            return res

        def pv2(st, which, Pk, PT):
            for pi in range(2):
                h = 2 * (2 * st + pi) + which
                reg = PVb[h // 4][:, (h % 4) * 65:(h % 4) * 65 + 65]
                S.op("pe", [Pk, pVk], ["PV%d" % (h // 4)], lambda: nc.tensor.matmul(
                    reg, lhsT=PT[:, (2 * pi) * 128:(2 * pi + 1) * 128], rhs=pV[:, h, :], start=True, stop=False))
                S.op("pe", [Pk, Vk], ["PV%d" % (h // 4)], lambda: nc.tensor.matmul(
                    reg, lhsT=PT[:, (2 * pi + 1) * 128:(2 * pi + 2) * 128], rhs=Vt[:, h, :], start=False, stop=True))

        r0 = scores2(0)
        if mid is not None:
            mid()
        pv2(0, 0, *r0[0])
        r1 = scores2(1)
        pv2(0, 1, *r0[1])
        pv2(1, 0, *r1[0])
        pv2(1, 1, *r1[1])
        if late is not None:
            late()
        if d != 1:
            ok_, ot = ost_r.next()
            S.op("dve", ["PV0"], [ok_], lambda: nc.vector.tensor_copy(out=ot[:, 0, :], in_=PVb[0][:, 0:260]))
            S.op("act", ["PV1"], [ok_], lambda: nc.scalar.activation(out=ot[:, 1, :], in_=PVb[1][:, 0:260], func=AF.Copy))
            row0 = (512 * Bo + r) if d == 4 else r
            S.dma("sp", ok_, [ok_], [ok_], scr[d][row0:row0 + 127 * d + 1:d, :], ot.rearrange("p b e -> p (b e)"))
            return
        for hb in range(2):
            S.op("dve", ["PV%d" % hb, info["nat4"][0]], ["pmA"], lambda: nc.vector.tensor_tensor(
                out=pmA[:, hb, :], in0=PVb[hb][:, 0:260], in1=info["nat4"][1][:, hb, :], op=ALU.add))
            S.op("dve", ["pmA", info["nat16"][0]], ["pmA"], lambda: nc.vector.tensor_tensor(
                out=pmA[:, hb, :], in0=pmA[:, hb, :], in1=info["nat16"][1][:, hb, :], op=ALU.add))
        pm4 = pmA.rearrange("p b (h e) -> p (b h) e", e=65)
        S.op("dve", ["pmA"], ["rl"], lambda: nc.vector.reciprocal(out=rl, in_=pm4[:, :, 64]))
        uk, uab = uab_r.next()
        S.op("dve", ["pmA", "rl"], [uk], lambda: nc.vector.tensor_tensor(
            out=uab, in0=pm4[:, :, 0:64], in1=rl.to_broadcast([128, 8, 64]), op=ALU.mult))

        def ua_transposes():
            bk, bank = trbank
            pv = bf_view(bank)[:, 0:512].rearrange("p (c t) -> p c t", t=128)
            flat = uab.rearrange("p h e -> p (h e)")
            for c in range(4):
                S.op("pe", [uk, "const"], [bk], lambda: nc.tensor.transpose(
                    out=pv[:, c, :], in_=flat[:, c * 128:(c + 1) * 128], identity=ident))
            S.op("dve", [bk], ["uaT"], lambda: nc.vector.tensor_copy(out=uaT[:, :, J * 128:(J + 1) * 128], in_=pv))
        pending_ua.append(ua_transposes)

    for gidx, (d, gi) in enumerate(GROUPS):
        pf.flush()
        wb = wbuf[gidx % 2]
        Wsel.update(key="W%d" % (gidx % 2), q=wb[:, :, 0:512], k=wb[:, :, 512:1024], v=wb[:, :, 1024:1536])
        if d == 1:
            S._need("sp", [(k, v) for k, v in S.cnt.items() if k.startswith("D_ost") and v > 0])
        if gidx + 1 < len(GROUPS):
            pf.add(group_w_steps(load_w, gidx + 1, "dve"))
        else:
            ob = wbuf[(gidx + 1) % 2]
            okey = "W%d" % ((gidx + 1) % 2)
            pf.add(load_w.steps(ob[:, :, 0:512], okey, w_in[:, C_QG:C_QG + 512], 512, 8, scale=ngT, eng="dve"))
            pf.add(load_w.steps(ob[:, :, 512:1024], okey, w_in[:, C_KG:C_KG + 512], 512, 8, scale=ngT, eng="dve"))
            pf.add(load_w.steps(ob[:, :, 1024:1040], okey, w_in[:, C_GLR:C_GLR + 16], 16, 8, scale=ngT, eng="dve"))
        L = blocks[d]
        infos = [None] * len(L)
        for i in range(len(L) + 3):
            if i < len(L):
                infos[i] = stageA(d, L[i])
            while pending_ua:
                pending_ua.pop(0)()
            doB = (lambda: stageB(infos[i - 2])) if 0 <= i - 2 < len(L) else None
            doE = (lambda: stageB_evac(infos[i - 2])) if doB is not None else None
            if 0 <= i - 3 < len(L) and not infos[i - 3]["halo"]:
                stageC(d, infos[i - 3], infos[i - 4], mid=doB, late=doE)
            elif doB is not None:
                doB()
                doE()
            if i < len(L):
                stageA2(infos[i])
            pf.pump(2)
    while pending_ua:
        pending_ua.pop(0)()
    S.barrier()

    if stop_after <= 2:
        return nc, S
    pf.flush()
    A3 = Arena(XW)
    ubT = A3.alloc([128, 8, T], BF16)
    X3 = A3.top
    load_w = make_loader(A3, 1024, 2)
    _gb = wbuf[len(GROUPS) % 2]
    _vb = wbuf[(len(GROUPS) + 1) % 2]
    KG, KV = "W%d" % (len(GROUPS) % 2), "W%d" % ((len(GROUPS) + 1) % 2)
    Wqg = _gb[:, :, 0:512]
    Wkg = _gb[:, :, 512:1024]
    Wgl = _gb[:, :, 1024:1040]
    Wvg = _vb[:, :, 0:1024]
    Sf = A3.alloc([128, 4, 256], F32)
    Sb_ = A3.alloc([128, 4, 256], BF16)
    glr_r = Ring("glr", [A3.alloc([32, 128], F32) for _ in range(3)])
    ef = A3.alloc([128, 512], F32)
    spf = ef
    lgh_r = Ring("lgh", [_vb[:, c, 1024:1536] for c in (0, 1, 2)])
    lgl_r = Ring("lgl", [_vb[:, c, 1024:1536] for c in (3, 4, 5)])
    ek_r = Ring("ek", [A3.alloc([128, 4, 128], F32) for _ in range(2)])
    eq_r = Ring("eq", [A3.alloc([128, 4, 128], F32) for _ in range(2)])
    ktT_r = Ring("ktT", [_gb[:, 4 * i:4 * i + 4, 1040:1168] for i in range(2)])
    qtT_r = Ring("qtT", [_gb[:, 4 * i:4 * i + 4, 1168:1296] for i in range(2)])
    kt_r = Ring("kt", [_gb[:, 4 * i:4 * i + 4, 1296:1424] for i in range(2)])
    v_r = Ring("vg", [A3.alloc([128, 1024], BF16) for _ in range(2)])
    am_r = Ring("am", [_vb[:, c, 1024:1536].rearrange("p (h t) -> p h t", t=128) for c in (6, 7)])
    stmp_r = Ring("stmp", [A3.alloc([128, 256], F32) for _ in range(2)])
    ss3 = A3.alloc([128, 4], F32)
    junk3 = A3.alloc([128, 256], BF16)
    ubb_r = Ring("ubb", [A3.alloc([128, 1024], BF16) for _ in range(2)])
    ring3 = Ring("pb", [banks[i] for i in range(6)])
    oring3 = Ring("ob", [banks[6], banks[7]])

    S.op("dve", [], ["Sfinit"], lambda: nc.vector.memset(Sf, 0.0))
    S.op("dve", [], ["Sbinit"], lambda: nc.vector.memset(Sb_, 0.0))
    for a in glr_r.aps:
        S.op("dve", [], ["glrinit"], lambda: nc.vector.memset(a, 1.0))
    S.barrier()
    load_w(Wvg, KV, w_in[:, C_VG:C_VG + 1024], 1024, 8, scale=ngT)

    pre3 = {}

    pre3a = {}

    def gate_a(j):
        L0g = j * 128
        gb_, gbank = ring3.next()
        for c in range(8):
            S.op("pe", [KG], [gb_], lambda: nc.tensor.matmul(
                gbank[0:16, 0:128], lhsT=Wgl[:, c, :], rhs=hsrc(L0g, 128, 1, c), start=(c == 0), stop=(c == 7)))
        gk_, glr = glr_r.next()
        S.op("act", [gb_, "glrinit"], [gk_], lambda: nc.scalar.activation(out=glr[0:16, :], in_=gbank[0:16, 0:128], func=AF.Copy))
        pre3a[j] = (gk_, glr)

    def gate_b(j):
        gk_, glr = pre3a.pop(j)
        lb, lbank = ring3.next()
        S.op("pe", [gk_, "const"], [lb], lambda: nc.tensor.matmul(lbank, lhsT=glr, rhs=w2aug, start=True, stop=True))
        S.op("act", [lb], ["ef"], lambda: nc.scalar.activation(out=ef, in_=lbank, func=AF.Exp, scale=-1.0))
        S.op("act", ["ef"], ["ef"], lambda: nc.scalar.activation(out=spf, in_=ef, func=AF.Ln, bias=1.0))
        hk, lgh = lgh_r.next()
        lk, lgl = lgl_r.next()
        S.op("dve", ["ef"], [hk], lambda: nc.vector.tensor_scalar(
            out=lgh, in0=spf, scalar1=-1.0 / 16, scalar2=None, op0=ALU.mult))
        S.op("dve", ["ef", hk], [lk], lambda: nc.vector.scalar_tensor_tensor(
            out=lgl, in0=spf, scalar=-1.0 / 16, in1=lgh, op0=ALU.mult, op1=ALU.subtract))
        pre3[j] = (hk, lgh, lk, lgl)

    def stageA3(j):
        own = j >= 16
        info = {}
        L0 = j * 128
        J = j - 16
        hs = [hsrc(L0, 128, 1, c) for c in range(8)]
        hk, lgh, lk, lgl = pre3.pop(j)
        if j + 1 < 32:
            gate_a(j + 1)
        kb, kbank = ring3.next()
        for h in range(4):
            for c in range(8):
                S.op("pe", [KG], [kb], lambda: nc.tensor.matmul(
                    kbank[:, h * 128:(h + 1) * 128], lhsT=Wkg[:, c, h * 128:(h + 1) * 128], rhs=hs[c],
                    start=(c == 0), stop=(c == 7)))
        if own:
            qb, qbank = ring3.next()
            for h in range(4):
                for c in range(8):
                    S.op("pe", [KG], [qb], lambda: nc.tensor.matmul(
                        qbank[:, h * 128:(h + 1) * 128], lhsT=Wqg[:, c, h * 128:(h + 1) * 128], rhs=hs[c],
                        start=(c == 0), stop=(c == 7)))
        cb_, cbank = ring3.next()
        for h in range(4):
            reg = cbank[:, h * 128:(h + 1) * 128]
            S.op("pe", [hk, "const"], [cb_], lambda: nc.tensor.matmul(
                reg, lhsT=lgh[:, h * 128:(h + 1) * 128], rhs=tri, start=True, stop=False))
            S.op("pe", [lk, "const"], [cb_], lambda: nc.tensor.matmul(
                reg, lhsT=lgl[:, h * 128:(h + 1) * 128], rhs=tri, start=False, stop=True))
        ekk, ek = ek_r.next()
        eqk, eq = eq_r.next()
        c4 = cbank.rearrange("p (h t) -> p h t", t=128)
        S.op("act", [cb_], [ekk], lambda: nc.scalar.activation(out=ek, in_=c4, func=AF.Exp, scale=-1.0))
        S.op("act", [cb_], [eqk], lambda: nc.scalar.activation(out=eq, in_=c4, func=AF.Exp))
        vk, vg = v_r.next()
        for half in range(2):
            vb, vbank = ring3.next()
            for c in range(8):
                S.op("pe", [KV], [vb], lambda: nc.tensor.matmul(
                    vbank, lhsT=hs[c], rhs=Wvg[:, c, half * 512:(half + 1) * 512], start=(c == 0), stop=(c == 7)))
            S.op("act", [vb], [vk], lambda: nc.scalar.activation(
                out=vg[:, half * 512:(half + 1) * 512], in_=vbank, func=AF.Copy))
        ktk, ktT = ktT_r.next()
        S.op("dve", [kb, ekk], [ktk], lambda: nc.vector.tensor_tensor(
            out=ktT, in0=kbank.rearrange("p (h t) -> p h t", t=128), in1=ek, op=ALU.mult))
        if own:
            qtk, qtT = qtT_r.next()
            S.op("dve", [qb, eqk], [qtk], lambda: nc.vector.scalar_tensor_tensor(
                out=qtT, in0=qbank.rearrange("p (h t) -> p h t", t=128), scalar=128.0 ** -0.5,
                in1=eq, op0=ALU.mult, op1=ALU.mult))
        tb, tbank = ring3.next()
        tv = bf_view(tbank)[:, 0:512].rearrange("p (h t) -> p h t", t=128)
        for h in range(4):
            S.op("pe", [ktk, "const"], [tb], lambda: nc.tensor.transpose(out=tv[:, h, :], in_=ktT[:, h, :], identity=ident))
        ktok, kt = kt_r.next()
        S.op("dve", [tb], [ktok], lambda: nc.vector.tensor_copy(out=kt, in_=tv))
        if own:
            ab, abank = ring3.next()
            for h in range(4):
                S.op("pe", [ktk, qtk], [ab], lambda: nc.tensor.matmul(
                    abank[:, h * 128:(h + 1) * 128], lhsT=ktT[:, h, :], rhs=qtT[:, h, :], start=True, stop=True))
            amk, am = am_r.next()
            S.op("dve", [ab, "const"], [amk], lambda: nc.vector.tensor_tensor(
                out=am, in0=abank.rearrange("p (h t) -> p h t", t=128),
                in1=trif.unsqueeze(1).to_broadcast([128, 4, 128]), op=ALU.mult))
        if j + 1 < 32:
            gate_b(j + 1)
        info.update(dict(ktk=ktk, ktT=ktT, vk=vk, vg=vg, ktok=ktok, kt=kt, eqk=eqk, eq=eq))
        if own:
            info.update(dict(qtk=qtk, qtT=qtT, amk=amk, am=am))
        return info

    def stageB3(j, info):
        own = j >= 16
        J = j - 16
        ktk, ktT, vk, vg, ktok, kt, eqk, eq = (info[n] for n in ("ktk", "ktT", "vk", "vg", "ktok", "kt", "eqk", "eq"))
        if own:
            qtk, qtT, amk, am = (info[n] for n in ("qtk", "qtT", "amk", "am"))
            obanks = []
            for hp in range(2):
                ob, obank = oring3.next()
                obanks.append((ob, obank))
                for hh in range(2):
                    h = 2 * hp + hh
                    reg = obank[:, hh * 256:(hh + 1) * 256]
                    S.op("pe", [qtk, "Sb%d" % h], [ob], lambda: nc.tensor.matmul(
                        reg, lhsT=qtT[:, h, :], rhs=Sb_[:, h, :], start=True, stop=False))
                    S.op("pe", [amk, vk], [ob], lambda: nc.tensor.matmul(
                        reg, lhsT=am[:, h, :], rhs=vg[:, h * 256:(h + 1) * 256], start=False, stop=True))
        if j < 31:
            for hp in range(2):
                db, dbank = ring3.next()
                for hh in range(2):
                    h = 2 * hp + hh
                    S.op("pe", [ktok, vk], [db], lambda: nc.tensor.matmul(
                        dbank[:, hh * 256:(hh + 1) * 256], lhsT=kt[:, h, :], rhs=vg[:, h * 256:(h + 1) * 256],
                        start=True, stop=True))
                for hh in range(2):
                    h = 2 * hp + hh
                    dec = eq[:, h, 127:128]
                    stk, stmp = stmp_r.next()
                    S.op("dve", [db, "Sf%d" % h], [stk], lambda: nc.vector.tensor_tensor(
                        out=stmp, in0=dbank[:, hh * 256:(hh + 1) * 256], in1=Sf[:, h, :], op=ALU.add))
                    S.op("act", [stk, eqk], ["Sb%d" % h], lambda: nc.scalar.activation(
                        out=Sb_[:, h, :], in_=stmp, func=AF.Copy, scale=dec))
                    S.op("dve", [stk, eqk], ["Sf%d" % h], lambda: nc.vector.tensor_scalar(
                        out=Sf[:, h, :], in0=stmp, scalar1=dec, scalar2=None, op0=ALU.mult))
        if own:
            ubk, ubb = ubb_r.next()
            for hp in range(2):
                ob, obank = obanks[hp]
                for hh in range(2):
                    h = 2 * hp + hh
                    S.op("act", [ob], ["junk3", "ss3"], lambda: nc.scalar.activation(
                        out=junk3, in_=obank[:, hh * 256:(hh + 1) * 256], func=AF.Square, accum_out=ss3[:, h:h + 1]))
            S.op("act", ["ss3"], ["ss3"], lambda: nc.scalar.activation(out=ss3, in_=ss3, func=AF.Ln, scale=1.0 / 256, bias=EPS))
            S.op("act", ["ss3"], ["ss3"], lambda: nc.scalar.activation(out=ss3, in_=ss3, func=AF.Exp, scale=-0.5))
            for hp in range(2):
                ob, obank = obanks[hp]
                for hh in range(2):
                    h = 2 * hp + hh
                    S.op("dve", [ob, "ss3", "const"], [ubk], lambda: nc.vector.scalar_tensor_tensor(
                        out=ubb[:, h * 256:(h + 1) * 256], in0=obank[:, hh * 256:(hh + 1) * 256], scalar=ss3[:, h:h + 1],
                        in1=gnb, op0=ALU.mult, op1=ALU.mult))
            info["ubb"] = (ubk, ubb)

    def stageC3(j, info):
        J = j - 16
        ubk, ubb = info["ubb"]
        ub_, ubank = ring3.next()
        uv = bf_view(ubank).rearrange("p (c t) -> p c t", t=128)
        for c in range(8):
            S.op("pe", [ubk, "const"], [ub_], lambda: nc.tensor.transpose(
                out=uv[:, c, :], in_=ubb[:, c * 128:(c + 1) * 128], identity=ident))
        S.op("act", [ub_], ["ubT"], lambda: nc.scalar.activation(out=ubT[:, :, J * 128:(J + 1) * 128], in_=uv, func=AF.Copy))
    infos3 = {}
    gate_a(0)
    gate_b(0)
    infos3[0] = stageA3(0)
    for j in range(32):
        stageB3(j, infos3[j])
        if j - 1 >= 16:
            stageC3(j - 1, infos3.pop(j - 1))
        if j + 1 < 32:
            infos3[j + 1] = stageA3(j + 1)
    stageC3(31, infos3.pop(31))
    S.barrier()

    if stop_after <= 3:
        return nc, S
    A4 = Arena(XBASE)
    stg4 = Ring("stg4_", [A4.alloc([128, 8, 128], F32) for _ in range(4)])
    wfc_r = Ring("wfc", [A4.alloc([128, 28, 128], BF16) for _ in range(3)])
    sg_r = Ring("sg", [A4.alloc([128, 512], F32) for _ in range(2)])
    y1_r = Ring("y1", [A4.alloc([128, 512], F32) for _ in range(2)])
    assert A4.top <= XW
    A3b = Arena(X3)
    zt_r = Ring("zt", [A3b.alloc([128, 8, 128], BF16) for _ in range(3)])
    sz_r = Ring("sz", [A3b.alloc([128, 512], F32) for _ in range(2)])
    ring3b = Ring("pb", [banks[i] for i in range(8)])

    def z_steps(c0, zkey, zt):
        def dma():
            sk, st = stg4.next()
            S.dma("sp", sk, [], [sk], st, w_in[:, c0:c0 + 128].rearrange("(c p) n -> p c n", p=128))
            return sk, st

        def cast(h):
            sk, st = h
            S.op("dve", [sk, "const"], [zkey], lambda: nc.vector.tensor_tensor(
                out=zt, in0=st, in1=ngT[:, 0:8].to_broadcast([128, 8, 128]), op=ALU.mult))
        return [(dma, cast)]

    def fc_steps(fc, wkey, wt):
        cols = slice(fc * 128, (fc + 1) * 128)
        steps = []
        for (src, nch, off, scaled) in ((w_pa, 4, 0, False), (w_in[:, C_GA:C_GA + 1024], 8, 4, True),
                                        (w_pb, 8, 12, False), (w_in[:, C_GB:C_GB + 1024], 8, 20, True)):
            def dma(src=src, nch=nch):
                sk, st = stg4.next()
                S.dma("sp", sk, [], [sk], st[:, 0:nch, :], src.rearrange("(c p) n -> p c n", p=128)[:, :, cols])
                return sk, st

            def cast(h, nch=nch, off=off, scaled=scaled):
                sk, st = h
                o = wt[:, off:off + nch, :]
                if scaled:
                    S.op("dve", [sk, "const"], [wkey], lambda: nc.vector.tensor_tensor(
                        out=o, in0=st[:, 0:nch, :], in1=ngT[:, 0:nch].to_broadcast([128, nch, 128]), op=ALU.mult))
                else:
                    S.op("act", [sk], [wkey], lambda: nc.scalar.activation(out=o, in_=st[:, 0:nch, :], func=AF.Copy))
            steps.append((dma, cast))
        return steps

    wtiles = [wfc_r.next() for _ in range(8)]
    zjobs = [(C_ZG + fc * 128, ubT, fc) for fc in range(8)] + [(C_ZA + fc * 128, uaT, fc) for fc in range(4)]
    ztiles = [zt_r.next() for _ in zjobs]
    pf.add(z_steps(zjobs[0][0], *ztiles[0]))
    pf.flush()
    for n, (c0, dstT, fc) in enumerate(zjobs):
        if n + 1 < len(zjobs):
            pf.add(z_steps(zjobs[n + 1][0], *ztiles[n + 1]))
        elif True:
            pf.add(fc_steps(0, *wtiles[0]))
        zk, zt = ztiles[n]
        for tg in range(4):
            tok = slice(tg * 512, (tg + 1) * 512)
            zb, zbank = ring3b.next()
            for c in range(8):
                S.op("pe", [zk], [zb], lambda: nc.tensor.matmul(
                    zbank, lhsT=zt[:, c, :], rhs=hTo[:, c, tok], start=(c == 0), stop=(c == 7)))
            szk, sz = sz_r.next()
            S.op("act", [zb], [szk], lambda: nc.scalar.activation(out=sz, in_=zbank, func=AF.Silu))
            S.op("dve", [szk], ["uT"], lambda: nc.vector.tensor_tensor(
                out=dstT[:, fc, tok], in0=dstT[:, fc, tok], in1=sz, op=ALU.mult))
            pf.pump(1)
    pf.flush()
    S.barrier()

    if debug:
        S.dma("sp", "dbg1", [], ["dbg"], dbg["ua"], uaT)
        S.dma("sp", "dbg2", [], ["dbg"], dbg["ub"], ubT)
        S.barrier()

    yT = hTh
    A4h = Arena(X3)
    Wo = A4h.alloc([128, 8, 1024], BF16)
    Wpg = A4h.alloc([128, 8, 1024], BF16)
    Wpl = A4h.alloc([128, 2, 1024], BF16)
    ring4 = Ring("pb", [banks[i] for i in range(8)])

    def big_steps(dst, key, src, nchunk, scale=None):
        steps = []
        for c in range(nchunk):
            def dma(c=c):
                sk, st = stg4.next()
                S.dma("sp", sk, [], [sk], st.rearrange("p a b -> p (a b)"), src[c * 128:(c + 1) * 128, :])
                return sk, st

            def cast(h, c=c):
                sk, st = h
                flat = st.rearrange("p a b -> p (a b)")
                if scale is None:
                    S.op("act", [sk], [key], lambda: nc.scalar.activation(out=dst[:, c, :], in_=flat, func=AF.Copy))
                else:
                    S.op("act", [sk, "const"], [key], lambda: nc.scalar.activation(
                        out=dst[:, c, :], in_=flat, func=AF.Copy, scale=scale[:, c:c + 1]))
            steps.append((dma, cast))
        return steps

    later = big_steps(Wo, "Wo", w_out, 8) + big_steps(Wpg, "Wpg", w_pg, 8, scale=pngT) + big_steps(Wpl, "Wpl", w_ple, 2)
    for fc in range(1, 8):
        pf.add(fc_steps(fc, *wtiles[fc]))
        pf.add(later[:3])
        later = later[3:]
    pf.add(later)
    for fc in range(8):
        wk, wt = wtiles[fc]
        for tg in range(4):
            tok = slice(tg * 512, (tg + 1) * 512)
            res = []
            for (poff, src, nk, goff) in ((0, uaT, 4, 4), (12, ubT, 8, 20)):
                yb, ybank = ring4.next()
                for c in range(nk):
                    S.op("pe", [wk], [yb], lambda: nc.tensor.matmul(
                        ybank, lhsT=wt[:, poff + c, :], rhs=src[:, c, tok], start=(c == 0), stop=(c == nk - 1)))
                gb2, gbank2 = ring4.next()
                for c in range(8):
                    S.op("pe", [wk], [gb2], lambda: nc.tensor.matmul(
                        gbank2, lhsT=wt[:, goff + c, :], rhs=hTo[:, c, tok], start=(c == 0), stop=(c == 7)))
                sk, sg = sg_r.next()
                S.op("act", [gb2], [sk], lambda: nc.scalar.activation(out=sg, in_=gbank2, func=AF.Sigmoid))
                res.append((yb, ybank, sk, sg))
            y1k, y1 = y1_r.next()
            S.op("dve", [res[0][0], res[0][2]], [y1k], lambda: nc.vector.tensor_tensor(
                out=y1, in0=res[0][1], in1=res[0][3], op=ALU.mult))
            S.op("dve", [res[1][0], res[1][2]], [res[1][2]], lambda: nc.vector.tensor_tensor(
                out=res[1][3], in0=res[1][1], in1=res[1][3], op=ALU.mult))
            S.op("dve", [y1k, res[1][2]], ["yT"], lambda: nc.vector.tensor_tensor(
                out=yT[:, fc, tok], in0=y1, in1=res[1][3], op=ALU.add))
            pf.pump(2)
    pf.flush()
    S.barrier()
    if debug:
        S.dma("sp", "dbg3", [], ["dbg"], dbg["y"], yT)
        S.barrier()

    if stop_after <= 4:
        return nc, S
    A5 = Arena(XBASE)
    load_w = make_loader(A5)
    pTb = A5.alloc([128, 2, T], BF16)
    xt5_r = Ring("xt5", [A5.alloc([128, D], F32) for _ in range(3)])
    x1_r = Ring("x1", [A5.alloc([128, D], F32) for _ in range(3)])
    xn5_r = Ring("xn5", [A5.alloc([128, D], BF16) for _ in range(3)])
    xnT_r = Ring("xnT", [A5.alloc([128, 8, 128], BF16) for _ in range(2)])
    sg5_r = Ring("sg5", [A5.alloc([128, D], F32) for _ in range(2)])
    o5_r = Ring("o5", [A5.alloc([128, D], F32) for _ in range(2)])
    junk5 = A5.alloc([128, D], BF16)
    ss5 = A5.alloc([128, NT], F32)
    mhalf5 = A5.alloc([128, 1], F32)
    S.op("pool", [], ["mh5"], lambda: nc.gpsimd.memset(mhalf5, -0.5))
    assert A5.top <= X3
    ring5 = Ring("pb", [banks[i] for i in range(8)])
    load_w(pTb, "pTb", pT_d, T, 2)
    def stageA5(J):
        tok = slice(J * 128, (J + 1) * 128)
        xk, xt = xt5_r.next()
        S.dma("sp", xk, [], [xk], xt, xs[T + J * 128:T + (J + 1) * 128, :])
        x1k, x1 = x1_r.next()
        for half in range(2):
            hc = slice(half * 512, (half + 1) * 512)
            zb, zbank = ring5.next()
            for c in range(8):
                S.op("pe", ["Wo"], [zb], lambda: nc.tensor.matmul(
                    zbank, lhsT=yT[:, c, tok], rhs=Wo[:, c, hc], start=(c == 0), stop=(c == 7)))
            S.op("dve", [zb, xk], [x1k], lambda: nc.vector.tensor_tensor(out=x1[:, hc], in0=zbank, in1=xt[:, hc], op=ALU.add))
        return (x1k, x1)

    def stageA5b(J, st):
        x1k, x1 = st
        S.op("act", [x1k], ["junk5", "ss5_%d" % J], lambda: nc.scalar.activation(
            out=junk5, in_=x1, func=AF.Square, accum_out=ss5[:, J:J + 1]))
        S.op("pool", ["ss5_%d" % J], ["ss5_%d" % J], lambda: nc.gpsimd.tensor_scalar(
            out=ss5[:, J:J + 1], in0=ss5[:, J:J + 1], scalar1=1.0 / D, scalar2=EPS, op0=ALU.mult, op1=ALU.add))
        S.op("pool", ["ss5_%d" % J, "mh5"], ["ss5_%d" % J], lambda: nc.gpsimd.tensor_tensor(
            out=ss5[:, J:J + 1], in0=ss5[:, J:J + 1], in1=mhalf5, op=ALU.pow))
        nk, xn = xn5_r.next()
        S.op("dve", [x1k, "ss5_%d" % J], [nk], lambda: nc.vector.tensor_scalar(
            out=xn, in0=x1, scalar1=ss5[:, J:J + 1], scalar2=None, op0=ALU.mult))
        return (x1k, x1, nk, xn)

    def stageB5(J, st):
        x1k, x1, nk, xn = st
        tok = slice(J * 128, (J + 1) * 128)
        tb, tbank = ring5.next()
        tv = bf_view(tbank).rearrange("p (c t) -> p c t", t=128)
        for c in range(8):
            S.op("pe", [nk, "const"], [tb], lambda: nc.tensor.transpose(
                out=tv[:, c, :], in_=xn[:, c * 128:(c + 1) * 128], identity=ident))
        tk, xnT = xnT_r.next()
        S.op("act", [tb], [tk], lambda: nc.scalar.activation(out=xnT, in_=tv, func=AF.Copy))
        return (x1k, x1, tk, xnT)

    def stageB5b(J, st):
        x1k, x1, tk, xnT = st
        tok = slice(J * 128, (J + 1) * 128)
        sk, sg = sg5_r.next()
        ok, o5 = o5_r.next()
        pbs = []
        for half in range(2):
            hc = slice(half * 512, (half + 1) * 512)
            pb2, pbank2 = ring5.next()
            for c in range(2):
                S.op("pe", ["pTb", "Wpl"], [pb2], lambda: nc.tensor.matmul(
                    pbank2, lhsT=pTb[:, c, tok], rhs=Wpl[:, c, hc], start=(c == 0), stop=(c == 1)))
            pbs.append((pb2, pbank2))
        for half in range(2):
            hc = slice(half * 512, (half + 1) * 512)
            gb2, gbank2 = ring5.next()
            for c in range(8):
                S.op("pe", [tk, "Wpg"], [gb2], lambda: nc.tensor.matmul(
                    gbank2, lhsT=xnT[:, c, :], rhs=Wpg[:, c, hc], start=(c == 0), stop=(c == 7)))
            skh, okh = "%s_%d" % (sk, half), "%s_%d" % (ok, half)
            S.op("act", [gb2], [skh], lambda: nc.scalar.activation(out=sg[:, hc], in_=gbank2, func=AF.Sigmoid))
            pb2, pbank2 = pbs[half]
            S.op("dve", [pb2, skh], [skh], lambda: nc.vector.tensor_tensor(out=sg[:, hc], in0=pbank2, in1=sg[:, hc], op=ALU.mult))
            S.op("dve", [skh, x1k], [okh], lambda: nc.vector.tensor_tensor(out=o5[:, hc], in0=sg[:, hc], in1=x1[:, hc], op=ALU.add))
        S.dma("pool", ok, [ok + "_0", ok + "_1"], [ok + "_0", ok + "_1"], out_d[tok, :], o5)

    st5 = {0: stageA5b(0, stageA5(0))}
    for J in range(NT):
        a_next = stageA5(J + 1) if J + 1 < NT else None
        b_mid = stageB5(J, st5.pop(J))
        if a_next is not None:
            st5[J + 1] = stageA5b(J + 1, a_next)
        stageB5b(J, b_mid)
    S.barrier()
    return nc, S


def _host_consts(hf):
    p = np.arange(128)
    kp, qp = p[:, None], p[None, :]
    ident = np.eye(128, dtype=np.float32)
    m_prev = np.where(kp >= qp, 0.0, NEG).astype(np.float32)
    m_cur = np.where(kp <= qp, 0.0, NEG).astype(np.float32)
    m_halo = m_prev if hf == 1 else np.full((128, 128), NEG, np.float32)
    tri = (kp <= qp).astype(np.float32)
    cbf = np.concatenate([ident, m_prev, m_cur, m_halo, tri, m_prev, m_cur, m_prev, m_cur,
                          m_halo, m_cur, m_halo, m_cur], axis=1).astype(NPBF)
    return cbf, tri


_CACHE = {}


def _prep_inputs(x, p, positions, norm_g, w_in, qk_norm_q, qk_norm_k, gla_gate_w2, gla_gate_b,
                 gla_norm_g, w_att_proj, w_gla_proj, w_out, ple_norm_g, w_ple_gate, w_ple):
    blocks, NB = _blocks()
    half = 8
    inv = np.power(np.float32(500000.0), -np.arange(half, dtype=np.float32) * np.float32(2.0) / np.float32(16)).astype(np.float32)
    invf = np.ascontiguousarray(np.broadcast_to(inv[None, :], (128, 8))).astype(np.float32)
    w2aug = np.zeros((32, 512), np.float32)
    w2aug[0:16] = gla_gate_w2[0]
    w2aug[16] = gla_gate_b[0]
    shared = {
        "invf": invf,
        "ng": np.ascontiguousarray(norm_g[0].reshape(8, 128).T),
        "png": np.ascontiguousarray(ple_norm_g[0].reshape(8, 128).T),
        "gqb": np.ascontiguousarray(np.broadcast_to(qk_norm_q[0][None, :], (128, 64))),
        "gkb": np.ascontiguousarray(np.broadcast_to(qk_norm_k[0][None, :], (128, 64))),
        "gnb": np.ascontiguousarray(np.broadcast_to(gla_norm_g[0][None, :], (128, 256))),
        "w2aug": w2aug,
        "w_in": np.ascontiguousarray(w_in[0]), "w_pa": np.ascontiguousarray(w_att_proj[0]),
        "w_pb": np.ascontiguousarray(w_gla_proj[0]), "w_out": np.ascontiguousarray(w_out[0]),
        "w_pg": np.ascontiguousarray(w_ple_gate[0]), "w_ple": np.ascontiguousarray(w_ple[0]),
    }
    in_maps = []
    for core in range(8):
        b, hf = core // 2, core % 2
        cbf, tri = _host_consts(hf)
        xs = np.zeros((4096, D), np.float32)
        posl = np.zeros((4096,), np.int32)
        if hf == 1:
            xs[:] = x[b]
            posl[:] = positions[b]
        else:
            xs[T:] = x[b, :T]
            posl[T:] = positions[b, :T]
        posb = np.zeros((128, NB), np.int32)
        pp = np.arange(128)
        for d, gi in GROUPS:
            for (r, B, halo, bid) in blocks[d]:
                posb[:, bid] = posl[r + d * (128 * B + pp)]
        m = dict(shared)
        m.update({"xs": xs, "posb": posb, "pT": np.ascontiguousarray(p[0, b, hf * T:(hf + 1) * T, :].T),
                  "cbf": cbf, "trif": tri})
        in_maps.append(m)
    return in_maps


def kernel(**inputs):
    inputs = {k: np.asarray(v) for k, v in inputs.items()}
    in_maps = _prep_inputs(**inputs)
    if "nc" not in _CACHE:
        _CACHE["nc"] = build_program(False)[0]
    res = run_bass_kernel_spmd(_CACHE["nc"], in_maps, core_ids=list(range(8)))
    out = np.zeros((4, 4096, D), np.float32)
    for core in range(8):
        b, hf = core // 2, core % 2
        out[b, hf * T:(hf + 1) * T] = res.results[core]["out"]
    return out
```

```python
import numpy as np
import ml_dtypes
import concourse.bass as bass
import concourse.mybir as mybir
from concourse.bass_utils import run_bass_kernel_spmd

F32 = mybir.dt.float32
BF16 = mybir.dt.bfloat16
I32 = mybir.dt.int32
AF = mybir.ActivationFunctionType
ALU = mybir.AluOpType
AX = mybir.AxisListType
NPBF = ml_dtypes.bfloat16

D = 1024
T = 2048
NT = T // 128
EPS = 1e-6
C_QA, C_KA, C_VA, C_ZA, C_QG, C_KG, C_VG, C_GLR, C_ZG, C_GA, C_GB = (
    0, 1536, 3072, 4608, 5120, 5632, 6144, 7168, 7184, 8208, 9232)
GROUPS = ((16, 2), (4, 1), (1, 0))
NEG = -30000.0


class Sched:
    def __init__(self, nc):
        self.nc = nc
        self.E = {"pe": nc.tensor, "dve": nc.vector, "act": nc.scalar,
                  "pool": nc.gpsimd, "sp": nc.sync}
        self.sems, self.cnt, self.waited = {}, {}, {}
        self.lastw, self.readers = {}, {}
        self.ninst = 0
        self.nwait = 0
        for e in ("pe", "dve", "act", "pool"):
            self._sem("E_" + e)

    def _sem(self, key):
        if key not in self.sems:
            self.sems[key] = self.nc.alloc_semaphore("s_" + key)
            self.cnt[key] = 0
        return self.sems[key]

    def _need(self, eng, toks):
        best = {}
        for t in toks:
            if t is None:
                continue
            k, v = t
            if v > best.get(k, 0):
                best[k] = v
        for k, v in best.items():
            if eng == "pe" and k == "E_pe":
                continue
            if self.waited.get((eng, k), 0) >= v:
                continue
            self.E[eng].wait_ge(self.sems[k], v)
            self.waited[(eng, k)] = v
            self.nwait += 1

    def deps(self, eng, reads, writes):
        toks = []
        for r in reads:
            toks.append(self.lastw.get(r))
        for w in writes:
            toks.append(self.lastw.get(w))
            toks.extend(self.readers.get(w, ()))
        self._need(eng, toks)

    def commit(self, tok, reads, writes):
        for r in reads:
            self.readers.setdefault(r, []).append(tok)
        for w in writes:
            self.lastw[w] = tok
            self.readers[w] = []

    def op(self, eng, reads, writes, fn):
        self.deps(eng, reads, writes)
        inst = fn()
        k = "E_" + eng
        self.cnt[k] += 1
        inst.then_inc(self.sems[k], 1)
        self.commit((k, self.cnt[k]), reads, writes)
        self.ninst += 1

    def dma(self, q, slot, reads, writes, out, in_):
        self.deps(q, reads, writes)
        k = "D_" + slot
        self._sem(k)
        inst = self.E[q].dma_start(out=out, in_=in_)
        self.cnt[k] += 16
        inst.then_inc(self.sems[k], 16)
        self.commit((k, self.cnt[k]), reads, writes)
        self.ninst += 1

    def pe_drain(self):
        v = self.cnt["E_pe"]
        if v > self.waited.get(("pe", "E_pe"), 0):
            self.E["pe"].wait_ge(self.sems["E_pe"], v)
            self.waited[("pe", "E_pe")] = v
            self.nwait += 1

    def barrier(self):
        toks = [(k, v) for k, v in self.cnt.items() if v > 0]
        for e in ("pe", "dve", "act", "pool", "sp"):
            self._need(e, toks)
        self.lastw.clear()
        self.readers.clear()


class Ring:
    def __init__(self, name, aps):
        self.name, self.aps, self.i = name, aps, 0

    def next(self):
        j = self.i % len(self.aps)
        self.i += 1
        return "%s%d" % (self.name, j), self.aps[j]


def _blocks():
    out = {}
    bid = 0
    for d, gi in GROUPS:
        lst = []
        for r in range(d):
            for B in range(16 // d - 1, 32 // d):
                lst.append((r, B, B == 16 // d - 1, bid))
                bid += 1
        out[d] = lst
    return out, bid


def build_program(debug=False, stop_after=99):
    nc = bass.Bass("TRN2", target_bir_lowering=False)
    S = Sched(nc)
    blocks, NB = _blocks()

    def din(name, shape, dt):
        return nc.dram_tensor(name, list(shape), dt, kind="ExternalInput").ap()

    xs = din("xs", [4096, D], F32)
    posb = din("posb", [128, NB], I32)
    invf = din("invf", [128, 8], F32)
    pT_d = din("pT", [256, T], F32)
    consts_d = din("cbf", [128, 13 * 128], BF16)
    trif_d = din("trif", [128, 128], F32)
    ng_d = din("ng", [128, 8], F32)
    png_d = din("png", [128, 8], F32)
    gq_d = din("gqb", [128, 64], F32)
    gk_d = din("gkb", [128, 64], F32)
    gn_d = din("gnb", [128, 256], F32)
    w2_d = din("w2aug", [32, 512], F32)
    w_in = din("w_in", [D, 10256], F32)
    w_pa = din("w_pa", [512, D], F32)
    w_pb = din("w_pb", [D, D], F32)
    w_out = din("w_out", [D, D], F32)
    w_pg = din("w_pg", [D, D], F32)
    w_ple = din("w_ple", [256, D], F32)
    out_d = nc.dram_tensor("out", [T, D], F32, kind="ExternalOutput").ap()
    scr = {4: nc.dram_tensor("scr4", [T, 520], BF16, kind="Internal").ap(),
           16: nc.dram_tensor("scr16", [T, 520], BF16, kind="Internal").ap()}
    dbg = {}
    if debug:
        dbg["ua"] = nc.dram_tensor("dbg_ua", [128, 4, T], BF16, kind="ExternalOutput").ap()
        dbg["ub"] = nc.dram_tensor("dbg_ub", [128, 8, T], BF16, kind="ExternalOutput").ap()
        dbg["y"] = nc.dram_tensor("dbg_y", [128, 8, T], BF16, kind="ExternalOutput").ap()

    _CNT = [0]

    class Arena:
        def __init__(self, base):
            self.top = base
            self.n = 0

        def alloc(self, shape, dt):
            nbytes = int(np.prod(shape[1:])) * (4 if dt in (F32, I32) else 2)
            off = (self.top + 63) // 64 * 64
            self.top = off + nbytes
            self.n += 1
            assert self.top <= 229344, ("SBUF overflow", self.top)
            _CNT[0] += 1
            return nc.alloc_sbuf_tensor_at("sb%d" % _CNT[0], list(shape), dt, offset=off).ap()

    P = Arena(16512)
    cbf = P.alloc([128, 13 * 128], BF16)
    ident, m_prev, m_cur, m_halo, tri = (cbf[:, i * 128:(i + 1) * 128] for i in range(5))
    mask4 = cbf[:, 5 * 128:9 * 128]
    mask4h = cbf[:, 9 * 128:13 * 128]
    trif = P.alloc([128, 128], F32)
    ngT = P.alloc([128, 8], F32)
    pngT = P.alloc([128, 8], F32)
    gqb = P.alloc([128, 64], F32)
    gkb = P.alloc([128, 64], F32)
    gnb = P.alloc([128, 256], F32)
    w2aug = P.alloc([32, 512], F32)
    hTo = P.alloc([128, 8, T], BF16)
    hTh = P.alloc([128, 8, T], BF16)
    uaT = P.alloc([128, 4, T], BF16)
    XBASE = P.top

    banks = [nc.alloc_psum_tensor("bank%d" % i, [128, 512], F32).ap() for i in range(8)]

    def bf_view(bank):
        return bank.bitcast(BF16)

    for i, (dst, src) in enumerate(((cbf, consts_d), (trif, trif_d), (ngT, ng_d), (pngT, png_d), (gqb, gq_d),
                                    (gkb, gk_d), (gnb, gn_d), (w2aug, w2_d))):
        S.dma("sp", "c%d" % i, [], ["const"], dst, src)
    S.barrier()

    def make_loader(arena, width=1024, nslot=3, name="stg"):
        stg = Ring(name, [arena.alloc([128, width], F32) for _ in range(nslot)])

        def load_w(dst, key, src, ncols, nchunk, scale=None, eng="dve"):
            for c in range(nchunk):
                for c0 in range(0, ncols, width):
                    cw = min(width, ncols - c0)
                    sk, st = stg.next()
                    S.dma("sp", sk, [], [sk], st[:, 0:cw], src[c * 128:(c + 1) * 128, c0:c0 + cw])
                    o = dst[:, c, c0:c0 + cw]
                    if scale is None:
                        S.op(eng, [sk], [key], lambda: S.E[eng].tensor_copy(out=o, in_=st[:, 0:cw]))
                    else:
                        S.op(eng, [sk, "const"], [key], lambda: S.E[eng].tensor_scalar(
                            out=o, in0=st[:, 0:cw], scalar1=scale[:, c:c + 1], scalar2=None, op0=ALU.mult))
        def load_steps(dst, key, src, ncols, nchunk, scale=None, eng="dve", q="sp"):
            steps = []
            for c in range(nchunk):
                for c0 in range(0, ncols, width):
                    cw = min(width, ncols - c0)

                    def dma(c=c, c0=c0, cw=cw):
                        sk, st = stg.next()
                        S.dma(q, sk, [], [sk], st[:, 0:cw], src[c * 128:(c + 1) * 128, c0:c0 + cw])
                        return sk, st

                    def cast(h, c=c, c0=c0, cw=cw):
                        sk, st = h
                        o = dst[:, c, c0:c0 + cw]
                        if scale is None:
                            if eng == "act":
                                S.op("act", [sk], [key], lambda: nc.scalar.activation(out=o, in_=st[:, 0:cw], func=AF.Copy))
                            else:
                                S.op(eng, [sk], [key], lambda: S.E[eng].tensor_copy(out=o, in_=st[:, 0:cw]))
                        elif eng == "act":
                            S.op("act", [sk, "const"], [key], lambda: nc.scalar.activation(
                                out=o, in_=st[:, 0:cw], func=AF.Copy, scale=scale[:, c:c + 1]))
                        else:
                            S.op(eng, [sk, "const"], [key], lambda: S.E[eng].tensor_scalar(
                                out=o, in0=st[:, 0:cw], scalar1=scale[:, c:c + 1], scalar2=None, op0=ALU.mult))
                    steps.append((dma, cast))
            return steps
        load_w.steps = load_steps
        load_w.nslot = nslot
        return load_w

    class Prefetch:
        def __init__(self):
            self.q = []
            self.inflight = []

        def add(self, steps):
            self.q.extend(steps)

        def _casts(self):
            for cast, h in self.inflight:
                cast(h)
            self.inflight = []

        def pump(self, n=1):
            self._casts()
            for _ in range(min(n, len(self.q))):
                dma, cast = self.q.pop(0)
                self.inflight.append((cast, dma()))

        def flush(self, ring=2):
            while self.q or self.inflight:
                self.pump(ring)

    pf = Prefetch()
    AW = Arena(XBASE)
    wbuf = [AW.alloc([128, 8, 1536], BF16) for _ in range(2)]
    XW = AW.top

    def group_w_steps(loader, gidx, eng, q="sp"):
        d_, gi_ = GROUPS[gidx]
        b = wbuf[gidx % 2]
        st = []
        for (o0, c0) in ((0, C_QA + gi_ * 512), (512, C_KA + gi_ * 512), (1024, C_VA + gi_ * 512)):
            st += loader.steps(b[:, :, o0:o0 + 512], "W%d" % (gidx % 2), w_in[:, c0:c0 + 512], 512, 8, scale=ngT, eng=eng, q=q)
        return st

    AT = Arena(XW)
    cos2T = AT.alloc([128, NB, 16], F32)
    sin2T = AT.alloc([128, NB, 16], F32)
    XT = AT.top
    A1 = Arena(XT)
    load_w1 = make_loader(A1, 512, 3, name="stgp")
    pf.add(group_w_steps(load_w1, 0, "dve", q="pool"))
    xt_r = Ring("xt", [A1.alloc([128, D], F32) for _ in range(8)])
    xn_r = Ring("xn", [A1.alloc([128, D], BF16) for _ in range(4)])
    junk = A1.alloc([128, D], BF16)
    ss1 = A1.alloc([128, 32], F32)
    posi = A1.alloc([128, NB], I32)
    posf = A1.alloc([128, NB], F32)
    ang = A1.alloc([128, NB, 8], F32)
    ang2 = A1.alloc([128, NB, 8], F32)
    invt = A1.alloc([128, 8], F32)
    pring1 = Ring("pb", [(banks[i]) for i in range(4)])

    S.dma("sp", "posi", [], ["posi"], posi, posb)
    S.dma("sp", "invt", [], ["invt"], invt, invf)
    S.op("dve", ["posi"], ["posf"], lambda: nc.vector.tensor_copy(out=posf, in_=posi))
    S.op("dve", ["posf", "invt"], ["ang"], lambda: nc.vector.tensor_tensor(
        out=ang, in0=posf.to_broadcast([128, NB, 8]), in1=invt.unsqueeze(1).to_broadcast([128, NB, 8]), op=ALU.mult))
    PI = float(np.pi)
    MAGIC = 12582912.0
    C1 = 6.28125
    C2 = 2.0 * PI - C1
    kf = A1.alloc([128, NB, 8], F32)
    for (dst, shift, sgn) in ((sin2T, 0.0, -1.0), (cos2T, 0.5 * PI, 1.0)):
        S.op("dve", ["ang"], ["ang2"], lambda: nc.vector.tensor_scalar(
            out=ang2, in0=ang, scalar1=shift, scalar2=None, op0=ALU.add))
        S.op("dve", ["ang2"], ["kf"], lambda: nc.vector.tensor_scalar(
            out=kf, in0=ang2, scalar1=1.0 / (2.0 * PI), scalar2=MAGIC, op0=ALU.mult, op1=ALU.add))
        S.op("dve", ["kf"], ["kf"], lambda: nc.vector.tensor_scalar(
            out=kf, in0=kf, scalar1=-MAGIC, scalar2=None, op0=ALU.add))
        S.op("dve", ["kf", "ang2"], ["ang2"], lambda: nc.vector.scalar_tensor_tensor(
            out=ang2.rearrange("p a b -> p (a b)"), in0=kf.rearrange("p a b -> p (a b)"), scalar=-C1,
            in1=ang2.rearrange("p a b -> p (a b)"), op0=ALU.mult, op1=ALU.add))
        S.op("dve", ["kf", "ang2"], ["ang2"], lambda: nc.vector.scalar_tensor_tensor(
            out=ang2.rearrange("p a b -> p (a b)"), in0=kf.rearrange("p a b -> p (a b)"), scalar=-C2,
            in1=ang2.rearrange("p a b -> p (a b)"), op0=ALU.mult, op1=ALU.add))
        S.op("dve", ["ang2"], ["ang2"], lambda: nc.vector.tensor_scalar(
            out=ang2, in0=ang2, scalar1=-PI, scalar2=PI, op0=ALU.max, op1=ALU.min))
        S.op("act", ["ang2"], ["rot"], lambda: nc.scalar.activation(out=dst[:, :, 0:8], in_=ang2, func=AF.Sin, scale=sgn))
        S.op("act", ["ang2"], ["rot"], lambda: nc.scalar.activation(out=dst[:, :, 8:16], in_=ang2, func=AF.Sin))

    def stageA1(j):
        xk, xt = xt_r.next()
        S.dma("sp", xk, [], [xk], xt, xs[j * 128:(j + 1) * 128, :])
        S.op("act", [xk], ["junk", "ss1_%d" % j], lambda: nc.scalar.activation(
            out=junk, in_=xt, func=AF.Square, accum_out=ss1[:, j:j + 1]))
        S.op("act", ["ss1_%d" % j], ["ss1_%d" % j], lambda: nc.scalar.activation(
            out=ss1[:, j:j + 1], in_=ss1[:, j:j + 1], func=AF.Sqrt, scale=1.0 / D, bias=EPS))
        S.op("dve", ["ss1_%d" % j], ["ss1_%d" % j], lambda: nc.vector.reciprocal(out=ss1[:, j:j + 1], in_=ss1[:, j:j + 1]))
        nk, xn = xn_r.next()
        S.op("dve", [xk, "ss1_%d" % j], [nk], lambda: nc.vector.tensor_scalar(
            out=xn, in0=xt, scalar1=ss1[:, j:j + 1], scalar2=None, op0=ALU.mult))
        return nk, xn

    def stageB1(j, nk, xn):
        bk, bank = pring1.next()
        pv = bf_view(bank)[:, 0:1024].rearrange("p (c t) -> p c t", t=128)
        for c in range(8):
            S.op("pe", [nk, "const"], [bk], lambda: nc.tensor.transpose(
                out=pv[:, c, :], in_=xn[:, c * 128:(c + 1) * 128], identity=ident))
        dst = (hTh if j < 16 else hTo)[:, :, (j % 16) * 128:(j % 16 + 1) * 128]
        if j % 2 == 0:
            S.op("act", [bk], ["hT%d" % j], lambda: nc.scalar.activation(out=dst, in_=pv, func=AF.Copy))
        else:
            S.op("dve", [bk], ["hT%d" % j], lambda: nc.vector.tensor_copy(out=dst, in_=pv))

    st1 = {0: stageA1(0), 1: stageA1(1)}
    for j in range(32):
        if j + 2 < 32:
            st1[j + 2] = stageA1(j + 2)
        stageB1(j, *st1.pop(j))
        pf.pump(1)
    pf.flush()
    S.barrier()

    if stop_after <= 1:
        return nc, S

    def hsrc(L0, n, step, c):
        src, s = (hTh, L0) if L0 < T else (hTo, L0 - T)
        return src[:, c, s:s + (n - 1) * step + 1:step]

    A2 = Arena(XT)
    load_w = make_loader(A2, 512, 3)
    Wsel = {}
    ost_r = Ring("ost", [A2.alloc([128, 2, 260], BF16) for _ in range(3)])
    nat_r = {4: Ring("nat4_", [A2.alloc([128, 2, 260], BF16) for _ in range(4)]),
             16: Ring("nat16_", [A2.alloc([128, 2, 260], BF16) for _ in range(4)])}
    sqs_r = Ring("sqs", [A2.alloc([128, 512], F32) for _ in range(2)])
    ssq_r = Ring("ssq", [A2.alloc([128, 8], F32) for _ in range(4)])
    tmpn_r = Ring("tmpn", [A2.alloc([128, 8, 64], F32) for _ in range(4)])
    t16_r = Ring("t16", [A2.alloc([128, 8, 16], F32) for _ in range(2)])
    rtmp_r = Ring("rtmp", [A2.alloc([128, 2, 8, 16], F32) for _ in range(2)])
    qn_r = Ring("qn", [A2.alloc([128, 8, 64], BF16) for _ in range(3)])
    kn_r = Ring("kn", [A2.alloc([128, 8, 64], BF16) for _ in range(3)])
    qT_r = Ring("qT", [A2.alloc([128, 4, 128], BF16) for _ in range(4)])
    kT_r = Ring("kT", [A2.alloc([128, 4, 128], BF16) for _ in range(5)])
    V_r = Ring("V", [A2.alloc([128, 8, 65], BF16) for _ in range(7)])
    PT_r = Ring("PT", [A2.alloc([128, 512], BF16) for _ in range(4)])
    pmA = A2.alloc([128, 2, 260], F32)
    rl = A2.alloc([128, 8], F32)
    uab_r = Ring("uab", [A2.alloc([128, 8, 64], BF16) for _ in range(2)])
    pending_ua = []
    pring = Ring("pb", [banks[i] for i in range(3)])
    trbank = ("pb3", banks[3])
    sring = {16: Ring("sb", [banks[4], banks[5]]), 4: Ring("sb", [banks[4], banks[5]]), 1: Ring("sb", [banks[4], banks[5]])}
    PVb = [banks[6], banks[7]]

    for Vt0 in V_r.aps:
        S.op("pool", [], ["Vinit"], lambda: nc.gpsimd.memset(Vt0[:, :, 64:65], 1.0))
    S.barrier()

    def norm_early(ps_key, ps):
        sk, sqs = sqs_r.next()
        tk, raw = tmpn_r.next()
        S.op("act", [ps_key], [sk], lambda: nc.scalar.activation(out=sqs, in_=ps, func=AF.Square))
        S.op("act", [ps_key], [tk], lambda: nc.scalar.activation(
            out=raw.rearrange("p h e -> p (h e)"), in_=ps, func=AF.Copy))
        return (sk, sqs, tk, raw)

    def norm_steps(early, gb, bid, out_ring):
        sk, sqs, tk, raw = early
        qk_, ssq = ssq_r.next()
        ok, on = out_ring.next()
        k16, t16 = t16_r.next()
        rk, rt = rtmp_r.next()
        c2 = cos2T[:, bid, :].unsqueeze(1).to_broadcast([128, 8, 16])
        s2 = sin2T[:, bid, :].rearrange("p (two e) -> p two e", two=2).unsqueeze(1).to_broadcast([128, 8, 2, 8])
        sw = t16.rearrange("p h (two e) -> p h two e", two=2)[:, :, ::-1, :]
        steps = [
            lambda: S.op("dve", [sk], [qk_], lambda: nc.vector.tensor_reduce(
                out=ssq, in_=sqs.rearrange("p (h e) -> p h e", e=64), axis=AX.X, op=ALU.add)),
            lambda: S.op("act", [qk_], [qk_], lambda: nc.scalar.activation(
                out=ssq, in_=ssq, func=AF.Ln, scale=1.0 / 64, bias=EPS)),
            lambda: S.op("act", [qk_], [qk_], lambda: nc.scalar.activation(out=ssq, in_=ssq, func=AF.Exp, scale=-0.5)),
            lambda: S.op("dve", [tk, qk_], [tk], lambda: nc.vector.tensor_tensor(
                out=raw, in0=raw, in1=ssq.to_broadcast([128, 8, 64]), op=ALU.mult)),
            lambda: S.op("dve", [tk], [k16], lambda: nc.vector.tensor_tensor(
                out=t16, in0=raw[:, :, 0:16], in1=gb[:, 0:16].unsqueeze(1).to_broadcast([128, 8, 16]), op=ALU.mult)),
            lambda: S.op("dve", [tk], [ok], lambda: nc.vector.tensor_tensor(
                out=on[:, :, 16:64], in0=raw[:, :, 16:64], in1=gb[:, 16:64].unsqueeze(1).to_broadcast([128, 8, 48]), op=ALU.mult)),
        ]
        steps.append(lambda: S.op("dve", [k16, "rot"], [rk], lambda: nc.vector.tensor_tensor(
            out=rt[:, 0], in0=t16, in1=c2, op=ALU.mult)))
        steps.append(lambda: S.op("dve", [k16, "rot"], [rk], lambda: nc.vector.tensor_tensor(
            out=rt[:, 1].rearrange("p h (two e) -> p h two e", two=2), in0=sw, in1=s2, op=ALU.mult)))
        steps.append(lambda: S.op("dve", [rk], [ok], lambda: nc.vector.tensor_tensor(
            out=on[:, :, 0:16], in0=rt[:, 0], in1=rt[:, 1], op=ALU.add)))
        return ok, on, steps

    def stageA(d, blk):
        r, B, halo, bid = blk
        L0 = r + d * 128 * B
        info = {"blk": blk, "halo": halo}
        kb, kbank = pring.next()
        for c in range(8):
            S.op("pe", [Wsel["key"]], [kb], lambda: nc.tensor.matmul(
                kbank, lhsT=hsrc(L0, 128, d, c), rhs=Wsel["k"][:, c, :], start=(c == 0), stop=(c == 7)))
        info["kearly"] = norm_early(kb, kbank)
        vb, vbank = pring.next()
        for c in range(8):
            S.op("pe", [Wsel["key"]], [vb], lambda: nc.tensor.matmul(
                vbank, lhsT=hsrc(L0, 128, d, c), rhs=Wsel["v"][:, c, :], start=(c == 0), stop=(c == 7)))
        Vk, Vt = V_r.next()
        info["V"] = (Vk, Vt)
        S.op("act", [vb, "Vinit"], [Vk], lambda: nc.scalar.activation(
            out=Vt[:, :, 0:64], in_=vbank.rearrange("p (h e) -> p h e", e=64), func=AF.Copy))
        late = []
        info["late"] = late
        if not halo:
            qb, qbank = pring.next()
            for c in range(8):
                S.op("pe", [Wsel["key"]], [qb], lambda: nc.tensor.matmul(
                    qbank, lhsT=hsrc(L0, 128, d, c), rhs=Wsel["q"][:, c, :], start=(c == 0), stop=(c == 7)))
            late.append(lambda: info.__setitem__("qearly", norm_early(qb, qbank)))
            if d == 1:
                J0 = B - 16
                for dd in (4, 16):
                    nk_, nt_ = nat_r[dd].next()
                    S.dma("sp", nk_, [], [nk_], nt_.rearrange("p b e -> p (b e)"), scr[dd][J0 * 128:(J0 + 1) * 128, :])
                    info["nat%d" % dd] = (nk_, nt_)
        return info

    def stageA2(info):
        bid = info["blk"][3]
        knk, kn, ksteps = norm_steps(info["kearly"], gkb, bid, kn_r)
        info["kn"] = (knk, kn)
        qsteps = []
        if "qearly" in info:
            qnk, qn, qsteps = norm_steps(info["qearly"], gqb, bid, qn_r)
            info["qn"] = (qnk, qn)
        for i in range(max(len(ksteps), len(qsteps))):
            if i < len(ksteps):
                ksteps[i]()
            if i < len(qsteps):
                qsteps[i]()

    def stageB(info):
        bk, bank = trbank
        pvw = bf_view(bank).rearrange("p (w c t) -> p w c t", w=2, t=128)
        todo = []
        for w, name, ring, eng in ((0, "kn", kT_r, "dve"), (1, "qn", qT_r, "act")):
            if name not in info:
                continue
            sk, src = info[name]
            flat = src.rearrange("p h e -> p (h e)")
            for c in range(4):
                S.op("pe", [sk, "const"], [bk], lambda: nc.tensor.transpose(
                    out=pvw[:, w, c, :], in_=flat[:, c * 128:(c + 1) * 128], identity=ident))
            todo.append((w, ring, eng))
        info["evac_todo"] = (bk, pvw, todo)

    def stageB_evac(info):
        bk, pvw, todo = info.pop("evac_todo")
        for w, ring, eng in todo:
            ok, o = ring.next()
            if eng == "act":
                S.op("act", [bk], [ok, bk], lambda: nc.scalar.activation(out=o, in_=pvw[:, w], func=AF.Copy))
            else:
                S.op("dve", [bk], [ok, bk], lambda: nc.vector.tensor_copy(out=o, in_=pvw[:, w]))
            info["kT" if w == 0 else "qT"] = (ok, o)

    def stageC(d, info, prev, mid=None, late=None):
        r, B, halo, bid = info["blk"]
        Bo = B - 16 // d
        J = Bo
        kTk, kT = info["kT"]
        qTk, qT = info["qT"]
        Vk, Vt = info["V"]
        pkTk, pkT = prev["kT"]
        pVk, pV = prev["V"]
        msk = mask4h if prev["halo"] else mask4

        def scores2(st):
            xb, xbank = sring[d].next()
            yb, ybank = sring[d].next()
            S.op("pe", ["const"], [xb], lambda: nc.tensor.matmul(
                xbank, lhsT=ident, rhs=msk, start=True, stop=False, skip_group_check=True))
            S.op("pe", ["const"], [yb], lambda: nc.tensor.matmul(
                ybank, lhsT=ident, rhs=msk, start=True, stop=False, skip_group_check=True))
            for pi in range(2):
                hp = 2 * st + pi
                for (kk, ktile, kkey) in ((0, pkT, pkTk), (1, kT, kTk)):
                    reg = slice((2 * pi + kk) * 128, (2 * pi + kk + 1) * 128)
                    last = (pi == 1 and kk == 1)
                    S.op("pe", [kkey, qTk], [xb], lambda: nc.tensor.matmul(
                        xbank[:, reg], lhsT=ktile[0:64, hp, :], rhs=qT[0:64, hp, :], start=False, stop=last,
                        skip_group_check=True))
                    S.op("pe", [kkey, qTk], [yb], lambda: nc.tensor.matmul(
                        ybank[:, reg], lhsT=ktile[64:128, hp, :], rhs=qT[64:128, hp, :], start=False, stop=last,
                        skip_group_check=True))
            res = []
            for (bk_, bank_) in ((xb, xbank), (yb, ybank)):
                Pk, PT = PT_r.next()
                S.op("act", [bk_], [Pk], lambda: nc.scalar.activation(out=PT, in_=bank_, func=AF.Exp, scale=0.125))
                res.append((Pk, PT))
            return res

        def pv2(st, which, Pk, PT):
            for pi in range(2):
                h = 2 * (2 * st + pi) + which
                reg = PVb[h // 4][:, (h % 4) * 65:(h % 4) * 65 + 65]
                S.op("pe", [Pk, pVk], ["PV%d" % (h // 4)], lambda: nc.tensor.matmul(
                    reg, lhsT=PT[:, (2 * pi) * 128:(2 * pi + 1) * 128], rhs=pV[:, h, :], start=True, stop=False))
                S.op("pe", [Pk, Vk], ["PV%d" % (h // 4)], lambda: nc.tensor.matmul(
                    reg, lhsT=PT[:, (2 * pi + 1) * 128:(2 * pi + 2) * 128], rhs=Vt[:, h, :], start=False, stop=True))

        r0 = scores2(0)
        if mid is not None:
            mid()
        pv2(0, 0, *r0[0])
        r1 = scores2(1)
        pv2(0, 1, *r0[1])
        pv2(1, 0, *r1[0])
        pv2(1, 1, *r1[1])
        if late is not None:
            late()
        if d != 1:
            ok_, ot = ost_r.next()
            S.op("dve", ["PV0"], [ok_], lambda: nc.vector.tensor_copy(out=ot[:, 0, :], in_=PVb[0][:, 0:260]))
            S.op("act", ["PV1"], [ok_], lambda: nc.scalar.activation(out=ot[:, 1, :], in_=PVb[1][:, 0:260], func=AF.Copy))
            row0 = (512 * Bo + r) if d == 4 else r
            S.dma("sp", ok_, [ok_], [ok_], scr[d][row0:row0 + 127 * d + 1:d, :], ot.rearrange("p b e -> p (b e)"))
            return
        for hb in range(2):
            S.op("dve", ["PV%d" % hb, info["nat4"][0]], ["pmA"], lambda: nc.vector.tensor_tensor(
                out=pmA[:, hb, :], in0=PVb[hb][:, 0:260], in1=info["nat4"][1][:, hb, :], op=ALU.add))
            S.op("dve", ["pmA", info["nat16"][0]], ["pmA"], lambda: nc.vector.tensor_tensor(
                out=pmA[:, hb, :], in0=pmA[:, hb, :], in1=info["nat16"][1][:, hb, :], op=ALU.add))
        pm4 = pmA.rearrange("p b (h e) -> p (b h) e", e=65)
        S.op("dve", ["pmA"], ["rl"], lambda: nc.vector.reciprocal(out=rl, in_=pm4[:, :, 64]))
        uk, uab = uab_r.next()
        S.op("dve", ["pmA", "rl"], [uk], lambda: nc.vector.tensor_tensor(
            out=uab, in0=pm4[:, :, 0:64], in1=rl.to_broadcast([128, 8, 64]), op=ALU.mult))

        def ua_transposes():
            bk, bank = trbank
            pv = bf_view(bank)[:, 0:512].rearrange("p (c t) -> p c t", t=128)
            flat = uab.rearrange("p h e -> p (h e)")
            for c in range(4):
                S.op("pe", [uk, "const"], [bk], lambda: nc.tensor.transpose(
                    out=pv[:, c, :], in_=flat[:, c * 128:(c + 1) * 128], identity=ident))
            S.op("dve", [bk], ["uaT"], lambda: nc.vector.tensor_copy(out=uaT[:, :, J * 128:(J + 1) * 128], in_=pv))
        pending_ua.append(ua_transposes)

    for gidx, (d, gi) in enumerate(GROUPS):
        pf.flush()
        wb = wbuf[gidx % 2]
        Wsel.update(key="W%d" % (gidx % 2), q=wb[:, :, 0:512], k=wb[:, :, 512:1024], v=wb[:, :, 1024:1536])
        if d == 1:
            S._need("sp", [(k, v) for k, v in S.cnt.items() if k.startswith("D_ost") and v > 0])
        if gidx + 1 < len(GROUPS):
            pf.add(group_w_steps(load_w, gidx + 1, "act"))
        else:
            ob = wbuf[(gidx + 1) % 2]
            okey = "W%d" % ((gidx + 1) % 2)
            pf.add(load_w.steps(ob[:, :, 0:512], okey, w_in[:, C_QG:C_QG + 512], 512, 8, scale=ngT, eng="act"))
            pf.add(load_w.steps(ob[:, :, 512:1024], okey, w_in[:, C_KG:C_KG + 512], 512, 8, scale=ngT, eng="act"))
            pf.add(load_w.steps(ob[:, :, 1024:1040], okey, w_in[:, C_GLR:C_GLR + 16], 16, 8, scale=ngT, eng="act"))
        L = blocks[d]
        infos = [None] * len(L)
        for i in range(len(L) + 3):
            if i < len(L):
                infos[i] = stageA(d, L[i])
            while pending_ua:
                pending_ua.pop(0)()
            hasB = 0 <= i - 2 < len(L)

            def mid(i=i, hasB=hasB):
                if hasB:
                    stageB(infos[i - 2])
                if i < len(L):
                    for f in infos[i].pop("late"):
                        f()
            doE = (lambda: stageB_evac(infos[i - 2])) if hasB else None
            if 0 <= i - 3 < len(L) and not infos[i - 3]["halo"]:
                stageC(d, infos[i - 3], infos[i - 4], mid=mid, late=doE)
            else:
                mid()
                if doE is not None:
                    doE()
            if i < len(L):
                stageA2(infos[i])
            pf.pump(2)
    while pending_ua:
        pending_ua.pop(0)()
    S.barrier()

    if stop_after <= 2:
        return nc, S
    pf.flush()
    A3 = Arena(XW)
    ubT = A3.alloc([128, 8, T], BF16)
    X3 = A3.top
    load_w = make_loader(A3, 1024, 2)
    _gb = wbuf[len(GROUPS) % 2]
    _vb = wbuf[(len(GROUPS) + 1) % 2]
    KG, KV = "W%d" % (len(GROUPS) % 2), "W%d" % ((len(GROUPS) + 1) % 2)
    Wqg = _gb[:, :, 0:512]
    Wkg = _gb[:, :, 512:1024]
    Wgl = _gb[:, :, 1024:1040]
    Wvg = _vb[:, :, 0:1024]
    Sf = A3.alloc([128, 4, 256], F32)
    Sb_ = A3.alloc([128, 4, 256], BF16)
    glr_r = Ring("glr", [A3.alloc([32, 128], F32) for _ in range(3)])
    ef = A3.alloc([128, 512], F32)
    spf = ef
    lgh_r = Ring("lgh", [_vb[:, c, 1024:1536] for c in (0, 1, 2)])
    lgl_r = Ring("lgl", [_vb[:, c, 1024:1536] for c in (3, 4, 5)])
    ek_r = Ring("ek", [A3.alloc([128, 4, 128], F32) for _ in range(2)])
    eq_r = Ring("eq", [A3.alloc([128, 4, 128], F32) for _ in range(2)])
    ktT_r = Ring("ktT", [_gb[:, 4 * i:4 * i + 4, 1040:1168] for i in range(2)])
    qtT_r = Ring("qtT", [_gb[:, 4 * i:4 * i + 4, 1168:1296] for i in range(2)])
    kt_r = Ring("kt", [_gb[:, 4 * i:4 * i + 4, 1296:1424] for i in range(2)])
    v_r = Ring("vg", [A3.alloc([128, 1024], BF16) for _ in range(2)])
    am_r = Ring("am", [_vb[:, c, 1024:1536].rearrange("p (h t) -> p h t", t=128) for c in (6, 7)])
    stmp_r = Ring("stmp", [A3.alloc([128, 256], F32) for _ in range(2)])
    ss3 = A3.alloc([128, 4], F32)
    junk3 = A3.alloc([128, 256], BF16)
    ubb_r = Ring("ubb", [A3.alloc([128, 1024], BF16) for _ in range(2)])
    ring3 = Ring("pb", [banks[i] for i in range(6)])
    oring3 = Ring("ob", [banks[6], banks[7]])

    S.op("dve", [], ["Sfinit"], lambda: nc.vector.memset(Sf, 0.0))
    S.op("dve", [], ["Sbinit"], lambda: nc.vector.memset(Sb_, 0.0))
    for a in glr_r.aps:
        S.op("dve", [], ["glrinit"], lambda: nc.vector.memset(a, 1.0))
    S.barrier()
    load_w(Wvg, KV, w_in[:, C_VG:C_VG + 1024], 1024, 8, scale=ngT)

    pre3 = {}

    pre3a = {}

    def gate_a(j):
        L0g = j * 128
        gb_, gbank = ring3.next()
        for c in range(8):
            S.op("pe", [KG], [gb_], lambda: nc.tensor.matmul(
                gbank[0:16, 0:128], lhsT=Wgl[:, c, :], rhs=hsrc(L0g, 128, 1, c), start=(c == 0), stop=(c == 7)))
        gk_, glr = glr_r.next()
        S.op("act", [gb_, "glrinit"], [gk_], lambda: nc.scalar.activation(out=glr[0:16, :], in_=gbank[0:16, 0:128], func=AF.Copy))
        pre3a[j] = (gk_, glr)

    def gate_b(j):
        gk_, glr = pre3a.pop(j)
        lb, lbank = ring3.next()
        S.op("pe", [gk_, "const"], [lb], lambda: nc.tensor.matmul(lbank, lhsT=glr, rhs=w2aug, start=True, stop=True))
        S.op("act", [lb], ["ef"], lambda: nc.scalar.activation(out=ef, in_=lbank, func=AF.Exp, scale=-1.0))
        S.op("act", ["ef"], ["ef"], lambda: nc.scalar.activation(out=spf, in_=ef, func=AF.Ln, bias=1.0))
        hk, lgh = lgh_r.next()
        lk, lgl = lgl_r.next()
        S.op("dve", ["ef"], [hk], lambda: nc.vector.tensor_scalar(
            out=lgh, in0=spf, scalar1=-1.0 / 16, scalar2=None, op0=ALU.mult))
        S.op("dve", ["ef", hk], [lk], lambda: nc.vector.scalar_tensor_tensor(
            out=lgl, in0=spf, scalar=-1.0 / 16, in1=lgh, op0=ALU.mult, op1=ALU.subtract))
        pre3[j] = (hk, lgh, lk, lgl)

    def stageA3(j):
        own = j >= 16
        info = {}
        L0 = j * 128
        J = j - 16
        hs = [hsrc(L0, 128, 1, c) for c in range(8)]
        hk, lgh, lk, lgl = pre3.pop(j)
        if j + 1 < 32:
            gate_a(j + 1)
        kb, kbank = ring3.next()
        for h in range(4):
            for c in range(8):
                S.op("pe", [KG], [kb], lambda: nc.tensor.matmul(
                    kbank[:, h * 128:(h + 1) * 128], lhsT=Wkg[:, c, h * 128:(h + 1) * 128], rhs=hs[c],
                    start=(c == 0), stop=(c == 7)))
        if own:
            qb, qbank = ring3.next()
            for h in range(4):
                for c in range(8):
                    S.op("pe", [KG], [qb], lambda: nc.tensor.matmul(
                        qbank[:, h * 128:(h + 1) * 128], lhsT=Wqg[:, c, h * 128:(h + 1) * 128], rhs=hs[c],
                        start=(c == 0), stop=(c == 7)))
        cb_, cbank = ring3.next()
        for h in range(4):
            reg = cbank[:, h * 128:(h + 1) * 128]
            S.op("pe", [hk, "const"], [cb_], lambda: nc.tensor.matmul(
                reg, lhsT=lgh[:, h * 128:(h + 1) * 128], rhs=tri, start=True, stop=False))
            S.op("pe", [lk, "const"], [cb_], lambda: nc.tensor.matmul(
                reg, lhsT=lgl[:, h * 128:(h + 1) * 128], rhs=tri, start=False, stop=True))
        ekk, ek = ek_r.next()
        eqk, eq = eq_r.next()
        c4 = cbank.rearrange("p (h t) -> p h t", t=128)
        S.op("act", [cb_], [ekk], lambda: nc.scalar.activation(out=ek, in_=c4, func=AF.Exp, scale=-1.0))
        S.op("act", [cb_], [eqk], lambda: nc.scalar.activation(out=eq, in_=c4, func=AF.Exp))
        vk, vg = v_r.next()
        for half in range(2):
            vb, vbank = ring3.next()
            for c in range(8):
                S.op("pe", [KV], [vb], lambda: nc.tensor.matmul(
                    vbank, lhsT=hs[c], rhs=Wvg[:, c, half * 512:(half + 1) * 512], start=(c == 0), stop=(c == 7)))
            S.op("act", [vb], [vk], lambda: nc.scalar.activation(
                out=vg[:, half * 512:(half + 1) * 512], in_=vbank, func=AF.Copy))
        ktk, ktT = ktT_r.next()
        S.op("dve", [kb, ekk], [ktk], lambda: nc.vector.tensor_tensor(
            out=ktT, in0=kbank.rearrange("p (h t) -> p h t", t=128), in1=ek, op=ALU.mult))
        if own:
            qtk, qtT = qtT_r.next()
            S.op("dve", [qb, eqk], [qtk], lambda: nc.vector.scalar_tensor_tensor(
                out=qtT, in0=qbank.rearrange("p (h t) -> p h t", t=128), scalar=128.0 ** -0.5,
                in1=eq, op0=ALU.mult, op1=ALU.mult))
        tb, tbank = ring3.next()
        tv = bf_view(tbank)[:, 0:512].rearrange("p (h t) -> p h t", t=128)
        for h in range(4):
            S.op("pe", [ktk, "const"], [tb], lambda: nc.tensor.transpose(out=tv[:, h, :], in_=ktT[:, h, :], identity=ident))
        ktok, kt = kt_r.next()
        S.op("dve", [tb], [ktok], lambda: nc.vector.tensor_copy(out=kt, in_=tv))
        if own:
            ab, abank = ring3.next()
            for h in range(4):
                S.op("pe", [ktk, qtk], [ab], lambda: nc.tensor.matmul(
                    abank[:, h * 128:(h + 1) * 128], lhsT=ktT[:, h, :], rhs=qtT[:, h, :], start=True, stop=True))
            amk, am = am_r.next()
            S.op("dve", [ab, "const"], [amk], lambda: nc.vector.tensor_tensor(
                out=am, in0=abank.rearrange("p (h t) -> p h t", t=128),
                in1=trif.unsqueeze(1).to_broadcast([128, 4, 128]), op=ALU.mult))
        if j + 1 < 32:
            gate_b(j + 1)
        info.update(dict(ktk=ktk, ktT=ktT, vk=vk, vg=vg, ktok=ktok, kt=kt, eqk=eqk, eq=eq))
        if own:
            info.update(dict(qtk=qtk, qtT=qtT, amk=amk, am=am))
        return info

    def stageB3(j, info):
        own = j >= 16
        J = j - 16
        ktk, ktT, vk, vg, ktok, kt, eqk, eq = (info[n] for n in ("ktk", "ktT", "vk", "vg", "ktok", "kt", "eqk", "eq"))
        if own:
            qtk, qtT, amk, am = (info[n] for n in ("qtk", "qtT", "amk", "am"))
            obanks = []
            for hp in range(2):
                ob, obank = oring3.next()
                obanks.append((ob, obank))
                for hh in range(2):
                    h = 2 * hp + hh
                    reg = obank[:, hh * 256:(hh + 1) * 256]
                    S.op("pe", [qtk, "Sb%d" % h], [ob], lambda: nc.tensor.matmul(
                        reg, lhsT=qtT[:, h, :], rhs=Sb_[:, h, :], start=True, stop=False))
                    S.op("pe", [amk, vk], [ob], lambda: nc.tensor.matmul(
                        reg, lhsT=am[:, h, :], rhs=vg[:, h * 256:(h + 1) * 256], start=False, stop=True))
        if j < 31:
            for hp in range(2):
                db, dbank = ring3.next()
                for hh in range(2):
                    h = 2 * hp + hh
                    S.op("pe", [ktok, vk], [db], lambda: nc.tensor.matmul(
                        dbank[:, hh * 256:(hh + 1) * 256], lhsT=kt[:, h, :], rhs=vg[:, h * 256:(h + 1) * 256],
                        start=True, stop=True))
                for hh in range(2):
                    h = 2 * hp + hh
                    dec = eq[:, h, 127:128]
                    stk, stmp = stmp_r.next()
                    S.op("dve", [db, "Sf%d" % h], [stk], lambda: nc.vector.tensor_tensor(
                        out=stmp, in0=dbank[:, hh * 256:(hh + 1) * 256], in1=Sf[:, h, :], op=ALU.add))
                    S.op("act", [stk, eqk], ["Sb%d" % h], lambda: nc.scalar.activation(
                        out=Sb_[:, h, :], in_=stmp, func=AF.Copy, scale=dec))
                    S.op("dve", [stk, eqk], ["Sf%d" % h], lambda: nc.vector.tensor_scalar(
                        out=Sf[:, h, :], in0=stmp, scalar1=dec, scalar2=None, op0=ALU.mult))
        if own:
            ubk, ubb = ubb_r.next()
            for hp in range(2):
                ob, obank = obanks[hp]
                for hh in range(2):
                    h = 2 * hp + hh
                    S.op("act", [ob], ["junk3", "ss3"], lambda: nc.scalar.activation(
                        out=junk3, in_=obank[:, hh * 256:(hh + 1) * 256], func=AF.Square, accum_out=ss3[:, h:h + 1]))
            S.op("act", ["ss3"], ["ss3"], lambda: nc.scalar.activation(out=ss3, in_=ss3, func=AF.Ln, scale=1.0 / 256, bias=EPS))
            S.op("act", ["ss3"], ["ss3"], lambda: nc.scalar.activation(out=ss3, in_=ss3, func=AF.Exp, scale=-0.5))
            for hp in range(2):
                ob, obank = obanks[hp]
                for hh in range(2):
                    h = 2 * hp + hh
                    S.op("dve", [ob, "ss3", "const"], [ubk], lambda: nc.vector.scalar_tensor_tensor(
                        out=ubb[:, h * 256:(h + 1) * 256], in0=obank[:, hh * 256:(hh + 1) * 256], scalar=ss3[:, h:h + 1],
                        in1=gnb, op0=ALU.mult, op1=ALU.mult))
            info["ubb"] = (ubk, ubb)

    def stageC3(j, info):
        J = j - 16
        ubk, ubb = info["ubb"]
        ub_, ubank = ring3.next()
        uv = bf_view(ubank).rearrange("p (c t) -> p c t", t=128)
        for c in range(8):
            S.op("pe", [ubk, "const"], [ub_], lambda: nc.tensor.transpose(
                out=uv[:, c, :], in_=ubb[:, c * 128:(c + 1) * 128], identity=ident))
        S.op("act", [ub_], ["ubT"], lambda: nc.scalar.activation(out=ubT[:, :, J * 128:(J + 1) * 128], in_=uv, func=AF.Copy))
    infos3 = {}
    gate_a(0)
    gate_b(0)
    infos3[0] = stageA3(0)
    for j in range(32):
        stageB3(j, infos3[j])
        if j - 1 >= 16:
            stageC3(j - 1, infos3.pop(j - 1))
        if j + 1 < 32:
            infos3[j + 1] = stageA3(j + 1)
    stageC3(31, infos3.pop(31))
    S.barrier()

    if stop_after <= 3:
        return nc, S
    A4 = Arena(XBASE)
    stg4 = Ring("stg4_", [A4.alloc([128, 8, 128], F32) for _ in range(4)])
    wfc_r = Ring("wfc", [A4.alloc([128, 28, 128], BF16) for _ in range(3)])
    sg_r = Ring("sg", [A4.alloc([128, 512], F32) for _ in range(2)])
    y1_r = Ring("y1", [A4.alloc([128, 512], F32) for _ in range(2)])
    assert A4.top <= XW
    A3b = Arena(X3)
    zt_r = Ring("zt", [A3b.alloc([128, 8, 128], BF16) for _ in range(3)])
    sz_r = Ring("sz", [A3b.alloc([128, 512], F32) for _ in range(2)])
    ring3b = Ring("pb", [banks[i] for i in range(8)])

    def z_steps(c0, zkey, zt):
        def dma():
            sk, st = stg4.next()
            S.dma("sp", sk, [], [sk], st, w_in[:, c0:c0 + 128].rearrange("(c p) n -> p c n", p=128))
            return sk, st

        def cast(h):
            sk, st = h
            S.op("dve", [sk, "const"], [zkey], lambda: nc.vector.tensor_tensor(
                out=zt, in0=st, in1=ngT[:, 0:8].to_broadcast([128, 8, 128]), op=ALU.mult))
        return [(dma, cast)]

    def fc_steps(fc, wkey, wt):
        cols = slice(fc * 128, (fc + 1) * 128)
        steps = []
        for (src, nch, off, scaled) in ((w_pa, 4, 0, False), (w_in[:, C_GA:C_GA + 1024], 8, 4, True),
                                        (w_pb, 8, 12, False), (w_in[:, C_GB:C_GB + 1024], 8, 20, True)):
            def dma(src=src, nch=nch):
                sk, st = stg4.next()
                S.dma("sp", sk, [], [sk], st[:, 0:nch, :], src.rearrange("(c p) n -> p c n", p=128)[:, :, cols])
                return sk, st

            def cast(h, nch=nch, off=off, scaled=scaled):
                sk, st = h
                o = wt[:, off:off + nch, :]
                if scaled:
                    S.op("dve", [sk, "const"], [wkey], lambda: nc.vector.tensor_tensor(
                        out=o, in0=st[:, 0:nch, :], in1=ngT[:, 0:nch].to_broadcast([128, nch, 128]), op=ALU.mult))
                else:
                    S.op("act", [sk], [wkey], lambda: nc.scalar.activation(out=o, in_=st[:, 0:nch, :], func=AF.Copy))
            steps.append((dma, cast))
        return steps

    wtiles = [wfc_r.next() for _ in range(8)]
    zjobs = [(C_ZG + fc * 128, ubT, fc) for fc in range(8)] + [(C_ZA + fc * 128, uaT, fc) for fc in range(4)]
    ztiles = [zt_r.next() for _ in zjobs]
    pf.add(z_steps(zjobs[0][0], *ztiles[0]))
    pf.flush()
    for n, (c0, dstT, fc) in enumerate(zjobs):
        if n + 1 < len(zjobs):
            pf.add(z_steps(zjobs[n + 1][0], *ztiles[n + 1]))
        elif True:
            pf.add(fc_steps(0, *wtiles[0]))
        zk, zt = ztiles[n]
        for tg in range(4):
            tok = slice(tg * 512, (tg + 1) * 512)
            zb, zbank = ring3b.next()
            for c in range(8):
                S.op("pe", [zk], [zb], lambda: nc.tensor.matmul(
                    zbank, lhsT=zt[:, c, :], rhs=hTo[:, c, tok], start=(c == 0), stop=(c == 7)))
            szk, sz = sz_r.next()
            S.op("act", [zb], [szk], lambda: nc.scalar.activation(out=sz, in_=zbank, func=AF.Silu))
            S.op("dve", [szk], ["uT"], lambda: nc.vector.tensor_tensor(
                out=dstT[:, fc, tok], in0=dstT[:, fc, tok], in1=sz, op=ALU.mult))
            pf.pump(1)
    pf.flush()
    S.barrier()

    if debug:
        S.dma("sp", "dbg1", [], ["dbg"], dbg["ua"], uaT)
        S.dma("sp", "dbg2", [], ["dbg"], dbg["ub"], ubT)
        S.barrier()

    yT = hTh
    A4h = Arena(X3)
    Wo = A4h.alloc([128, 8, 1024], BF16)
    Wpg = A4h.alloc([128, 8, 1024], BF16)
    Wpl = A4h.alloc([128, 2, 1024], BF16)
    ring4 = Ring("pb", [banks[i] for i in range(8)])

    def big_steps(dst, key, src, nchunk, scale=None):
        steps = []
        for c in range(nchunk):
            def dma(c=c):
                sk, st = stg4.next()
                S.dma("sp", sk, [], [sk], st.rearrange("p a b -> p (a b)"), src[c * 128:(c + 1) * 128, :])
                return sk, st

            def cast(h, c=c):
                sk, st = h
                flat = st.rearrange("p a b -> p (a b)")
                if scale is None:
                    S.op("act", [sk], [key], lambda: nc.scalar.activation(out=dst[:, c, :], in_=flat, func=AF.Copy))
                else:
                    S.op("act", [sk, "const"], [key], lambda: nc.scalar.activation(
                        out=dst[:, c, :], in_=flat, func=AF.Copy, scale=scale[:, c:c + 1]))
            steps.append((dma, cast))
        return steps

    later = big_steps(Wo, "Wo", w_out, 8) + big_steps(Wpg, "Wpg", w_pg, 8, scale=pngT) + big_steps(Wpl, "Wpl", w_ple, 2)
    for fc in range(1, 8):
        pf.add(fc_steps(fc, *wtiles[fc]))
        pf.add(later[:3])
        later = later[3:]
    pf.add(later)
    for fc in range(8):
        wk, wt = wtiles[fc]
        for tg in range(4):
            tok = slice(tg * 512, (tg + 1) * 512)
            res = []
            for (poff, src, nk, goff) in ((0, uaT, 4, 4), (12, ubT, 8, 20)):
                yb, ybank = ring4.next()
                for c in range(nk):
                    S.op("pe", [wk], [yb], lambda: nc.tensor.matmul(
                        ybank, lhsT=wt[:, poff + c, :], rhs=src[:, c, tok], start=(c == 0), stop=(c == nk - 1)))
                gb2, gbank2 = ring4.next()
                for c in range(8):
                    S.op("pe", [wk], [gb2], lambda: nc.tensor.matmul(
                        gbank2, lhsT=wt[:, goff + c, :], rhs=hTo[:, c, tok], start=(c == 0), stop=(c == 7)))
                sk, sg = sg_r.next()
                S.op("act", [gb2], [sk], lambda: nc.scalar.activation(out=sg, in_=gbank2, func=AF.Sigmoid))
                res.append((yb, ybank, sk, sg))
            y1k, y1 = y1_r.next()
            S.op("dve", [res[0][0], res[0][2]], [y1k], lambda: nc.vector.tensor_tensor(
                out=y1, in0=res[0][1], in1=res[0][3], op=ALU.mult))
            S.op("dve", [res[1][0], res[1][2]], [res[1][2]], lambda: nc.vector.tensor_tensor(
                out=res[1][3], in0=res[1][1], in1=res[1][3], op=ALU.mult))
            S.op("dve", [y1k, res[1][2]], ["yT"], lambda: nc.vector.tensor_tensor(
                out=yT[:, fc, tok], in0=y1, in1=res[1][3], op=ALU.add))
            pf.pump(2)
    pf.flush()
    S.barrier()
    if debug:
        S.dma("sp", "dbg3", [], ["dbg"], dbg["y"], yT)
        S.barrier()

    if stop_after <= 4:
        return nc, S
    A5 = Arena(XBASE)
    load_w = make_loader(A5)
    pTb = A5.alloc([128, 2, T], BF16)
    xt5_r = Ring("xt5", [A5.alloc([128, D], F32) for _ in range(3)])
    x1_r = Ring("x1", [A5.alloc([128, D], F32) for _ in range(3)])
    xn5_r = Ring("xn5", [A5.alloc([128, D], BF16) for _ in range(3)])
    xnT_r = Ring("xnT", [A5.alloc([128, 8, 128], BF16) for _ in range(2)])
    sg5_r = Ring("sg5", [A5.alloc([128, D], F32) for _ in range(2)])
    o5_r = Ring("o5", [A5.alloc([128, D], F32) for _ in range(2)])
    junk5 = A5.alloc([128, D], BF16)
    ss5 = A5.alloc([128, NT], F32)
    mhalf5 = A5.alloc([128, 1], F32)
    S.op("pool", [], ["mh5"], lambda: nc.gpsimd.memset(mhalf5, -0.5))
    assert A5.top <= X3
    ring5 = Ring("pb", [banks[i] for i in range(8)])
    load_w(pTb, "pTb", pT_d, T, 2)
    def stageA5(J):
        tok = slice(J * 128, (J + 1) * 128)
        xk, xt = xt5_r.next()
        S.dma("sp", xk, [], [xk], xt, xs[T + J * 128:T + (J + 1) * 128, :])
        x1k, x1 = x1_r.next()
        for half in range(2):
            hc = slice(half * 512, (half + 1) * 512)
            zb, zbank = ring5.next()
            for c in range(8):
                S.op("pe", ["Wo"], [zb], lambda: nc.tensor.matmul(
                    zbank, lhsT=yT[:, c, tok], rhs=Wo[:, c, hc], start=(c == 0), stop=(c == 7)))
            S.op("dve", [zb, xk], [x1k], lambda: nc.vector.tensor_tensor(out=x1[:, hc], in0=zbank, in1=xt[:, hc], op=ALU.add))
        return (x1k, x1)

    def stageA5b(J, st):
        x1k, x1 = st
        S.op("act", [x1k], ["junk5", "ss5_%d" % J], lambda: nc.scalar.activation(
            out=junk5, in_=x1, func=AF.Square, accum_out=ss5[:, J:J + 1]))
        S.op("pool", ["ss5_%d" % J], ["ss5_%d" % J], lambda: nc.gpsimd.tensor_scalar(
            out=ss5[:, J:J + 1], in0=ss5[:, J:J + 1], scalar1=1.0 / D, scalar2=EPS, op0=ALU.mult, op1=ALU.add))
        S.op("pool", ["ss5_%d" % J, "mh5"], ["ss5_%d" % J], lambda: nc.gpsimd.tensor_tensor(
            out=ss5[:, J:J + 1], in0=ss5[:, J:J + 1], in1=mhalf5, op=ALU.pow))
        nk, xn = xn5_r.next()
        S.op("dve", [x1k, "ss5_%d" % J], [nk], lambda: nc.vector.tensor_scalar(
            out=xn, in0=x1, scalar1=ss5[:, J:J + 1], scalar2=None, op0=ALU.mult))
        return (x1k, x1, nk, xn)

    def stageB5(J, st):
        x1k, x1, nk, xn = st
        tok = slice(J * 128, (J + 1) * 128)
        tb, tbank = ring5.next()
        tv = bf_view(tbank).rearrange("p (c t) -> p c t", t=128)
        for c in range(8):
            S.op("pe", [nk, "const"], [tb], lambda: nc.tensor.transpose(
                out=tv[:, c, :], in_=xn[:, c * 128:(c + 1) * 128], identity=ident))
        tk, xnT = xnT_r.next()
        S.op("act", [tb], [tk], lambda: nc.scalar.activation(out=xnT, in_=tv, func=AF.Copy))
        return (x1k, x1, tk, xnT)

    def stageB5b(J, st):
        x1k, x1, tk, xnT = st
        tok = slice(J * 128, (J + 1) * 128)
        sk, sg = sg5_r.next()
        ok, o5 = o5_r.next()
        pbs = []
        for half in range(2):
            hc = slice(half * 512, (half + 1) * 512)
            pb2, pbank2 = ring5.next()
            for c in range(2):
                S.op("pe", ["pTb", "Wpl"], [pb2], lambda: nc.tensor.matmul(
                    pbank2, lhsT=pTb[:, c, tok], rhs=Wpl[:, c, hc], start=(c == 0), stop=(c == 1)))
            pbs.append((pb2, pbank2))
        for half in range(2):
            hc = slice(half * 512, (half + 1) * 512)
            gb2, gbank2 = ring5.next()
            for c in range(8):
                S.op("pe", [tk, "Wpg"], [gb2], lambda: nc.tensor.matmul(
                    gbank2, lhsT=xnT[:, c, :], rhs=Wpg[:, c, hc], start=(c == 0), stop=(c == 7)))
            skh, okh = "%s_%d" % (sk, half), "%s_%d" % (ok, half)
            S.op("act", [gb2], [skh], lambda: nc.scalar.activation(out=sg[:, hc], in_=gbank2, func=AF.Sigmoid))
            pb2, pbank2 = pbs[half]
            S.op("dve", [pb2, skh], [skh], lambda: nc.vector.tensor_tensor(out=sg[:, hc], in0=pbank2, in1=sg[:, hc], op=ALU.mult))
            S.op("dve", [skh, x1k], [okh], lambda: nc.vector.tensor_tensor(out=o5[:, hc], in0=sg[:, hc], in1=x1[:, hc], op=ALU.add))
        S.dma("pool", ok, [ok + "_0", ok + "_1"], [ok + "_0", ok + "_1"], out_d[tok, :], o5)

    st5 = {0: stageA5b(0, stageA5(0))}
    for J in range(NT):
        a_next = stageA5(J + 1) if J + 1 < NT else None
        b_mid = stageB5(J, st5.pop(J))
        if a_next is not None:
            st5[J + 1] = stageA5b(J + 1, a_next)
        stageB5b(J, b_mid)
    S.barrier()
    return nc, S


def _host_consts(hf):
    p = np.arange(128)
    kp, qp = p[:, None], p[None, :]
    ident = np.eye(128, dtype=np.float32)
    m_prev = np.where(kp >= qp, 0.0, NEG).astype(np.float32)
    m_cur = np.where(kp <= qp, 0.0, NEG).astype(np.float32)
    m_halo = m_prev if hf == 1 else np.full((128, 128), NEG, np.float32)
    tri = (kp <= qp).astype(np.float32)
    cbf = np.concatenate([ident, m_prev, m_cur, m_halo, tri, m_prev, m_cur, m_prev, m_cur,
                          m_halo, m_cur, m_halo, m_cur], axis=1).astype(NPBF)
    return cbf, tri


_CACHE = {}


def _prep_inputs(x, p, positions, norm_g, w_in, qk_norm_q, qk_norm_k, gla_gate_w2, gla_gate_b,
                 gla_norm_g, w_att_proj, w_gla_proj, w_out, ple_norm_g, w_ple_gate, w_ple):
    blocks, NB = _blocks()
    half = 8
    inv = np.power(np.float32(500000.0), -np.arange(half, dtype=np.float32) * np.float32(2.0) / np.float32(16)).astype(np.float32)
    invf = np.ascontiguousarray(np.broadcast_to(inv[None, :], (128, 8))).astype(np.float32)
    w2aug = np.zeros((32, 512), np.float32)
    w2aug[0:16] = gla_gate_w2[0]
    w2aug[16] = gla_gate_b[0]
    shared = {
        "invf": invf,
        "ng": np.ascontiguousarray(norm_g[0].reshape(8, 128).T),
        "png": np.ascontiguousarray(ple_norm_g[0].reshape(8, 128).T),
        "gqb": np.ascontiguousarray(np.broadcast_to(qk_norm_q[0][None, :], (128, 64))),
        "gkb": np.ascontiguousarray(np.broadcast_to(qk_norm_k[0][None, :], (128, 64))),
        "gnb": np.ascontiguousarray(np.broadcast_to(gla_norm_g[0][None, :], (128, 256))),
        "w2aug": w2aug,
        "w_in": np.ascontiguousarray(w_in[0]), "w_pa": np.ascontiguousarray(w_att_proj[0]),
        "w_pb": np.ascontiguousarray(w_gla_proj[0]), "w_out": np.ascontiguousarray(w_out[0]),
        "w_pg": np.ascontiguousarray(w_ple_gate[0]), "w_ple": np.ascontiguousarray(w_ple[0]),
    }
    in_maps = []
    for core in range(8):
        b, hf = core // 2, core % 2
        cbf, tri = _host_consts(hf)
        xs = np.zeros((4096, D), np.float32)
        posl = np.zeros((4096,), np.int32)
        if hf == 1:
            xs[:] = x[b]
            posl[:] = positions[b]
        else:
            xs[T:] = x[b, :T]
            posl[T:] = positions[b, :T]
        posb = np.zeros((128, NB), np.int32)
        pp = np.arange(128)
        for d, gi in GROUPS:
            for (r, B, halo, bid) in blocks[d]:
                posb[:, bid] = posl[r + d * (128 * B + pp)]
        m = dict(shared)
        m.update({"xs": xs, "posb": posb, "pT": np.ascontiguousarray(p[0, b, hf * T:(hf + 1) * T, :].T),
                  "cbf": cbf, "trif": tri})
        in_maps.append(m)
    return in_maps


def kernel(**inputs):
    inputs = {k: np.asarray(v) for k, v in inputs.items()}
    in_maps = _prep_inputs(**inputs)
    if "nc" not in _CACHE:
        _CACHE["nc"] = build_program(False)[0]
    res = run_bass_kernel_spmd(_CACHE["nc"], in_maps, core_ids=list(range(8)))
    out = np.zeros((4, 4096, D), np.float32)
    for core in range(8):
        b, hf = core // 2, core % 2
        out[b, hf * T:(hf + 1) * T] = res.results[core]["out"]
    return out
```

```python
import numpy as np
import ml_dtypes
import concourse.bass as bass
import concourse.mybir as mybir
from concourse.bass_utils import run_bass_kernel_spmd

F32 = mybir.dt.float32
BF16 = mybir.dt.bfloat16
I32 = mybir.dt.int32
AF = mybir.ActivationFunctionType
ALU = mybir.AluOpType
AX = mybir.AxisListType
NPBF = ml_dtypes.bfloat16

D = 1024
T = 2048
NT = T // 128
EPS = 1e-6
C_QA, C_KA, C_VA, C_ZA, C_QG, C_KG, C_VG, C_GLR, C_ZG, C_GA, C_GB = (
    0, 1536, 3072, 4608, 5120, 5632, 6144, 7168, 7184, 8208, 9232)
GROUPS = ((16, 2), (4, 1), (1, 0))
NEG = -30000.0


class Sched:
    def __init__(self, nc):
        self.nc = nc
        self.E = {"pe": nc.tensor, "dve": nc.vector, "act": nc.scalar,
                  "pool": nc.gpsimd, "sp": nc.sync}
        self.sems, self.cnt, self.waited = {}, {}, {}
        self.lastw, self.readers = {}, {}
        self.ninst = 0
        self.nwait = 0
        for e in ("pe", "dve", "act", "pool"):
            self._sem("E_" + e)

    def _sem(self, key):
        if key not in self.sems:
            self.sems[key] = self.nc.alloc_semaphore("s_" + key)
            self.cnt[key] = 0
        return self.sems[key]

    def _need(self, eng, toks):
        best = {}
        for t in toks:
            if t is None:
                continue
            k, v = t
            if v > best.get(k, 0):
                best[k] = v
        for k, v in best.items():
            if eng == "pe" and k == "E_pe":
                continue
            if self.waited.get((eng, k), 0) >= v:
                continue
            self.E[eng].wait_ge(self.sems[k], v)
            self.waited[(eng, k)] = v
            self.nwait += 1

    def deps(self, eng, reads, writes):
        toks = []
        for r in reads:
            toks.append(self.lastw.get(r))
        for w in writes:
            toks.append(self.lastw.get(w))
            toks.extend(self.readers.get(w, ()))
        self._need(eng, toks)

    def commit(self, tok, reads, writes):
        for r in reads:
            self.readers.setdefault(r, []).append(tok)
        for w in writes:
            self.lastw[w] = tok
            self.readers[w] = []

    def op(self, eng, reads, writes, fn):
        self.deps(eng, reads, writes)
        inst = fn()
        k = "E_" + eng
        self.cnt[k] += 1
        inst.then_inc(self.sems[k], 1)
        self.commit((k, self.cnt[k]), reads, writes)
        self.ninst += 1

    def dma(self, q, slot, reads, writes, out, in_):
        self.deps(q, reads, writes)
        k = "D_" + slot
        self._sem(k)
        inst = self.E[q].dma_start(out=out, in_=in_)
        self.cnt[k] += 16
        inst.then_inc(self.sems[k], 16)
        self.commit((k, self.cnt[k]), reads, writes)
        self.ninst += 1

    def pe_drain(self):
        v = self.cnt["E_pe"]
        if v > self.waited.get(("pe", "E_pe"), 0):
            self.E["pe"].wait_ge(self.sems["E_pe"], v)
            self.waited[("pe", "E_pe")] = v
            self.nwait += 1

    def barrier(self):
        toks = [(k, v) for k, v in self.cnt.items() if v > 0]
        for e in ("pe", "dve", "act", "pool", "sp"):
            self._need(e, toks)
        self.lastw.clear()
        self.readers.clear()


class Ring:
    def __init__(self, name, aps):
        self.name, self.aps, self.i = name, aps, 0

    def next(self):
        j = self.i % len(self.aps)
        self.i += 1
        return "%s%d" % (self.name, j), self.aps[j]


def _blocks():
    out = {}
    bid = 0
    for d, gi in GROUPS:
        lst = []
        for r in range(d):
            for B in range(16 // d - 1, 32 // d):
                lst.append((r, B, B == 16 // d - 1, bid))
                bid += 1
        out[d] = lst
    return out, bid


def build_program(debug=False, stop_after=99):
    nc = bass.Bass("TRN2", target_bir_lowering=False)
    S = Sched(nc)
    blocks, NB = _blocks()

    def din(name, shape, dt):
        return nc.dram_tensor(name, list(shape), dt, kind="ExternalInput").ap()

    xs = din("xs", [4096, D], F32)
    posb = din("posb", [128, NB], I32)
    invf = din("invf", [128, 8], F32)
    pT_d = din("pT", [256, T], F32)
    consts_d = din("cbf", [128, 13 * 128], BF16)
    trif_d = din("trif", [128, 128], F32)
    ng_d = din("ng", [128, 8], F32)
    png_d = din("png", [128, 8], F32)
    gq_d = din("gqb", [128, 64], F32)
    gk_d = din("gkb", [128, 64], F32)
    gn_d = din("gnb", [128, 256], F32)
    w2_d = din("w2aug", [32, 512], F32)
    w_in = din("w_in", [D, 10256], F32)
    w_pa = din("w_pa", [512, D], F32)
    w_pb = din("w_pb", [D, D], F32)
    w_out = din("w_out", [D, D], F32)
    w_pg = din("w_pg", [D, D], F32)
    w_ple = din("w_ple", [256, D], F32)
    out_d = nc.dram_tensor("out", [T, D], F32, kind="ExternalOutput").ap()
    scr = {4: nc.dram_tensor("scr4", [T, 520], BF16, kind="Internal").ap(),
           16: nc.dram_tensor("scr16", [T, 520], BF16, kind="Internal").ap()}
    dbg = {}
    if debug:
        dbg["ua"] = nc.dram_tensor("dbg_ua", [128, 4, T], BF16, kind="ExternalOutput").ap()
        dbg["ub"] = nc.dram_tensor("dbg_ub", [128, 8, T], BF16, kind="ExternalOutput").ap()
        dbg["y"] = nc.dram_tensor("dbg_y", [128, 8, T], BF16, kind="ExternalOutput").ap()

    _CNT = [0]

    class Arena:
        def __init__(self, base):
            self.top = base
            self.n = 0

        def alloc(self, shape, dt):
            nbytes = int(np.prod(shape[1:])) * (4 if dt in (F32, I32) else 2)
            off = (self.top + 63) // 64 * 64
            self.top = off + nbytes
            self.n += 1
            assert self.top <= 229344, ("SBUF overflow", self.top)
            _CNT[0] += 1
            return nc.alloc_sbuf_tensor_at("sb%d" % _CNT[0], list(shape), dt, offset=off).ap()

    P = Arena(16512)
    cbf = P.alloc([128, 13 * 128], BF16)
    ident, m_prev, m_cur, m_halo, tri = (cbf[:, i * 128:(i + 1) * 128] for i in range(5))
    mask4 = cbf[:, 5 * 128:9 * 128]
    mask4h = cbf[:, 9 * 128:13 * 128]
    trif = P.alloc([128, 128], F32)
    ngT = P.alloc([128, 8], F32)
    pngT = P.alloc([128, 8], F32)
    gqb = P.alloc([128, 64], F32)
    gkb = P.alloc([128, 64], F32)
    gnb = P.alloc([128, 256], F32)
    w2aug = P.alloc([32, 512], F32)
    hTo = P.alloc([128, 8, T], BF16)
    hTh = P.alloc([128, 8, T], BF16)
    uaT = P.alloc([128, 4, T], BF16)
    XBASE = P.top

    banks = [nc.alloc_psum_tensor("bank%d" % i, [128, 512], F32).ap() for i in range(8)]

    def bf_view(bank):
        return bank.bitcast(BF16)

    for i, (dst, src) in enumerate(((cbf, consts_d), (trif, trif_d), (ngT, ng_d), (pngT, png_d), (gqb, gq_d),
                                    (gkb, gk_d), (gnb, gn_d), (w2aug, w2_d))):
        S.dma("sp", "c%d" % i, [], ["const"], dst, src)
    S.barrier()

    def make_loader(arena, width=1024, nslot=3, name="stg"):
        stg = Ring(name, [arena.alloc([128, width], F32) for _ in range(nslot)])

        def load_w(dst, key, src, ncols, nchunk, scale=None, eng="dve", q="sp"):
            for c in range(nchunk):
                for c0 in range(0, ncols, width):
                    cw = min(width, ncols - c0)
                    sk, st = stg.next()
                    S.dma(q, sk, [], [sk], st[:, 0:cw], src[c * 128:(c + 1) * 128, c0:c0 + cw])
                    o = dst[:, c, c0:c0 + cw]
                    if scale is None:
                        S.op(eng, [sk], [key], lambda: S.E[eng].tensor_copy(out=o, in_=st[:, 0:cw]))
                    else:
                        S.op(eng, [sk, "const"], [key], lambda: S.E[eng].tensor_scalar(
                            out=o, in0=st[:, 0:cw], scalar1=scale[:, c:c + 1], scalar2=None, op0=ALU.mult))
        def load_steps(dst, key, src, ncols, nchunk, scale=None, eng="dve", q="sp"):
            steps = []
            for c in range(nchunk):
                for c0 in range(0, ncols, width):
                    cw = min(width, ncols - c0)

                    def dma(c=c, c0=c0, cw=cw):
                        sk, st = stg.next()
                        S.dma(q, sk, [], [sk], st[:, 0:cw], src[c * 128:(c + 1) * 128, c0:c0 + cw])
                        return sk, st

                    def cast(h, c=c, c0=c0, cw=cw):
                        sk, st = h
                        o = dst[:, c, c0:c0 + cw]
                        if scale is None:
                            if eng == "act":
                                S.op("act", [sk], [key], lambda: nc.scalar.activation(out=o, in_=st[:, 0:cw], func=AF.Copy))
                            else:
                                S.op(eng, [sk], [key], lambda: S.E[eng].tensor_copy(out=o, in_=st[:, 0:cw]))
                        elif eng == "act":
                            S.op("act", [sk, "const"], [key], lambda: nc.scalar.activation(
                                out=o, in_=st[:, 0:cw], func=AF.Copy, scale=scale[:, c:c + 1]))
                        else:
                            S.op(eng, [sk, "const"], [key], lambda: S.E[eng].tensor_scalar(
                                out=o, in0=st[:, 0:cw], scalar1=scale[:, c:c + 1], scalar2=None, op0=ALU.mult))
                    steps.append((dma, cast))
            return steps
        load_w.steps = load_steps
        load_w.nslot = nslot
        return load_w

    class Prefetch:
        def __init__(self):
            self.q = []
            self.inflight = []

        def add(self, steps):
            self.q.extend(steps)

        def _casts(self):
            for cast, h in self.inflight:
                cast(h)
            self.inflight = []

        def pump(self, n=1):
            self._casts()
            for _ in range(min(n, len(self.q))):
                dma, cast = self.q.pop(0)
                self.inflight.append((cast, dma()))

        def flush(self, ring=2):
            while self.q or self.inflight:
                self.pump(ring)

    pf = Prefetch()
    AW = Arena(XBASE)
    wbuf = [AW.alloc([128, 8, 1536], BF16) for _ in range(2)]
    XW = AW.top

    def group_w_steps(loader, gidx, eng, q="sp"):
        d_, gi_ = GROUPS[gidx]
        b = wbuf[gidx % 2]
        st = []
        for (o0, c0) in ((0, C_QA + gi_ * 512), (512, C_KA + gi_ * 512), (1024, C_VA + gi_ * 512)):
            st += loader.steps(b[:, :, o0:o0 + 512], "W%d" % (gidx % 2), w_in[:, c0:c0 + 512], 512, 8, scale=ngT, eng=eng, q=q)
        return st

    AT = Arena(XW)
    cos2T = AT.alloc([128, NB, 16], F32)
    sin2T = AT.alloc([128, NB, 16], F32)
    XT = AT.top
    A1 = Arena(XT)
    load_w1 = make_loader(A1, 512, 3, name="stgp")
    pf.add(group_w_steps(load_w1, 0, "dve", q="pool"))
    xt_r = Ring("xt", [A1.alloc([128, D], F32) for _ in range(8)])
    xn_r = Ring("xn", [A1.alloc([128, D], BF16) for _ in range(4)])
    junk = A1.alloc([128, D], BF16)
    ss1 = A1.alloc([128, 32], F32)
    posi = A1.alloc([128, NB], I32)
    posf = A1.alloc([128, NB], F32)
    ang = A1.alloc([128, NB, 8], F32)
    ang2 = A1.alloc([128, NB, 8], F32)
    invt = A1.alloc([128, 8], F32)
    pring1 = Ring("pb", [(banks[i]) for i in range(4)])

    S.dma("sp", "posi", [], ["posi"], posi, posb)
    S.dma("sp", "invt", [], ["invt"], invt, invf)
    S.op("dve", ["posi"], ["posf"], lambda: nc.vector.tensor_copy(out=posf, in_=posi))
    S.op("dve", ["posf", "invt"], ["ang"], lambda: nc.vector.tensor_tensor(
        out=ang, in0=posf.to_broadcast([128, NB, 8]), in1=invt.unsqueeze(1).to_broadcast([128, NB, 8]), op=ALU.mult))
    PI = float(np.pi)
    MAGIC = 12582912.0
    C1 = 6.28125
    C2 = 2.0 * PI - C1
    kf = A1.alloc([128, NB, 8], F32)
    for (dst, shift, sgn) in ((sin2T, 0.0, -1.0), (cos2T, 0.5 * PI, 1.0)):
        S.op("dve", ["ang"], ["ang2"], lambda: nc.vector.tensor_scalar(
            out=ang2, in0=ang, scalar1=shift, scalar2=None, op0=ALU.add))
        S.op("dve", ["ang2"], ["kf"], lambda: nc.vector.tensor_scalar(
            out=kf, in0=ang2, scalar1=1.0 / (2.0 * PI), scalar2=MAGIC, op0=ALU.mult, op1=ALU.add))
        S.op("dve", ["kf"], ["kf"], lambda: nc.vector.tensor_scalar(
            out=kf, in0=kf, scalar1=-MAGIC, scalar2=None, op0=ALU.add))
        S.op("dve", ["kf", "ang2"], ["ang2"], lambda: nc.vector.scalar_tensor_tensor(
            out=ang2.rearrange("p a b -> p (a b)"), in0=kf.rearrange("p a b -> p (a b)"), scalar=-C1,
            in1=ang2.rearrange("p a b -> p (a b)"), op0=ALU.mult, op1=ALU.add))
        S.op("dve", ["kf", "ang2"], ["ang2"], lambda: nc.vector.scalar_tensor_tensor(
            out=ang2.rearrange("p a b -> p (a b)"), in0=kf.rearrange("p a b -> p (a b)"), scalar=-C2,
            in1=ang2.rearrange("p a b -> p (a b)"), op0=ALU.mult, op1=ALU.add))
        S.op("dve", ["ang2"], ["ang2"], lambda: nc.vector.tensor_scalar(
            out=ang2, in0=ang2, scalar1=-PI, scalar2=PI, op0=ALU.max, op1=ALU.min))
        S.op("act", ["ang2"], ["rot"], lambda: nc.scalar.activation(out=dst[:, :, 0:8], in_=ang2, func=AF.Sin, scale=sgn))
        S.op("act", ["ang2"], ["rot"], lambda: nc.scalar.activation(out=dst[:, :, 8:16], in_=ang2, func=AF.Sin))

    def stageA1(j):
        xk, xt = xt_r.next()
        S.dma("sp", xk, [], [xk], xt, xs[j * 128:(j + 1) * 128, :])
        S.op("act", [xk], ["junk", "ss1_%d" % j], lambda: nc.scalar.activation(
            out=junk, in_=xt, func=AF.Square, accum_out=ss1[:, j:j + 1]))
        S.op("act", ["ss1_%d" % j], ["ss1_%d" % j], lambda: nc.scalar.activation(
            out=ss1[:, j:j + 1], in_=ss1[:, j:j + 1], func=AF.Sqrt, scale=1.0 / D, bias=EPS))
        S.op("dve", ["ss1_%d" % j], ["ss1_%d" % j], lambda: nc.vector.reciprocal(out=ss1[:, j:j + 1], in_=ss1[:, j:j + 1]))
        nk, xn = xn_r.next()
        S.op("dve", [xk, "ss1_%d" % j], [nk], lambda: nc.vector.tensor_scalar(
            out=xn, in0=xt, scalar1=ss1[:, j:j + 1], scalar2=None, op0=ALU.mult))
        return nk, xn

    def stageB1(j, nk, xn):
        bk, bank = pring1.next()
        pv = bf_view(bank)[:, 0:1024].rearrange("p (c t) -> p c t", t=128)
        for c in range(8):
            S.op("pe", [nk, "const"], [bk], lambda: nc.tensor.transpose(
                out=pv[:, c, :], in_=xn[:, c * 128:(c + 1) * 128], identity=ident))
        dst = (hTh if j < 16 else hTo)[:, :, (j % 16) * 128:(j % 16 + 1) * 128]
        if j % 2 == 0:
            S.op("act", [bk], ["hT%d" % j], lambda: nc.scalar.activation(out=dst, in_=pv, func=AF.Copy))
        else:
            S.op("dve", [bk], ["hT%d" % j], lambda: nc.vector.tensor_copy(out=dst, in_=pv))

    st1 = {0: stageA1(0), 1: stageA1(1)}
    for j in range(32):
        if j + 2 < 32:
            st1[j + 2] = stageA1(j + 2)
        stageB1(j, *st1.pop(j))
        pf.pump(1)
    pf.flush()
    S.barrier()

    if stop_after <= 1:
        return nc, S

    def hsrc(L0, n, step, c):
        src, s = (hTh, L0) if L0 < T else (hTo, L0 - T)
        return src[:, c, s:s + (n - 1) * step + 1:step]

    A2 = Arena(XT)
    load_w = make_loader(A2, 512, 3)
    Wsel = {}
    ost_r = Ring("ost", [A2.alloc([128, 2, 260], BF16) for _ in range(3)])
    nat_r = {4: Ring("nat4_", [A2.alloc([128, 2, 260], BF16) for _ in range(4)]),
             16: Ring("nat16_", [A2.alloc([128, 2, 260], BF16) for _ in range(4)])}
    sqs_r = Ring("sqs", [A2.alloc([128, 512], F32) for _ in range(2)])
    ssq_r = Ring("ssq", [A2.alloc([128, 8], F32) for _ in range(4)])
    tmpn_r = Ring("tmpn", [A2.alloc([128, 8, 64], F32) for _ in range(4)])
    t16_r = Ring("t16", [A2.alloc([128, 8, 16], F32) for _ in range(2)])
    rtmp_r = Ring("rtmp", [A2.alloc([128, 2, 8, 16], F32) for _ in range(2)])
    qn_r = Ring("qn", [A2.alloc([128, 8, 64], BF16) for _ in range(3)])
    kn_r = Ring("kn", [A2.alloc([128, 8, 64], BF16) for _ in range(3)])
    qT_r = Ring("qT", [A2.alloc([128, 4, 128], BF16) for _ in range(4)])
    kT_r = Ring("kT", [A2.alloc([128, 4, 128], BF16) for _ in range(5)])
    V_r = Ring("V", [A2.alloc([128, 8, 65], BF16) for _ in range(7)])
    PT_r = Ring("PT", [A2.alloc([128, 512], BF16) for _ in range(4)])
    pmA = A2.alloc([128, 2, 260], F32)
    rl = A2.alloc([128, 8], F32)
    uab_r = Ring("uab", [A2.alloc([128, 8, 64], BF16) for _ in range(2)])
    pending_ua = []
    pring = Ring("pb", [banks[i] for i in range(3)])
    trbank = ("pb3", banks[3])
    sring = {16: Ring("sb", [banks[4], banks[5]]), 4: Ring("sb", [banks[4], banks[5]]), 1: Ring("sb", [banks[4], banks[5]])}
    PVb = [banks[6], banks[7]]

    for Vt0 in V_r.aps:
        S.op("pool", [], ["Vinit"], lambda: nc.gpsimd.memset(Vt0[:, :, 64:65], 1.0))
    S.barrier()

    def norm_early(ps_key, ps):
        sk, sqs = sqs_r.next()
        tk, raw = tmpn_r.next()
        S.op("act", [ps_key], [sk], lambda: nc.scalar.activation(out=sqs, in_=ps, func=AF.Square))
        S.op("act", [ps_key], [tk], lambda: nc.scalar.activation(
            out=raw.rearrange("p h e -> p (h e)"), in_=ps, func=AF.Copy))
        return (sk, sqs, tk, raw)

    def norm_steps(early, gb, bid, out_ring):
        sk, sqs, tk, raw = early
        qk_, ssq = ssq_r.next()
        ok, on = out_ring.next()
        k16, t16 = t16_r.next()
        rk, rt = rtmp_r.next()
        c2 = cos2T[:, bid, :].unsqueeze(1).to_broadcast([128, 8, 16])
        s2 = sin2T[:, bid, :].rearrange("p (two e) -> p two e", two=2).unsqueeze(1).to_broadcast([128, 8, 2, 8])
        sw = t16.rearrange("p h (two e) -> p h two e", two=2)[:, :, ::-1, :]
        steps = [
            lambda: S.op("dve", [sk], [qk_], lambda: nc.vector.tensor_reduce(
                out=ssq, in_=sqs.rearrange("p (h e) -> p h e", e=64), axis=AX.X, op=ALU.add)),
            lambda: S.op("act", [qk_], [qk_], lambda: nc.scalar.activation(
                out=ssq, in_=ssq, func=AF.Ln, scale=1.0 / 64, bias=EPS)),
            lambda: S.op("act", [qk_], [qk_], lambda: nc.scalar.activation(out=ssq, in_=ssq, func=AF.Exp, scale=-0.5)),
            lambda: S.op("dve", [tk, qk_], [tk], lambda: nc.vector.tensor_tensor(
                out=raw, in0=raw, in1=ssq.to_broadcast([128, 8, 64]), op=ALU.mult)),
            lambda: S.op("dve", [tk], [k16], lambda: nc.vector.tensor_tensor(
                out=t16, in0=raw[:, :, 0:16], in1=gb[:, 0:16].unsqueeze(1).to_broadcast([128, 8, 16]), op=ALU.mult)),
            lambda: S.op("dve", [tk], [ok], lambda: nc.vector.tensor_tensor(
                out=on[:, :, 16:64], in0=raw[:, :, 16:64], in1=gb[:, 16:64].unsqueeze(1).to_broadcast([128, 8, 48]), op=ALU.mult)),
        ]
        steps.append(lambda: S.op("dve", [k16, "rot"], [rk], lambda: nc.vector.tensor_tensor(
            out=rt[:, 0], in0=t16, in1=c2, op=ALU.mult)))
        steps.append(lambda: S.op("dve", [k16, "rot"], [rk], lambda: nc.vector.tensor_tensor(
            out=rt[:, 1].rearrange("p h (two e) -> p h two e", two=2), in0=sw, in1=s2, op=ALU.mult)))
        steps.append(lambda: S.op("dve", [rk], [ok], lambda: nc.vector.tensor_tensor(
            out=on[:, :, 0:16], in0=rt[:, 0], in1=rt[:, 1], op=ALU.add)))
        return ok, on, steps

    def stageA(d, blk):
        r, B, halo, bid = blk
        L0 = r + d * 128 * B
        info = {"blk": blk, "halo": halo}
        kb, kbank = pring.next()
        for c in range(8):
            S.op("pe", [Wsel["key"]], [kb], lambda: nc.tensor.matmul(
                kbank, lhsT=hsrc(L0, 128, d, c), rhs=Wsel["k"][:, c, :], start=(c == 0), stop=(c == 7)))
        info["kearly"] = norm_early(kb, kbank)
        vb, vbank = pring.next()
        for c in range(8):
            S.op("pe", [Wsel["key"]], [vb], lambda: nc.tensor.matmul(
                vbank, lhsT=hsrc(L0, 128, d, c), rhs=Wsel["v"][:, c, :], start=(c == 0), stop=(c == 7)))
        Vk, Vt = V_r.next()
        info["V"] = (Vk, Vt)
        S.op("act", [vb, "Vinit"], [Vk], lambda: nc.scalar.activation(
            out=Vt[:, :, 0:64], in_=vbank.rearrange("p (h e) -> p h e", e=64), func=AF.Copy))
        late = []
        info["late"] = late
        if not halo:
            qb, qbank = pring.next()
            for c in range(8):
                S.op("pe", [Wsel["key"]], [qb], lambda: nc.tensor.matmul(
                    qbank, lhsT=hsrc(L0, 128, d, c), rhs=Wsel["q"][:, c, :], start=(c == 0), stop=(c == 7)))
            late.append(lambda: info.__setitem__("qearly", norm_early(qb, qbank)))
            if d == 1:
                J0 = B - 16
                for dd in (4, 16):
                    nk_, nt_ = nat_r[dd].next()
                    S.dma("sp", nk_, [], [nk_], nt_.rearrange("p b e -> p (b e)"), scr[dd][J0 * 128:(J0 + 1) * 128, :])
                    info["nat%d" % dd] = (nk_, nt_)
        return info

    def stageA2(info):
        bid = info["blk"][3]
        knk, kn, ksteps = norm_steps(info["kearly"], gkb, bid, kn_r)
        info["kn"] = (knk, kn)
        qsteps = []
        if "qearly" in info:
            qnk, qn, qsteps = norm_steps(info["qearly"], gqb, bid, qn_r)
            info["qn"] = (qnk, qn)
        for i in range(max(len(ksteps), len(qsteps))):
            if i < len(ksteps):
                ksteps[i]()
            if i < len(qsteps):
                qsteps[i]()

    def stageB(info):
        bk, bank = trbank
        pvw = bf_view(bank).rearrange("p (w c t) -> p w c t", w=2, t=128)
        todo = []
        for w, name, ring, eng in ((0, "kn", kT_r, "dve"), (1, "qn", qT_r, "act")):
            if name not in info:
                continue
            sk, src = info[name]
            flat = src.rearrange("p h e -> p (h e)")
            for c in range(4):
                S.op("pe", [sk, "const"], [bk], lambda: nc.tensor.transpose(
                    out=pvw[:, w, c, :], in_=flat[:, c * 128:(c + 1) * 128], identity=ident))
            todo.append((w, ring, eng))
        info["evac_todo"] = (bk, pvw, todo)

    def stageB_evac(info):
        bk, pvw, todo = info.pop("evac_todo")
        for w, ring, eng in todo:
            ok, o = ring.next()
            if eng == "act":
                S.op("act", [bk], [ok, bk], lambda: nc.scalar.activation(out=o, in_=pvw[:, w], func=AF.Copy))
            else:
                S.op("dve", [bk], [ok, bk], lambda: nc.vector.tensor_copy(out=o, in_=pvw[:, w]))
            info["kT" if w == 0 else "qT"] = (ok, o)

    def stageC(d, info, prev, mid=None, late=None):
        r, B, halo, bid = info["blk"]
        Bo = B - 16 // d
        J = Bo
        kTk, kT = info["kT"]
        qTk, qT = info["qT"]
        Vk, Vt = info["V"]
        pkTk, pkT = prev["kT"]
        pVk, pV = prev["V"]
        msk = mask4h if prev["halo"] else mask4

        def scores2(st):
            xb, xbank = sring[d].next()
            yb, ybank = sring[d].next()
            S.op("pe", ["const"], [xb], lambda: nc.tensor.matmul(
                xbank, lhsT=ident, rhs=msk, start=True, stop=False, skip_group_check=True))
            S.op("pe", ["const"], [yb], lambda: nc.tensor.matmul(
                ybank, lhsT=ident, rhs=msk, start=True, stop=False, skip_group_check=True))
            for pi in range(2):
                hp = 2 * st + pi
                for (kk, ktile, kkey) in ((0, pkT, pkTk), (1, kT, kTk)):
                    reg = slice((2 * pi + kk) * 128, (2 * pi + kk + 1) * 128)
                    last = (pi == 1 and kk == 1)
                    S.op("pe", [kkey, qTk], [xb], lambda: nc.tensor.matmul(
                        xbank[:, reg], lhsT=ktile[0:64, hp, :], rhs=qT[0:64, hp, :], start=False, stop=last,
                        skip_group_check=True))
                    S.op("pe", [kkey, qTk], [yb], lambda: nc.tensor.matmul(
                        ybank[:, reg], lhsT=ktile[64:128, hp, :], rhs=qT[64:128, hp, :], start=False, stop=last,
                        skip_group_check=True))
            res = []
            for (bk_, bank_) in ((xb, xbank), (yb, ybank)):
                Pk, PT = PT_r.next()
                S.op("act", [bk_], [Pk], lambda: nc.scalar.activation(out=PT, in_=bank_, func=AF.Exp, scale=0.125))
                res.append((Pk, PT))
            return res

        def pv2(st, which, Pk, PT):
            for pi in range(2):
                h = 2 * (2 * st + pi) + which
                reg = PVb[h // 4][:, (h % 4) * 65:(h % 4) * 65 + 65]
                S.op("pe", [Pk, pVk], ["PV%d" % (h // 4)], lambda: nc.tensor.matmul(
                    reg, lhsT=PT[:, (2 * pi) * 128:(2 * pi + 1) * 128], rhs=pV[:, h, :], start=True, stop=False))
                S.op("pe", [Pk, Vk], ["PV%d" % (h // 4)], lambda: nc.tensor.matmul(
                    reg, lhsT=PT[:, (2 * pi + 1) * 128:(2 * pi + 2) * 128], rhs=Vt[:, h, :], start=False, stop=True))

        r0 = scores2(0)
        if mid is not None:
            mid()
        pv2(0, 0, *r0[0])
        r1 = scores2(1)
        pv2(0, 1, *r0[1])
        pv2(1, 0, *r1[0])
        pv2(1, 1, *r1[1])
        if late is not None:
            late()
        if d != 1:
            ok_, ot = ost_r.next()
            S.op("dve", ["PV0"], [ok_], lambda: nc.vector.tensor_copy(out=ot[:, 0, :], in_=PVb[0][:, 0:260]))
            S.op("act", ["PV1"], [ok_], lambda: nc.scalar.activation(out=ot[:, 1, :], in_=PVb[1][:, 0:260], func=AF.Copy))
            row0 = (512 * Bo + r) if d == 4 else r
            S.dma("sp", ok_, [ok_], [ok_], scr[d][row0:row0 + 127 * d + 1:d, :], ot.rearrange("p b e -> p (b e)"))
            return
        for hb in range(2):
            S.op("dve", ["PV%d" % hb, info["nat4"][0]], ["pmA"], lambda: nc.vector.tensor_tensor(
                out=pmA[:, hb, :], in0=PVb[hb][:, 0:260], in1=info["nat4"][1][:, hb, :], op=ALU.add))
            S.op("dve", ["pmA", info["nat16"][0]], ["pmA"], lambda: nc.vector.tensor_tensor(
                out=pmA[:, hb, :], in0=pmA[:, hb, :], in1=info["nat16"][1][:, hb, :], op=ALU.add))
        pm4 = pmA.rearrange("p b (h e) -> p (b h) e", e=65)
        S.op("dve", ["pmA"], ["rl"], lambda: nc.vector.reciprocal(out=rl, in_=pm4[:, :, 64]))
        uk, uab = uab_r.next()
        S.op("dve", ["pmA", "rl"], [uk], lambda: nc.vector.tensor_tensor(
            out=uab, in0=pm4[:, :, 0:64], in1=rl.to_broadcast([128, 8, 64]), op=ALU.mult))

        def ua_transposes():
            bk, bank = trbank
            pv = bf_view(bank)[:, 0:512].rearrange("p (c t) -> p c t", t=128)
            flat = uab.rearrange("p h e -> p (h e)")
            for c in range(4):
                S.op("pe", [uk, "const"], [bk], lambda: nc.tensor.transpose(
                    out=pv[:, c, :], in_=flat[:, c * 128:(c + 1) * 128], identity=ident))
            S.op("dve", [bk], ["uaT"], lambda: nc.vector.tensor_copy(out=uaT[:, :, J * 128:(J + 1) * 128], in_=pv))
        pending_ua.append(ua_transposes)

    for gidx, (d, gi) in enumerate(GROUPS):
        pf.flush()
        wb = wbuf[gidx % 2]
        Wsel.update(key="W%d" % (gidx % 2), q=wb[:, :, 0:512], k=wb[:, :, 512:1024], v=wb[:, :, 1024:1536])
        if d == 1:
            S._need("sp", [(k, v) for k, v in S.cnt.items() if k.startswith("D_ost") and v > 0])
        if gidx + 1 < len(GROUPS):
            pf.add(group_w_steps(load_w, gidx + 1, "act"))
        else:
            ob = wbuf[(gidx + 1) % 2]
            okey = "W%d" % ((gidx + 1) % 2)
            pf.add(load_w.steps(ob[:, :, 0:512], okey, w_in[:, C_QG:C_QG + 512], 512, 8, scale=ngT, eng="act"))
            pf.add(load_w.steps(ob[:, :, 512:1024], okey, w_in[:, C_KG:C_KG + 512], 512, 8, scale=ngT, eng="act"))
            pf.add(load_w.steps(ob[:, :, 1024:1040], okey, w_in[:, C_GLR:C_GLR + 16], 16, 8, scale=ngT, eng="act"))
        L = blocks[d]
        infos = [None] * len(L)
        for i in range(len(L) + 3):
            if i < len(L):
                infos[i] = stageA(d, L[i])
            while pending_ua:
                pending_ua.pop(0)()
            hasB = 0 <= i - 2 < len(L)

            def mid(i=i, hasB=hasB):
                if hasB:
                    stageB(infos[i - 2])
                if i < len(L):
                    for f in infos[i].pop("late"):
                        f()
            doE = (lambda: stageB_evac(infos[i - 2])) if hasB else None
            if 0 <= i - 3 < len(L) and not infos[i - 3]["halo"]:
                stageC(d, infos[i - 3], infos[i - 4], mid=mid, late=doE)
            else:
                mid()
                if doE is not None:
                    doE()
            if i < len(L):
                stageA2(infos[i])
            pf.pump(2)
    while pending_ua:
        pending_ua.pop(0)()
    S.barrier()

    if stop_after <= 2:
        return nc, S
    pf.flush()
    A3 = Arena(XW)
    ubT = A3.alloc([128, 8, T], BF16)
    X3 = A3.top
    load_w = make_loader(A3, 1024, 2)
    _gb = wbuf[len(GROUPS) % 2]
    _vb = wbuf[(len(GROUPS) + 1) % 2]
    KG, KV = "W%d" % (len(GROUPS) % 2), "W%d" % ((len(GROUPS) + 1) % 2)
    Wqg = _gb[:, :, 0:512]
    Wkg = _gb[:, :, 512:1024]
    Wgl = _gb[:, :, 1024:1040]
    Wvg = _vb[:, :, 0:1024]
    Sf = A3.alloc([128, 4, 256], F32)
    Sb_ = A3.alloc([128, 4, 256], BF16)
    glr_r = Ring("glr", [A3.alloc([32, 128], F32) for _ in range(3)])
    ef = A3.alloc([128, 512], F32)
    spf = ef
    lgh_r = Ring("lgh", [_vb[:, c, 1024:1536] for c in (0, 1, 2)])
    lgl_r = Ring("lgl", [_vb[:, c, 1024:1536] for c in (3, 4, 5)])
    ek_r = Ring("ek", [A3.alloc([128, 4, 128], F32) for _ in range(2)])
    eq_r = Ring("eq", [A3.alloc([128, 4, 128], F32) for _ in range(2)])
    ktT_r = Ring("ktT", [_gb[:, 4 * i:4 * i + 4, 1040:1168] for i in range(2)])
    qtT_r = Ring("qtT", [_gb[:, 4 * i:4 * i + 4, 1168:1296] for i in range(2)])
    kt_r = Ring("kt", [_gb[:, 4 * i:4 * i + 4, 1296:1424] for i in range(2)])
    v_r = Ring("vg", [A3.alloc([128, 1024], BF16) for _ in range(2)])
    am_r = Ring("am", [_vb[:, c, 1024:1536].rearrange("p (h t) -> p h t", t=128) for c in (6, 7)])
    stmp_r = Ring("stmp", [A3.alloc([128, 256], F32) for _ in range(2)])
    ss3 = A3.alloc([128, 4], F32)
    junk3 = A3.alloc([128, 256], BF16)
    ubb_r = Ring("ubb", [A3.alloc([128, 1024], BF16) for _ in range(2)])
    ring3 = Ring("pb", [banks[i] for i in range(6)])
    oring3 = Ring("ob", [banks[6], banks[7]])

    S.op("dve", [], ["Sfinit"], lambda: nc.vector.memset(Sf, 0.0))
    S.op("dve", [], ["Sbinit"], lambda: nc.vector.memset(Sb_, 0.0))
    for a in glr_r.aps:
        S.op("dve", [], ["glrinit"], lambda: nc.vector.memset(a, 1.0))
    S.barrier()
    load_w(Wvg, KV, w_in[:, C_VG:C_VG + 1024], 1024, 8, scale=ngT)

    pre3 = {}

    pre3a = {}

    def gate_a(j):
        L0g = j * 128
        gb_, gbank = ring3.next()
        for c in range(8):
            S.op("pe", [KG], [gb_], lambda: nc.tensor.matmul(
                gbank[0:16, 0:128], lhsT=Wgl[:, c, :], rhs=hsrc(L0g, 128, 1, c), start=(c == 0), stop=(c == 7)))
        gk_, glr = glr_r.next()
        S.op("act", [gb_, "glrinit"], [gk_], lambda: nc.scalar.activation(out=glr[0:16, :], in_=gbank[0:16, 0:128], func=AF.Copy))
        pre3a[j] = (gk_, glr)

    def gate_b(j):
        gk_, glr = pre3a.pop(j)
        lb, lbank = ring3.next()
        S.op("pe", [gk_, "const"], [lb], lambda: nc.tensor.matmul(lbank, lhsT=glr, rhs=w2aug, start=True, stop=True))
        S.op("act", [lb], ["ef"], lambda: nc.scalar.activation(out=ef, in_=lbank, func=AF.Exp, scale=-1.0))
        S.op("act", ["ef"], ["ef"], lambda: nc.scalar.activation(out=spf, in_=ef, func=AF.Ln, bias=1.0))
        hk, lgh = lgh_r.next()
        lk, lgl = lgl_r.next()
        S.op("dve", ["ef"], [hk], lambda: nc.vector.tensor_scalar(
            out=lgh, in0=spf, scalar1=-1.0 / 16, scalar2=None, op0=ALU.mult))
        S.op("dve", ["ef", hk], [lk], lambda: nc.vector.scalar_tensor_tensor(
            out=lgl, in0=spf, scalar=-1.0 / 16, in1=lgh, op0=ALU.mult, op1=ALU.subtract))
        pre3[j] = (hk, lgh, lk, lgl)

    def stageA3(j):
        own = j >= 16
        info = {}
        L0 = j * 128
        J = j - 16
        hs = [hsrc(L0, 128, 1, c) for c in range(8)]
        hk, lgh, lk, lgl = pre3.pop(j)
        if j + 1 < 32:
            gate_a(j + 1)
        kb, kbank = ring3.next()
        for h in range(4):
            for c in range(8):
                S.op("pe", [KG], [kb], lambda: nc.tensor.matmul(
                    kbank[:, h * 128:(h + 1) * 128], lhsT=Wkg[:, c, h * 128:(h + 1) * 128], rhs=hs[c],
                    start=(c == 0), stop=(c == 7)))
        if own:
            qb, qbank = ring3.next()
            for h in range(4):
                for c in range(8):
                    S.op("pe", [KG], [qb], lambda: nc.tensor.matmul(
                        qbank[:, h * 128:(h + 1) * 128], lhsT=Wqg[:, c, h * 128:(h + 1) * 128], rhs=hs[c],
                        start=(c == 0), stop=(c == 7)))
        cb_, cbank = ring3.next()
        for h in range(4):
            reg = cbank[:, h * 128:(h + 1) * 128]
            S.op("pe", [hk, "const"], [cb_], lambda: nc.tensor.matmul(
                reg, lhsT=lgh[:, h * 128:(h + 1) * 128], rhs=tri, start=True, stop=False))
            S.op("pe", [lk, "const"], [cb_], lambda: nc.tensor.matmul(
                reg, lhsT=lgl[:, h * 128:(h + 1) * 128], rhs=tri, start=False, stop=True))
        ekk, ek = ek_r.next()
        eqk, eq = eq_r.next()
        c4 = cbank.rearrange("p (h t) -> p h t", t=128)
        S.op("act", [cb_], [ekk], lambda: nc.scalar.activation(out=ek, in_=c4, func=AF.Exp, scale=-1.0))
        S.op("act", [cb_], [eqk], lambda: nc.scalar.activation(out=eq, in_=c4, func=AF.Exp))
        vk, vg = v_r.next()
        for half in range(2):
            vb, vbank = ring3.next()
            for c in range(8):
                S.op("pe", [KV], [vb], lambda: nc.tensor.matmul(
                    vbank, lhsT=hs[c], rhs=Wvg[:, c, half * 512:(half + 1) * 512], start=(c == 0), stop=(c == 7)))
            S.op("act", [vb], [vk], lambda: nc.scalar.activation(
                out=vg[:, half * 512:(half + 1) * 512], in_=vbank, func=AF.Copy))
        ktk, ktT = ktT_r.next()
        S.op("dve", [kb, ekk], [ktk], lambda: nc.vector.tensor_tensor(
            out=ktT, in0=kbank.rearrange("p (h t) -> p h t", t=128), in1=ek, op=ALU.mult))
        if own:
            qtk, qtT = qtT_r.next()
            S.op("dve", [qb, eqk], [qtk], lambda: nc.vector.scalar_tensor_tensor(
                out=qtT, in0=qbank.rearrange("p (h t) -> p h t", t=128), scalar=128.0 ** -0.5,
                in1=eq, op0=ALU.mult, op1=ALU.mult))
        tb, tbank = ring3.next()
        tv = bf_view(tbank)[:, 0:512].rearrange("p (h t) -> p h t", t=128)
        for h in range(4):
            S.op("pe", [ktk, "const"], [tb], lambda: nc.tensor.transpose(out=tv[:, h, :], in_=ktT[:, h, :], identity=ident))
        ktok, kt = kt_r.next()
        S.op("dve", [tb], [ktok], lambda: nc.vector.tensor_copy(out=kt, in_=tv))
        if own:
            ab, abank = ring3.next()
            for h in range(4):
                S.op("pe", [ktk, qtk], [ab], lambda: nc.tensor.matmul(
                    abank[:, h * 128:(h + 1) * 128], lhsT=ktT[:, h, :], rhs=qtT[:, h, :], start=True, stop=True))
            amk, am = am_r.next()
            S.op("dve", [ab, "const"], [amk], lambda: nc.vector.tensor_tensor(
                out=am, in0=abank.rearrange("p (h t) -> p h t", t=128),
                in1=trif.unsqueeze(1).to_broadcast([128, 4, 128]), op=ALU.mult))
        if j + 1 < 32:
            gate_b(j + 1)
        info.update(dict(ktk=ktk, ktT=ktT, vk=vk, vg=vg, ktok=ktok, kt=kt, eqk=eqk, eq=eq))
        if own:
            info.update(dict(qtk=qtk, qtT=qtT, amk=amk, am=am))
        return info

    def stageB3(j, info):
        own = j >= 16
        J = j - 16
        ktk, ktT, vk, vg, ktok, kt, eqk, eq = (info[n] for n in ("ktk", "ktT", "vk", "vg", "ktok", "kt", "eqk", "eq"))
        if own:
            qtk, qtT, amk, am = (info[n] for n in ("qtk", "qtT", "amk", "am"))
            obanks = []
            for hp in range(2):
                ob, obank = oring3.next()
                obanks.append((ob, obank))
                for hh in range(2):
                    h = 2 * hp + hh
                    reg = obank[:, hh * 256:(hh + 1) * 256]
                    S.op("pe", [qtk, "Sb%d" % h], [ob], lambda: nc.tensor.matmul(
                        reg, lhsT=qtT[:, h, :], rhs=Sb_[:, h, :], start=True, stop=False))
                    S.op("pe", [amk, vk], [ob], lambda: nc.tensor.matmul(
                        reg, lhsT=am[:, h, :], rhs=vg[:, h * 256:(h + 1) * 256], start=False, stop=True))
        if j < 31:
            for hp in range(2):
                db, dbank = ring3.next()
                for hh in range(2):
                    h = 2 * hp + hh
                    S.op("pe", [ktok, vk], [db], lambda: nc.tensor.matmul(
                        dbank[:, hh * 256:(hh + 1) * 256], lhsT=kt[:, h, :], rhs=vg[:, h * 256:(h + 1) * 256],
                        start=True, stop=True))
                for hh in range(2):
                    h = 2 * hp + hh
                    dec = eq[:, h, 127:128]
                    stk, stmp = stmp_r.next()
                    S.op("dve", [db, "Sf%d" % h], [stk], lambda: nc.vector.tensor_tensor(
                        out=stmp, in0=dbank[:, hh * 256:(hh + 1) * 256], in1=Sf[:, h, :], op=ALU.add))
                    S.op("act", [stk, eqk], ["Sb%d" % h], lambda: nc.scalar.activation(
                        out=Sb_[:, h, :], in_=stmp, func=AF.Copy, scale=dec))
                    S.op("dve", [stk, eqk], ["Sf%d" % h], lambda: nc.vector.tensor_scalar(
                        out=Sf[:, h, :], in0=stmp, scalar1=dec, scalar2=None, op0=ALU.mult))
        if own:
            ubk, ubb = ubb_r.next()
            for hp in range(2):
                ob, obank = obanks[hp]
                for hh in range(2):
                    h = 2 * hp + hh
                    S.op("act", [ob], ["junk3", "ss3"], lambda: nc.scalar.activation(
                        out=junk3, in_=obank[:, hh * 256:(hh + 1) * 256], func=AF.Square, accum_out=ss3[:, h:h + 1]))
            S.op("act", ["ss3"], ["ss3"], lambda: nc.scalar.activation(out=ss3, in_=ss3, func=AF.Ln, scale=1.0 / 256, bias=EPS))
            S.op("act", ["ss3"], ["ss3"], lambda: nc.scalar.activation(out=ss3, in_=ss3, func=AF.Exp, scale=-0.5))
            for hp in range(2):
                ob, obank = obanks[hp]
                for hh in range(2):
                    h = 2 * hp + hh
                    S.op("dve", [ob, "ss3", "const"], [ubk], lambda: nc.vector.scalar_tensor_tensor(
                        out=ubb[:, h * 256:(h + 1) * 256], in0=obank[:, hh * 256:(hh + 1) * 256], scalar=ss3[:, h:h + 1],
                        in1=gnb, op0=ALU.mult, op1=ALU.mult))
            info["ubb"] = (ubk, ubb)

    def stageC3(j, info):
        J = j - 16
        ubk, ubb = info["ubb"]
        ub_, ubank = ring3.next()
        uv = bf_view(ubank).rearrange("p (c t) -> p c t", t=128)
        for c in range(8):
            S.op("pe", [ubk, "const"], [ub_], lambda: nc.tensor.transpose(
                out=uv[:, c, :], in_=ubb[:, c * 128:(c + 1) * 128], identity=ident))
        S.op("act", [ub_], ["ubT"], lambda: nc.scalar.activation(out=ubT[:, :, J * 128:(J + 1) * 128], in_=uv, func=AF.Copy))
    infos3 = {}
    gate_a(0)
    gate_b(0)
    infos3[0] = stageA3(0)
    for j in range(32):
        stageB3(j, infos3[j])
        if j - 1 >= 16:
            stageC3(j - 1, infos3.pop(j - 1))
        if j + 1 < 32:
            infos3[j + 1] = stageA3(j + 1)
    stageC3(31, infos3.pop(31))
    S.barrier()

    if stop_after <= 3:
        return nc, S
    A4 = Arena(XBASE)
    stg4 = Ring("stg4_", [A4.alloc([128, 8, 128], F32) for _ in range(4)])
    wfc_r = Ring("wfc", [A4.alloc([128, 28, 128], BF16) for _ in range(3)])
    sg_r = Ring("sg", [A4.alloc([128, 512], F32) for _ in range(2)])
    y1_r = Ring("y1", [A4.alloc([128, 512], F32) for _ in range(2)])
    assert A4.top <= XW
    A3b = Arena(X3)
    zt_r = Ring("zt", [A3b.alloc([128, 8, 128], BF16) for _ in range(3)])
    sz_r = Ring("sz", [A3b.alloc([128, 512], F32) for _ in range(2)])
    ring3b = Ring("pb", [banks[i] for i in range(8)])

    def z_steps(c0, zkey, zt):
        def dma():
            sk, st = stg4.next()
            S.dma("sp", sk, [], [sk], st, w_in[:, c0:c0 + 128].rearrange("(c p) n -> p c n", p=128))
            return sk, st

        def cast(h):
            sk, st = h
            S.op("dve", [sk, "const"], [zkey], lambda: nc.vector.tensor_tensor(
                out=zt, in0=st, in1=ngT[:, 0:8].to_broadcast([128, 8, 128]), op=ALU.mult))
        return [(dma, cast)]

    def fc_steps(fc, wkey, wt):
        cols = slice(fc * 128, (fc + 1) * 128)
        steps = []
        for (src, nch, off, scaled) in ((w_pa, 4, 0, False), (w_in[:, C_GA:C_GA + 1024], 8, 4, True),
                                        (w_pb, 8, 12, False), (w_in[:, C_GB:C_GB + 1024], 8, 20, True)):
            def dma(src=src, nch=nch):
                sk, st = stg4.next()
                S.dma("sp", sk, [], [sk], st[:, 0:nch, :], src.rearrange("(c p) n -> p c n", p=128)[:, :, cols])
                return sk, st

            def cast(h, nch=nch, off=off, scaled=scaled):
                sk, st = h
                o = wt[:, off:off + nch, :]
                if scaled:
                    S.op("dve", [sk, "const"], [wkey], lambda: nc.vector.tensor_tensor(
                        out=o, in0=st[:, 0:nch, :], in1=ngT[:, 0:nch].to_broadcast([128, nch, 128]), op=ALU.mult))
                else:
                    S.op("act", [sk], [wkey], lambda: nc.scalar.activation(out=o, in_=st[:, 0:nch, :], func=AF.Copy))
            steps.append((dma, cast))
        return steps

    wtiles = [wfc_r.next() for _ in range(8)]
    zjobs = [(C_ZG + fc * 128, ubT, fc) for fc in range(8)] + [(C_ZA + fc * 128, uaT, fc) for fc in range(4)]
    ztiles = [zt_r.next() for _ in zjobs]
    pf.add(z_steps(zjobs[0][0], *ztiles[0]))
    pf.flush()
    for n, (c0, dstT, fc) in enumerate(zjobs):
        if n + 1 < len(zjobs):
            pf.add(z_steps(zjobs[n + 1][0], *ztiles[n + 1]))
        elif True:
            pf.add(fc_steps(0, *wtiles[0]))
        zk, zt = ztiles[n]
        for tg in range(4):
            tok = slice(tg * 512, (tg + 1) * 512)
            zb, zbank = ring3b.next()
            for c in range(8):
                S.op("pe", [zk], [zb], lambda: nc.tensor.matmul(
                    zbank, lhsT=zt[:, c, :], rhs=hTo[:, c, tok], start=(c == 0), stop=(c == 7)))
            szk, sz = sz_r.next()
            S.op("act", [zb], [szk], lambda: nc.scalar.activation(out=sz, in_=zbank, func=AF.Silu))
            S.op("dve", [szk], ["uT"], lambda: nc.vector.tensor_tensor(
                out=dstT[:, fc, tok], in0=dstT[:, fc, tok], in1=sz, op=ALU.mult))
            pf.pump(1)
    pf.flush()
    S.barrier()

    if debug:
        S.dma("sp", "dbg1", [], ["dbg"], dbg["ua"], uaT)
        S.dma("sp", "dbg2", [], ["dbg"], dbg["ub"], ubT)
        S.barrier()

    yT = hTh
    A4h = Arena(X3)
    Wo = A4h.alloc([128, 8, 1024], BF16)
    Wpg = A4h.alloc([128, 8, 1024], BF16)
    Wpl = A4h.alloc([128, 2, 1024], BF16)
    ring4 = Ring("pb", [banks[i] for i in range(8)])

    def big_steps(dst, key, src, nchunk, scale=None):
        steps = []
        for c in range(nchunk):
            def dma(c=c):
                sk, st = stg4.next()
                S.dma("sp", sk, [], [sk], st.rearrange("p a b -> p (a b)"), src[c * 128:(c + 1) * 128, :])
                return sk, st

            def cast(h, c=c):
                sk, st = h
                flat = st.rearrange("p a b -> p (a b)")
                if scale is None:
                    S.op("act", [sk], [key], lambda: nc.scalar.activation(out=dst[:, c, :], in_=flat, func=AF.Copy))
                else:
                    S.op("act", [sk, "const"], [key], lambda: nc.scalar.activation(
                        out=dst[:, c, :], in_=flat, func=AF.Copy, scale=scale[:, c:c + 1]))
            steps.append((dma, cast))
        return steps

    later = big_steps(Wo, "Wo", w_out, 8) + big_steps(Wpg, "Wpg", w_pg, 8, scale=pngT) + big_steps(Wpl, "Wpl", w_ple, 2)
    for fc in range(1, 8):
        pf.add(fc_steps(fc, *wtiles[fc]))
        pf.add(later[:3])
        later = later[3:]
    pf.add(later)
    for fc in range(8):
        wk, wt = wtiles[fc]
        for tg in range(4):
            tok = slice(tg * 512, (tg + 1) * 512)
            res = []
            for (poff, src, nk, goff) in ((0, uaT, 4, 4), (12, ubT, 8, 20)):
                yb, ybank = ring4.next()
                for c in range(nk):
                    S.op("pe", [wk], [yb], lambda: nc.tensor.matmul(
                        ybank, lhsT=wt[:, poff + c, :], rhs=src[:, c, tok], start=(c == 0), stop=(c == nk - 1)))
                gb2, gbank2 = ring4.next()
                for c in range(8):
                    S.op("pe", [wk], [gb2], lambda: nc.tensor.matmul(
                        gbank2, lhsT=wt[:, goff + c, :], rhs=hTo[:, c, tok], start=(c == 0), stop=(c == 7)))
                sk, sg = sg_r.next()
                S.op("act", [gb2], [sk], lambda: nc.scalar.activation(out=sg, in_=gbank2, func=AF.Sigmoid))
                res.append((yb, ybank, sk, sg))
            y1k, y1 = y1_r.next()
            S.op("dve", [res[0][0], res[0][2]], [y1k], lambda: nc.vector.tensor_tensor(
                out=y1, in0=res[0][1], in1=res[0][3], op=ALU.mult))
            S.op("dve", [res[1][0], res[1][2]], [res[1][2]], lambda: nc.vector.tensor_tensor(
                out=res[1][3], in0=res[1][1], in1=res[1][3], op=ALU.mult))
            S.op("dve", [y1k, res[1][2]], ["yT"], lambda: nc.vector.tensor_tensor(
                out=yT[:, fc, tok], in0=y1, in1=res[1][3], op=ALU.add))
            pf.pump(2)
    pf.flush()
    S.barrier()
    if debug:
        S.dma("sp", "dbg3", [], ["dbg"], dbg["y"], yT)
        S.barrier()

    if stop_after <= 4:
        return nc, S
    A5 = Arena(XBASE)
    load_w = make_loader(A5, name="stgq")
    pTb = A5.alloc([128, 2, T], BF16)
    xt5_r = Ring("xt5", [A5.alloc([128, D], F32) for _ in range(3)])
    x1_r = Ring("x1", [A5.alloc([128, D], F32) for _ in range(3)])
    xn5_r = Ring("xn5", [A5.alloc([128, D], BF16) for _ in range(3)])
    xnT_r = Ring("xnT", [A5.alloc([128, 8, 128], BF16) for _ in range(2)])
    sg5_r = Ring("sg5", [A5.alloc([128, D], F32) for _ in range(2)])
    o5_r = Ring("o5", [A5.alloc([128, D], F32) for _ in range(2)])
    junk5 = A5.alloc([128, D], BF16)
    ss5 = A5.alloc([128, NT], F32)
    mhalf5 = A5.alloc([128, 1], F32)
    S.op("pool", [], ["mh5"], lambda: nc.gpsimd.memset(mhalf5, -0.5))
    assert A5.top <= X3
    ring5 = Ring("pb", [banks[i] for i in range(8)])
    load_w(pTb, "pTb", pT_d, T, 2, q="pool")
    def stageA5(J):
        tok = slice(J * 128, (J + 1) * 128)
        xk, xt = xt5_r.next()
        S.dma("sp", xk, [], [xk], xt, xs[T + J * 128:T + (J + 1) * 128, :])
        x1k, x1 = x1_r.next()
        for half in range(2):
            hc = slice(half * 512, (half + 1) * 512)
            zb, zbank = ring5.next()
            for c in range(8):
                S.op("pe", ["Wo"], [zb], lambda: nc.tensor.matmul(
                    zbank, lhsT=yT[:, c, tok], rhs=Wo[:, c, hc], start=(c == 0), stop=(c == 7)))
            S.op("dve", [zb, xk], [x1k], lambda: nc.vector.tensor_tensor(out=x1[:, hc], in0=zbank, in1=xt[:, hc], op=ALU.add))
        return (x1k, x1)

    def stageA5b(J, st):
        x1k, x1 = st
        S.op("act", [x1k], ["junk5", "ss5_%d" % J], lambda: nc.scalar.activation(
            out=junk5, in_=x1, func=AF.Square, accum_out=ss5[:, J:J + 1]))
        S.op("pool", ["ss5_%d" % J], ["ss5_%d" % J], lambda: nc.gpsimd.tensor_scalar(
            out=ss5[:, J:J + 1], in0=ss5[:, J:J + 1], scalar1=1.0 / D, scalar2=EPS, op0=ALU.mult, op1=ALU.add))
        S.op("pool", ["ss5_%d" % J, "mh5"], ["ss5_%d" % J], lambda: nc.gpsimd.tensor_tensor(
            out=ss5[:, J:J + 1], in0=ss5[:, J:J + 1], in1=mhalf5, op=ALU.pow))
        nk, xn = xn5_r.next()
        S.op("dve", [x1k, "ss5_%d" % J], [nk], lambda: nc.vector.tensor_scalar(
            out=xn, in0=x1, scalar1=ss5[:, J:J + 1], scalar2=None, op0=ALU.mult))
        return (x1k, x1, nk, xn)

    def stageB5(J, st):
        x1k, x1, nk, xn = st
        tok = slice(J * 128, (J + 1) * 128)
        tb, tbank = ring5.next()
        tv = bf_view(tbank).rearrange("p (c t) -> p c t", t=128)
        for c in range(8):
            S.op("pe", [nk, "const"], [tb], lambda: nc.tensor.transpose(
                out=tv[:, c, :], in_=xn[:, c * 128:(c + 1) * 128], identity=ident))
        tk, xnT = xnT_r.next()
        S.op("act", [tb], [tk], lambda: nc.scalar.activation(out=xnT, in_=tv, func=AF.Copy))
        return (x1k, x1, tk, xnT)

    def stageB5b(J, st):
        x1k, x1, tk, xnT = st
        tok = slice(J * 128, (J + 1) * 128)
        sk, sg = sg5_r.next()
        ok, o5 = o5_r.next()
        pbs = []
        for half in range(2):
            hc = slice(half * 512, (half + 1) * 512)
            pb2, pbank2 = ring5.next()
            for c in range(2):
                S.op("pe", ["pTb", "Wpl"], [pb2], lambda: nc.tensor.matmul(
                    pbank2, lhsT=pTb[:, c, tok], rhs=Wpl[:, c, hc], start=(c == 0), stop=(c == 1)))
            pbs.append((pb2, pbank2))
        for half in range(2):
            hc = slice(half * 512, (half + 1) * 512)
            gb2, gbank2 = ring5.next()
            for c in range(8):
                S.op("pe", [tk, "Wpg"], [gb2], lambda: nc.tensor.matmul(
                    gbank2, lhsT=xnT[:, c, :], rhs=Wpg[:, c, hc], start=(c == 0), stop=(c == 7)))
            skh, okh = "%s_%d" % (sk, half), "%s_%d" % (ok, half)
            S.op("act", [gb2], [skh], lambda: nc.scalar.activation(out=sg[:, hc], in_=gbank2, func=AF.Sigmoid))
            pb2, pbank2 = pbs[half]
            S.op("dve", [pb2, skh], [skh], lambda: nc.vector.tensor_tensor(out=sg[:, hc], in0=pbank2, in1=sg[:, hc], op=ALU.mult))
            S.op("dve", [skh, x1k], [okh], lambda: nc.vector.tensor_tensor(out=o5[:, hc], in0=sg[:, hc], in1=x1[:, hc], op=ALU.add))
        S.dma("pool", ok, [ok + "_0", ok + "_1"], [ok + "_0", ok + "_1"], out_d[tok, :], o5)

    st5 = {0: stageA5b(0, stageA5(0))}
    for J in range(NT):
        a_next = stageA5(J + 1) if J + 1 < NT else None
        b_mid = stageB5(J, st5.pop(J))
        if a_next is not None:
            st5[J + 1] = stageA5b(J + 1, a_next)
        stageB5b(J, b_mid)
    S.barrier()
    return nc, S


def _host_consts(hf):
    p = np.arange(128)
    kp, qp = p[:, None], p[None, :]
    ident = np.eye(128, dtype=np.float32)
    m_prev = np.where(kp >= qp, 0.0, NEG).astype(np.float32)
    m_cur = np.where(kp <= qp, 0.0, NEG).astype(np.float32)
    m_halo = m_prev if hf == 1 else np.full((128, 128), NEG, np.float32)
    tri = (kp <= qp).astype(np.float32)
    cbf = np.concatenate([ident, m_prev, m_cur, m_halo, tri, m_prev, m_cur, m_prev, m_cur,
                          m_halo, m_cur, m_halo, m_cur], axis=1).astype(NPBF)
    return cbf, tri


_CACHE = {}


def _prep_inputs(x, p, positions, norm_g, w_in, qk_norm_q, qk_norm_k, gla_gate_w2, gla_gate_b,
                 gla_norm_g, w_att_proj, w_gla_proj, w_out, ple_norm_g, w_ple_gate, w_ple):
    blocks, NB = _blocks()
    half = 8
    inv = np.power(np.float32(500000.0), -np.arange(half, dtype=np.float32) * np.float32(2.0) / np.float32(16)).astype(np.float32)
    invf = np.ascontiguousarray(np.broadcast_to(inv[None, :], (128, 8))).astype(np.float32)
    w2aug = np.zeros((32, 512), np.float32)
    w2aug[0:16] = gla_gate_w2[0]
    w2aug[16] = gla_gate_b[0]
    shared = {
        "invf": invf,
        "ng": np.ascontiguousarray(norm_g[0].reshape(8, 128).T),
        "png": np.ascontiguousarray(ple_norm_g[0].reshape(8, 128).T),
        "gqb": np.ascontiguousarray(np.broadcast_to(qk_norm_q[0][None, :], (128, 64))),
        "gkb": np.ascontiguousarray(np.broadcast_to(qk_norm_k[0][None, :], (128, 64))),
        "gnb": np.ascontiguousarray(np.broadcast_to(gla_norm_g[0][None, :], (128, 256))),
        "w2aug": w2aug,
        "w_in": np.ascontiguousarray(w_in[0]), "w_pa": np.ascontiguousarray(w_att_proj[0]),
        "w_pb": np.ascontiguousarray(w_gla_proj[0]), "w_out": np.ascontiguousarray(w_out[0]),
        "w_pg": np.ascontiguousarray(w_ple_gate[0]), "w_ple": np.ascontiguousarray(w_ple[0]),
    }
    in_maps = []
    for core in range(8):
        b, hf = core // 2, core % 2
        cbf, tri = _host_consts(hf)
        xs = np.zeros((4096, D), np.float32)
        posl = np.zeros((4096,), np.int32)
        if hf == 1:
            xs[:] = x[b]
            posl[:] = positions[b]
        else:
            xs[T:] = x[b, :T]
            posl[T:] = positions[b, :T]
        posb = np.zeros((128, NB), np.int32)
        pp = np.arange(128)
        for d, gi in GROUPS:
            for (r, B, halo, bid) in blocks[d]:
                posb[:, bid] = posl[r + d * (128 * B + pp)]
        m = dict(shared)
        m.update({"xs": xs, "posb": posb, "pT": np.ascontiguousarray(p[0, b, hf * T:(hf + 1) * T, :].T),
                  "cbf": cbf, "trif": tri})
        in_maps.append(m)
    return in_maps


def kernel(**inputs):
    inputs = {k: np.asarray(v) for k, v in inputs.items()}
    in_maps = _prep_inputs(**inputs)
    if "nc" not in _CACHE:
        _CACHE["nc"] = build_program(False)[0]
    res = run_bass_kernel_spmd(_CACHE["nc"], in_maps, core_ids=list(range(8)))
    out = np.zeros((4, 4096, D), np.float32)
    for core in range(8):
        b, hf = core // 2, core % 2
        out[b, hf * T:(hf + 1) * T] = res.results[core]["out"]
    return out
```

```python
import numpy as np
import ml_dtypes
import concourse.bass as bass
import concourse.mybir as mybir
from concourse.bass_utils import run_bass_kernel_spmd

F32 = mybir.dt.float32
BF16 = mybir.dt.bfloat16
I32 = mybir.dt.int32
AF = mybir.ActivationFunctionType
ALU = mybir.AluOpType
AX = mybir.AxisListType
NPBF = ml_dtypes.bfloat16

D = 1024
T = 2048
NT = T // 128
EPS = 1e-6
C_QA, C_KA, C_VA, C_ZA, C_QG, C_KG, C_VG, C_GLR, C_ZG, C_GA, C_GB = (
    0, 1536, 3072, 4608, 5120, 5632, 6144, 7168, 7184, 8208, 9232)
GROUPS = ((16, 2), (4, 1), (1, 0))
NEG = -30000.0


class Sched:
    def __init__(self, nc):
        self.nc = nc
        self.E = {"pe": nc.tensor, "dve": nc.vector, "act": nc.scalar,
                  "pool": nc.gpsimd, "sp": nc.sync}
        self.sems, self.cnt, self.waited = {}, {}, {}
        self.lastw, self.readers = {}, {}
        self.ninst = 0
        self.nwait = 0
        for e in ("pe", "dve", "act", "pool"):
            self._sem("E_" + e)

    def _sem(self, key):
        if key not in self.sems:
            self.sems[key] = self.nc.alloc_semaphore("s_" + key)
            self.cnt[key] = 0
        return self.sems[key]

    def _need(self, eng, toks):
        best = {}
        for t in toks:
            if t is None:
                continue
            k, v = t
            if v > best.get(k, 0):
                best[k] = v
        for k, v in best.items():
            if eng == "pe" and k == "E_pe":
                continue
            if self.waited.get((eng, k), 0) >= v:
                continue
            self.E[eng].wait_ge(self.sems[k], v)
            self.waited[(eng, k)] = v
            self.nwait += 1

    def deps(self, eng, reads, writes):
        toks = []
        for r in reads:
            toks.append(self.lastw.get(r))
        for w in writes:
            toks.append(self.lastw.get(w))
            toks.extend(self.readers.get(w, ()))
        self._need(eng, toks)

    def commit(self, tok, reads, writes):
        for r in reads:
            self.readers.setdefault(r, []).append(tok)
        for w in writes:
            self.lastw[w] = tok
            self.readers[w] = []

    def op(self, eng, reads, writes, fn):
        self.deps(eng, reads, writes)
        inst = fn()
        k = "E_" + eng
        self.cnt[k] += 1
        inst.then_inc(self.sems[k], 1)
        self.commit((k, self.cnt[k]), reads, writes)
        self.ninst += 1

    def dma(self, q, slot, reads, writes, out, in_):
        self.deps(q, reads, writes)
        k = "D_" + slot
        self._sem(k)
        inst = self.E[q].dma_start(out=out, in_=in_)
        self.cnt[k] += 16
        inst.then_inc(self.sems[k], 16)
        self.commit((k, self.cnt[k]), reads, writes)
        self.ninst += 1

    def pe_drain(self):
        v = self.cnt["E_pe"]
        if v > self.waited.get(("pe", "E_pe"), 0):
            self.E["pe"].wait_ge(self.sems["E_pe"], v)
            self.waited[("pe", "E_pe")] = v
            self.nwait += 1

    def barrier(self):
        toks = [(k, v) for k, v in self.cnt.items() if v > 0]
        for e in ("pe", "dve", "act", "pool", "sp"):
            self._need(e, toks)
        self.lastw.clear()
        self.readers.clear()


class Ring:
    def __init__(self, name, aps):
        self.name, self.aps, self.i = name, aps, 0

    def next(self):
        j = self.i % len(self.aps)
        self.i += 1
        return "%s%d" % (self.name, j), self.aps[j]


def _blocks():
    out = {}
    bid = 0
    for d, gi in GROUPS:
        lst = []
        for r in range(d):
            for B in range(16 // d - 1, 32 // d):
                lst.append((r, B, B == 16 // d - 1, bid))
                bid += 1
        out[d] = lst
    return out, bid


def build_program(debug=False, stop_after=99):
    nc = bass.Bass("TRN2", target_bir_lowering=False)
    S = Sched(nc)
    blocks, NB = _blocks()

    def din(name, shape, dt):
        return nc.dram_tensor(name, list(shape), dt, kind="ExternalInput").ap()

    xs = din("xs", [4096, D], F32)
    posb = din("posb", [128, NB], I32)
    invf = din("invf", [128, 8], F32)
    pT_d = din("pT", [256, T], F32)
    consts_d = din("cbf", [128, 13 * 128], BF16)
    trif_d = din("trif", [128, 128], F32)
    ng_d = din("ng", [128, 8], F32)
    png_d = din("png", [128, 8], F32)
    gq_d = din("gqb", [128, 64], F32)
    gk_d = din("gkb", [128, 64], F32)
    gn_d = din("gnb", [128, 256], F32)
    w2_d = din("w2aug", [32, 512], F32)
    w_in = din("w_in", [D, 10256], F32)
    w_pa = din("w_pa", [512, D], F32)
    w_pb = din("w_pb", [D, D], F32)
    w_out = din("w_out", [D, D], F32)
    w_pg = din("w_pg", [D, D], F32)
    w_ple = din("w_ple", [256, D], F32)
    out_d = nc.dram_tensor("out", [T, D], F32, kind="ExternalOutput").ap()
    scr = {4: nc.dram_tensor("scr4", [T, 520], BF16, kind="Internal").ap(),
           16: nc.dram_tensor("scr16", [T, 520], BF16, kind="Internal").ap()}
    dbg = {}
    if debug:
        dbg["ua"] = nc.dram_tensor("dbg_ua", [128, 4, T], BF16, kind="ExternalOutput").ap()
        dbg["ub"] = nc.dram_tensor("dbg_ub", [128, 8, T], BF16, kind="ExternalOutput").ap()
        dbg["y"] = nc.dram_tensor("dbg_y", [128, 8, T], BF16, kind="ExternalOutput").ap()

    _CNT = [0]

    class Arena:
        def __init__(self, base):
            self.top = base
            self.n = 0

        def alloc(self, shape, dt):
            nbytes = int(np.prod(shape[1:])) * (4 if dt in (F32, I32) else 2)
            off = (self.top + 63) // 64 * 64
            self.top = off + nbytes
            self.n += 1
            assert self.top <= 229344, ("SBUF overflow", self.top)
            _CNT[0] += 1
            return nc.alloc_sbuf_tensor_at("sb%d" % _CNT[0], list(shape), dt, offset=off).ap()

    P = Arena(16512)
    cbf = P.alloc([128, 13 * 128], BF16)
    ident, m_prev, m_cur, m_halo, tri = (cbf[:, i * 128:(i + 1) * 128] for i in range(5))
    mask4 = cbf[:, 5 * 128:9 * 128]
    mask4h = cbf[:, 9 * 128:13 * 128]
    trif = P.alloc([128, 128], F32)
    ngT = P.alloc([128, 8], F32)
    pngT = P.alloc([128, 8], F32)
    gqb = P.alloc([128, 64], F32)
    gkb = P.alloc([128, 64], F32)
    gnb = P.alloc([128, 256], F32)
    w2aug = P.alloc([32, 512], F32)
    hTo = P.alloc([128, 8, T], BF16)
    hTh = P.alloc([128, 8, T], BF16)
    uaT = P.alloc([128, 4, T], BF16)
    XBASE = P.top

    banks = [nc.alloc_psum_tensor("bank%d" % i, [128, 512], F32).ap() for i in range(8)]

    def bf_view(bank):
        return bank.bitcast(BF16)

    for i, (dst, src) in enumerate(((cbf, consts_d), (trif, trif_d), (ngT, ng_d), (pngT, png_d), (gqb, gq_d),
                                    (gkb, gk_d), (gnb, gn_d), (w2aug, w2_d))):
        S.dma("sp", "c%d" % i, [], ["const"], dst, src)
    S.barrier()

    def make_loader(arena, width=1024, nslot=3, name="stg"):
        stg = Ring(name, [arena.alloc([128, width], F32) for _ in range(nslot)])

        def load_w(dst, key, src, ncols, nchunk, scale=None, eng="dve"):
            for c in range(nchunk):
                for c0 in range(0, ncols, width):
                    cw = min(width, ncols - c0)
                    sk, st = stg.next()
                    S.dma("sp", sk, [], [sk], st[:, 0:cw], src[c * 128:(c + 1) * 128, c0:c0 + cw])
                    o = dst[:, c, c0:c0 + cw]
                    if scale is None:
                        S.op(eng, [sk], [key], lambda: S.E[eng].tensor_copy(out=o, in_=st[:, 0:cw]))
                    else:
                        S.op(eng, [sk, "const"], [key], lambda: S.E[eng].tensor_scalar(
                            out=o, in0=st[:, 0:cw], scalar1=scale[:, c:c + 1], scalar2=None, op0=ALU.mult))
        def load_steps(dst, key, src, ncols, nchunk, scale=None, eng="dve", q="sp"):
            steps = []
            for c in range(nchunk):
                for c0 in range(0, ncols, width):
                    cw = min(width, ncols - c0)

                    def dma(c=c, c0=c0, cw=cw):
                        sk, st = stg.next()
                        S.dma(q, sk, [], [sk], st[:, 0:cw], src[c * 128:(c + 1) * 128, c0:c0 + cw])
                        return sk, st

                    def cast(h, c=c, c0=c0, cw=cw):
                        sk, st = h
                        o = dst[:, c, c0:c0 + cw]
                        if scale is None:
                            if eng == "act":
                                S.op("act", [sk], [key], lambda: nc.scalar.activation(out=o, in_=st[:, 0:cw], func=AF.Copy))
                            else:
                                S.op(eng, [sk], [key], lambda: S.E[eng].tensor_copy(out=o, in_=st[:, 0:cw]))
                        elif eng == "act":
                            S.op("act", [sk, "const"], [key], lambda: nc.scalar.activation(
                                out=o, in_=st[:, 0:cw], func=AF.Copy, scale=scale[:, c:c + 1]))
                        else:
                            S.op(eng, [sk, "const"], [key], lambda: S.E[eng].tensor_scalar(
                                out=o, in0=st[:, 0:cw], scalar1=scale[:, c:c + 1], scalar2=None, op0=ALU.mult))
                    steps.append((dma, cast))
            return steps
        load_w.steps = load_steps
        load_w.nslot = nslot
        return load_w

    class Prefetch:
        def __init__(self):
            self.q = []
            self.inflight = []

        def add(self, steps):
            self.q.extend(steps)

        def _casts(self):
            for cast, h in self.inflight:
                cast(h)
            self.inflight = []

        def pump(self, n=1):
            self._casts()
            for _ in range(min(n, len(self.q))):
                dma, cast = self.q.pop(0)
                self.inflight.append((cast, dma()))

        def flush(self, ring=2):
            while self.q or self.inflight:
                self.pump(ring)

    pf = Prefetch()
    AW = Arena(XBASE)
    wbuf = [AW.alloc([128, 8, 1536], BF16) for _ in range(2)]
    XW = AW.top

    def group_w_steps(loader, gidx, eng, q="sp"):
        d_, gi_ = GROUPS[gidx]
        b = wbuf[gidx % 2]
        st = []
        for (o0, c0) in ((0, C_QA + gi_ * 512), (512, C_KA + gi_ * 512), (1024, C_VA + gi_ * 512)):
            st += loader.steps(b[:, :, o0:o0 + 512], "W%d" % (gidx % 2), w_in[:, c0:c0 + 512], 512, 8, scale=ngT, eng=eng, q=q)
        return st

    AT = Arena(XW)
    cos2T = AT.alloc([128, NB, 16], F32)
    sin2T = AT.alloc([128, NB, 16], F32)
    XT = AT.top
    A1 = Arena(XT)
    load_w1 = make_loader(A1, 512, 3, name="stgp")
    pf.add(group_w_steps(load_w1, 0, "dve", q="pool"))
    xt_r = Ring("xt", [A1.alloc([128, D], F32) for _ in range(8)])
    xn_r = Ring("xn", [A1.alloc([128, D], BF16) for _ in range(4)])
    junk = A1.alloc([128, D], BF16)
    ss1 = A1.alloc([128, 32], F32)
    posi = A1.alloc([128, NB], I32)
    posf = A1.alloc([128, NB], F32)
    ang = A1.alloc([128, NB, 8], F32)
    ang2 = A1.alloc([128, NB, 8], F32)
    invt = A1.alloc([128, 8], F32)
    pring1 = Ring("pb", [(banks[i]) for i in range(4)])

    S.dma("sp", "posi", [], ["posi"], posi, posb)
    S.dma("sp", "invt", [], ["invt"], invt, invf)
    S.op("dve", ["posi"], ["posf"], lambda: nc.vector.tensor_copy(out=posf, in_=posi))
    S.op("dve", ["posf", "invt"], ["ang"], lambda: nc.vector.tensor_tensor(
        out=ang, in0=posf.to_broadcast([128, NB, 8]), in1=invt.unsqueeze(1).to_broadcast([128, NB, 8]), op=ALU.mult))
    PI = float(np.pi)
    MAGIC = 12582912.0
    C1 = 6.28125
    C2 = 2.0 * PI - C1
    kf = A1.alloc([128, NB, 8], F32)
    for (dst, shift, sgn) in ((sin2T, 0.0, -1.0), (cos2T, 0.5 * PI, 1.0)):
        S.op("dve", ["ang"], ["ang2"], lambda: nc.vector.tensor_scalar(
            out=ang2, in0=ang, scalar1=shift, scalar2=None, op0=ALU.add))
        S.op("dve", ["ang2"], ["kf"], lambda: nc.vector.tensor_scalar(
            out=kf, in0=ang2, scalar1=1.0 / (2.0 * PI), scalar2=MAGIC, op0=ALU.mult, op1=ALU.add))
        S.op("dve", ["kf"], ["kf"], lambda: nc.vector.tensor_scalar(
            out=kf, in0=kf, scalar1=-MAGIC, scalar2=None, op0=ALU.add))
        S.op("dve", ["kf", "ang2"], ["ang2"], lambda: nc.vector.scalar_tensor_tensor(
            out=ang2.rearrange("p a b -> p (a b)"), in0=kf.rearrange("p a b -> p (a b)"), scalar=-C1,
            in1=ang2.rearrange("p a b -> p (a b)"), op0=ALU.mult, op1=ALU.add))
        S.op("dve", ["kf", "ang2"], ["ang2"], lambda: nc.vector.scalar_tensor_tensor(
            out=ang2.rearrange("p a b -> p (a b)"), in0=kf.rearrange("p a b -> p (a b)"), scalar=-C2,
            in1=ang2.rearrange("p a b -> p (a b)"), op0=ALU.mult, op1=ALU.add))
        S.op("dve", ["ang2"], ["ang2"], lambda: nc.vector.tensor_scalar(
            out=ang2, in0=ang2, scalar1=-PI, scalar2=PI, op0=ALU.max, op1=ALU.min))
        S.op("act", ["ang2"], ["rot"], lambda: nc.scalar.activation(out=dst[:, :, 0:8], in_=ang2, func=AF.Sin, scale=sgn))
        S.op("act", ["ang2"], ["rot"], lambda: nc.scalar.activation(out=dst[:, :, 8:16], in_=ang2, func=AF.Sin))

    def stageA1(j):
        xk, xt = xt_r.next()
        S.dma("sp", xk, [], [xk], xt, xs[j * 128:(j + 1) * 128, :])
        S.op("act", [xk], ["junk", "ss1_%d" % j], lambda: nc.scalar.activation(
            out=junk, in_=xt, func=AF.Square, accum_out=ss1[:, j:j + 1]))
        S.op("act", ["ss1_%d" % j], ["ss1_%d" % j], lambda: nc.scalar.activation(
            out=ss1[:, j:j + 1], in_=ss1[:, j:j + 1], func=AF.Sqrt, scale=1.0 / D, bias=EPS))
        S.op("dve", ["ss1_%d" % j], ["ss1_%d" % j], lambda: nc.vector.reciprocal(out=ss1[:, j:j + 1], in_=ss1[:, j:j + 1]))
        nk, xn = xn_r.next()
        S.op("dve", [xk, "ss1_%d" % j], [nk], lambda: nc.vector.tensor_scalar(
            out=xn, in0=xt, scalar1=ss1[:, j:j + 1], scalar2=None, op0=ALU.mult))
        return nk, xn

    def stageB1(j, nk, xn):
        bk, bank = pring1.next()
        pv = bf_view(bank)[:, 0:1024].rearrange("p (c t) -> p c t", t=128)
        for c in range(8):
            S.op("pe", [nk, "const"], [bk], lambda: nc.tensor.transpose(
                out=pv[:, c, :], in_=xn[:, c * 128:(c + 1) * 128], identity=ident))
        dst = (hTh if j < 16 else hTo)[:, :, (j % 16) * 128:(j % 16 + 1) * 128]
        if j % 2 == 0:
            S.op("act", [bk], ["hT%d" % j], lambda: nc.scalar.activation(out=dst, in_=pv, func=AF.Copy))
        else:
            S.op("dve", [bk], ["hT%d" % j], lambda: nc.vector.tensor_copy(out=dst, in_=pv))

    st1 = {0: stageA1(0), 1: stageA1(1)}
    for j in range(32):
        if j + 2 < 32:
            st1[j + 2] = stageA1(j + 2)
        stageB1(j, *st1.pop(j))
        pf.pump(1)
    pf.flush()
    S.barrier()

    if stop_after <= 1:
        return nc, S

    def hsrc(L0, n, step, c):
        src, s = (hTh, L0) if L0 < T else (hTo, L0 - T)
        return src[:, c, s:s + (n - 1) * step + 1:step]

    A2 = Arena(XT)
    load_w = make_loader(A2, 512, 3)
    Wsel = {}
    ost_r = Ring("ost", [A2.alloc([128, 2, 260], BF16) for _ in range(3)])
    nat_r = {4: Ring("nat4_", [A2.alloc([128, 2, 260], BF16) for _ in range(4)]),
             16: Ring("nat16_", [A2.alloc([128, 2, 260], BF16) for _ in range(4)])}
    sqs_r = Ring("sqs", [A2.alloc([128, 512], F32) for _ in range(2)])
    ssq_r = Ring("ssq", [A2.alloc([128, 8], F32) for _ in range(4)])
    tmpn_r = Ring("tmpn", [A2.alloc([128, 8, 64], F32) for _ in range(4)])
    t16_r = Ring("t16", [A2.alloc([128, 8, 16], F32) for _ in range(2)])
    rtmp_r = Ring("rtmp", [A2.alloc([128, 2, 8, 16], F32) for _ in range(2)])
    qn_r = Ring("qn", [A2.alloc([128, 8, 64], BF16) for _ in range(3)])
    kn_r = Ring("kn", [A2.alloc([128, 8, 64], BF16) for _ in range(3)])
    qT_r = Ring("qT", [A2.alloc([128, 4, 128], BF16) for _ in range(4)])
    kT_r = Ring("kT", [A2.alloc([128, 4, 128], BF16) for _ in range(5)])
    V_r = Ring("V", [A2.alloc([128, 8, 65], BF16) for _ in range(7)])
    PT_r = Ring("PT", [A2.alloc([128, 512], BF16) for _ in range(4)])
    pmA = A2.alloc([128, 2, 260], F32)
    rl = A2.alloc([128, 8], F32)
    uab_r = Ring("uab", [A2.alloc([128, 8, 64], BF16) for _ in range(2)])
    pending_ua = []
    pring = Ring("pb", [banks[i] for i in range(3)])
    trbank = ("pb3", banks[3])
    sring = {16: Ring("sb", [banks[4], banks[5]]), 4: Ring("sb", [banks[4], banks[5]]), 1: Ring("sb", [banks[4], banks[5]])}
    PVb = [banks[6], banks[7]]

    for Vt0 in V_r.aps:
        S.op("pool", [], ["Vinit"], lambda: nc.gpsimd.memset(Vt0[:, :, 64:65], 1.0))

    def norm_early(ps_key, ps):
        sk, sqs = sqs_r.next()
        tk, raw = tmpn_r.next()
        S.op("act", [ps_key], [sk], lambda: nc.scalar.activation(out=sqs, in_=ps, func=AF.Square))
        S.op("act", [ps_key], [tk], lambda: nc.scalar.activation(
            out=raw.rearrange("p h e -> p (h e)"), in_=ps, func=AF.Copy))
        return (sk, sqs, tk, raw)

    def norm_steps(early, gb, bid, out_ring):
        sk, sqs, tk, raw = early
        qk_, ssq = ssq_r.next()
        ok, on = out_ring.next()
        k16, t16 = t16_r.next()
        rk, rt = rtmp_r.next()
        c2 = cos2T[:, bid, :].unsqueeze(1).to_broadcast([128, 8, 16])
        s2 = sin2T[:, bid, :].rearrange("p (two e) -> p two e", two=2).unsqueeze(1).to_broadcast([128, 8, 2, 8])
        sw = t16.rearrange("p h (two e) -> p h two e", two=2)[:, :, ::-1, :]
        steps = [
            lambda: S.op("dve", [sk], [qk_], lambda: nc.vector.tensor_reduce(
                out=ssq, in_=sqs.rearrange("p (h e) -> p h e", e=64), axis=AX.X, op=ALU.add)),
            lambda: S.op("act", [qk_], [qk_], lambda: nc.scalar.activation(
                out=ssq, in_=ssq, func=AF.Ln, scale=1.0 / 64, bias=EPS)),
            lambda: S.op("act", [qk_], [qk_], lambda: nc.scalar.activation(out=ssq, in_=ssq, func=AF.Exp, scale=-0.5)),
            lambda: S.op("dve", [tk, qk_], [tk], lambda: nc.vector.tensor_tensor(
                out=raw, in0=raw, in1=ssq.to_broadcast([128, 8, 64]), op=ALU.mult)),
            lambda: S.op("dve", [tk], [k16], lambda: nc.vector.tensor_tensor(
                out=t16, in0=raw[:, :, 0:16], in1=gb[:, 0:16].unsqueeze(1).to_broadcast([128, 8, 16]), op=ALU.mult)),
            lambda: S.op("dve", [tk], [ok], lambda: nc.vector.tensor_tensor(
                out=on[:, :, 16:64], in0=raw[:, :, 16:64], in1=gb[:, 16:64].unsqueeze(1).to_broadcast([128, 8, 48]), op=ALU.mult)),
        ]
        steps.append(lambda: S.op("dve", [k16, "rot"], [rk], lambda: nc.vector.tensor_tensor(
            out=rt[:, 0], in0=t16, in1=c2, op=ALU.mult)))
        steps.append(lambda: S.op("dve", [k16, "rot"], [rk], lambda: nc.vector.tensor_tensor(
            out=rt[:, 1].rearrange("p h (two e) -> p h two e", two=2), in0=sw, in1=s2, op=ALU.mult)))
        steps.append(lambda: S.op("dve", [rk], [ok], lambda: nc.vector.tensor_tensor(
            out=on[:, :, 0:16], in0=rt[:, 0], in1=rt[:, 1], op=ALU.add)))
        return ok, on, steps

    def stageA(d, blk):
        r, B, halo, bid = blk
        L0 = r + d * 128 * B
        info = {"blk": blk, "halo": halo}
        kb, kbank = pring.next()
        for c in range(8):
            S.op("pe", [Wsel["key"]], [kb], lambda: nc.tensor.matmul(
                kbank, lhsT=hsrc(L0, 128, d, c), rhs=Wsel["k"][:, c, :], start=(c == 0), stop=(c == 7)))
        info["kearly"] = norm_early(kb, kbank)
        vb, vbank = pring.next()
        for c in range(8):
            S.op("pe", [Wsel["key"]], [vb], lambda: nc.tensor.matmul(
                vbank, lhsT=hsrc(L0, 128, d, c), rhs=Wsel["v"][:, c, :], start=(c == 0), stop=(c == 7)))
        Vk, Vt = V_r.next()
        info["V"] = (Vk, Vt)
        S.op("act", [vb, "Vinit"], [Vk], lambda: nc.scalar.activation(
            out=Vt[:, :, 0:64], in_=vbank.rearrange("p (h e) -> p h e", e=64), func=AF.Copy))
        late = []
        info["late"] = late
        if not halo:
            qb, qbank = pring.next()
            for c in range(8):
                S.op("pe", [Wsel["key"]], [qb], lambda: nc.tensor.matmul(
                    qbank, lhsT=hsrc(L0, 128, d, c), rhs=Wsel["q"][:, c, :], start=(c == 0), stop=(c == 7)))
            late.append(lambda: info.__setitem__("qearly", norm_early(qb, qbank)))
            if d == 1:
                J0 = B - 16
                for dd in (4, 16):
                    nk_, nt_ = nat_r[dd].next()
                    S.dma("sp", nk_, [], [nk_], nt_.rearrange("p b e -> p (b e)"), scr[dd][J0 * 128:(J0 + 1) * 128, :])
                    info["nat%d" % dd] = (nk_, nt_)
        return info

    def stageA2(info):
        bid = info["blk"][3]
        knk, kn, ksteps = norm_steps(info["kearly"], gkb, bid, kn_r)
        info["kn"] = (knk, kn)
        qsteps = []
        if "qearly" in info:
            qnk, qn, qsteps = norm_steps(info["qearly"], gqb, bid, qn_r)
            info["qn"] = (qnk, qn)
        for i in range(max(len(ksteps), len(qsteps))):
            if i < len(ksteps):
                ksteps[i]()
            if i < len(qsteps):
                qsteps[i]()

    def stageB(info):
        bk, bank = trbank
        pvw = bf_view(bank).rearrange("p (w c t) -> p w c t", w=2, t=128)
        todo = []
        for w, name, ring, eng in ((0, "kn", kT_r, "dve"), (1, "qn", qT_r, "act")):
            if name not in info:
                continue
            sk, src = info[name]
            flat = src.rearrange("p h e -> p (h e)")
            for c in range(4):
                S.op("pe", [sk, "const"], [bk], lambda: nc.tensor.transpose(
                    out=pvw[:, w, c, :], in_=flat[:, c * 128:(c + 1) * 128], identity=ident))
            todo.append((w, ring, eng))
        info["evac_todo"] = (bk, pvw, todo)

    def stageB_evac(info):
        bk, pvw, todo = info.pop("evac_todo")
        for w, ring, eng in todo:
            ok, o = ring.next()
            if eng == "act":
                S.op("act", [bk], [ok, bk], lambda: nc.scalar.activation(out=o, in_=pvw[:, w], func=AF.Copy))
            else:
                S.op("dve", [bk], [ok, bk], lambda: nc.vector.tensor_copy(out=o, in_=pvw[:, w]))
            info["kT" if w == 0 else "qT"] = (ok, o)

    def stageC(d, info, prev, mid=None, late=None):
        r, B, halo, bid = info["blk"]
        Bo = B - 16 // d
        J = Bo
        kTk, kT = info["kT"]
        qTk, qT = info["qT"]
        Vk, Vt = info["V"]
        pkTk, pkT = prev["kT"]
        pVk, pV = prev["V"]
        msk = mask4h if prev["halo"] else mask4

        def scores2(st):
            xb, xbank = sring[d].next()
            yb, ybank = sring[d].next()
            S.op("pe", ["const"], [xb], lambda: nc.tensor.matmul(
                xbank, lhsT=ident, rhs=msk, start=True, stop=False, skip_group_check=True))
            S.op("pe", ["const"], [yb], lambda: nc.tensor.matmul(
                ybank, lhsT=ident, rhs=msk, start=True, stop=False, skip_group_check=True))
            for pi in range(2):
                hp = 2 * st + pi
                for (kk, ktile, kkey) in ((0, pkT, pkTk), (1, kT, kTk)):
                    reg = slice((2 * pi + kk) * 128, (2 * pi + kk + 1) * 128)
                    last = (pi == 1 and kk == 1)
                    S.op("pe", [kkey, qTk], [xb], lambda: nc.tensor.matmul(
                        xbank[:, reg], lhsT=ktile[0:64, hp, :], rhs=qT[0:64, hp, :], start=False, stop=last,
                        skip_group_check=True))
                    S.op("pe", [kkey, qTk], [yb], lambda: nc.tensor.matmul(
                        ybank[:, reg], lhsT=ktile[64:128, hp, :], rhs=qT[64:128, hp, :], start=False, stop=last,
                        skip_group_check=True))
            res = []
            for (bk_, bank_) in ((xb, xbank), (yb, ybank)):
                Pk, PT = PT_r.next()
                S.op("act", [bk_], [Pk], lambda: nc.scalar.activation(out=PT, in_=bank_, func=AF.Exp, scale=0.125))
                res.append((Pk, PT))
            return res

        def pv2(st, which, Pk, PT):
            for pi in range(2):
                h = 2 * (2 * st + pi) + which
                reg = PVb[h // 4][:, (h % 4) * 65:(h % 4) * 65 + 65]
                S.op("pe", [Pk, pVk], ["PV%d" % (h // 4)], lambda: nc.tensor.matmul(
                    reg, lhsT=PT[:, (2 * pi) * 128:(2 * pi + 1) * 128], rhs=pV[:, h, :], start=True, stop=False))
                S.op("pe", [Pk, Vk], ["PV%d" % (h // 4)], lambda: nc.tensor.matmul(
                    reg, lhsT=PT[:, (2 * pi + 1) * 128:(2 * pi + 2) * 128], rhs=Vt[:, h, :], start=False, stop=True))

        r0 = scores2(0)
        if mid is not None:
            mid()
        pv2(0, 0, *r0[0])
        r1 = scores2(1)
        pv2(0, 1, *r0[1])
        pv2(1, 0, *r1[0])
        pv2(1, 1, *r1[1])
        if late is not None:
            late()
        if d != 1:
            ok_, ot = ost_r.next()
            S.op("dve", ["PV0"], [ok_], lambda: nc.vector.tensor_copy(out=ot[:, 0, :], in_=PVb[0][:, 0:260]))
            S.op("act", ["PV1"], [ok_], lambda: nc.scalar.activation(out=ot[:, 1, :], in_=PVb[1][:, 0:260], func=AF.Copy))
            row0 = (512 * Bo + r) if d == 4 else r
            S.dma("sp", ok_, [ok_], [ok_], scr[d][row0:row0 + 127 * d + 1:d, :], ot.rearrange("p b e -> p (b e)"))
            return
        for hb in range(2):
            S.op("dve", ["PV%d" % hb, info["nat4"][0]], ["pmA"], lambda: nc.vector.tensor_tensor(
                out=pmA[:, hb, :], in0=PVb[hb][:, 0:260], in1=info["nat4"][1][:, hb, :], op=ALU.add))
            S.op("dve", ["pmA", info["nat16"][0]], ["pmA"], lambda: nc.vector.tensor_tensor(
                out=pmA[:, hb, :], in0=pmA[:, hb, :], in1=info["nat16"][1][:, hb, :], op=ALU.add))
        pm4 = pmA.rearrange("p b (h e) -> p (b h) e", e=65)
        S.op("dve", ["pmA"], ["rl"], lambda: nc.vector.reciprocal(out=rl, in_=pm4[:, :, 64]))
        uk, uab = uab_r.next()
        S.op("dve", ["pmA", "rl"], [uk], lambda: nc.vector.tensor_tensor(
            out=uab, in0=pm4[:, :, 0:64], in1=rl.to_broadcast([128, 8, 64]), op=ALU.mult))

        def ua_transposes():
            bk, bank = trbank
            pv = bf_view(bank)[:, 0:512].rearrange("p (c t) -> p c t", t=128)
            flat = uab.rearrange("p h e -> p (h e)")
            for c in range(4):
                S.op("pe", [uk, "const"], [bk], lambda: nc.tensor.transpose(
                    out=pv[:, c, :], in_=flat[:, c * 128:(c + 1) * 128], identity=ident))
            S.op("dve", [bk], ["uaT"], lambda: nc.vector.tensor_copy(out=uaT[:, :, J * 128:(J + 1) * 128], in_=pv))
        pending_ua.append(ua_transposes)

    for gidx, (d, gi) in enumerate(GROUPS):
        pf.flush()
        wb = wbuf[gidx % 2]
        Wsel.update(key="W%d" % (gidx % 2), q=wb[:, :, 0:512], k=wb[:, :, 512:1024], v=wb[:, :, 1024:1536])
        if d == 1:
            S._need("sp", [(k, v) for k, v in S.cnt.items() if k.startswith("D_ost") and v > 0])
        if gidx + 1 < len(GROUPS):
            pf.add(group_w_steps(load_w, gidx + 1, "act"))
        else:
            ob = wbuf[(gidx + 1) % 2]
            okey = "W%d" % ((gidx + 1) % 2)
            pf.add(load_w.steps(ob[:, :, 0:512], okey, w_in[:, C_QG:C_QG + 512], 512, 8, scale=ngT, eng="act"))
            pf.add(load_w.steps(ob[:, :, 512:1024], okey, w_in[:, C_KG:C_KG + 512], 512, 8, scale=ngT, eng="act"))
            pf.add(load_w.steps(ob[:, :, 1024:1040], okey, w_in[:, C_GLR:C_GLR + 16], 16, 8, scale=ngT, eng="act"))
        L = blocks[d]
        infos = [None] * len(L)
        for i in range(len(L) + 3):
            if i < len(L):
                infos[i] = stageA(d, L[i])
            while pending_ua:
                pending_ua.pop(0)()
            hasB = 0 <= i - 2 < len(L)

            def mid(i=i, hasB=hasB):
                if hasB:
                    stageB(infos[i - 2])
                if i < len(L):
                    for f in infos[i].pop("late"):
                        f()
            doE = (lambda: stageB_evac(infos[i - 2])) if hasB else None
            if 0 <= i - 3 < len(L) and not infos[i - 3]["halo"]:
                stageC(d, infos[i - 3], infos[i - 4], mid=mid, late=doE)
            else:
                mid()
                if doE is not None:
                    doE()
            if i < len(L):
                stageA2(infos[i])
            pf.pump(2)
    while pending_ua:
        pending_ua.pop(0)()
    S.barrier()

    if stop_after <= 2:
        return nc, S
    pf.flush()
    A3 = Arena(XW)
    ubT = A3.alloc([128, 8, T], BF16)
    X3 = A3.top
    load_w = make_loader(A3, 1024, 2)
    _gb = wbuf[len(GROUPS) % 2]
    _vb = wbuf[(len(GROUPS) + 1) % 2]
    KG, KV = "W%d" % (len(GROUPS) % 2), "W%d" % ((len(GROUPS) + 1) % 2)
    Wqg = _gb[:, :, 0:512]
    Wkg = _gb[:, :, 512:1024]
    Wgl = _gb[:, :, 1024:1040]
    Wvg = _vb[:, :, 0:1024]
    Sf = A3.alloc([128, 4, 256], F32)
    Sb_ = A3.alloc([128, 4, 256], BF16)
    glr_r = Ring("glr", [A3.alloc([32, 128], F32) for _ in range(3)])
    ef = A3.alloc([128, 512], F32)
    spf = ef
    lgh_r = Ring("lgh", [_vb[:, c, 1024:1536] for c in (0, 1, 2)])
    lgl_r = Ring("lgl", [_vb[:, c, 1024:1536] for c in (3, 4, 5)])
    ek_r = Ring("ek", [A3.alloc([128, 4, 128], F32) for _ in range(2)])
    eq_r = Ring("eq", [A3.alloc([128, 4, 128], F32) for _ in range(2)])
    ktT_r = Ring("ktT", [_gb[:, 4 * i:4 * i + 4, 1040:1168] for i in range(2)])
    qtT_r = Ring("qtT", [_gb[:, 4 * i:4 * i + 4, 1168:1296] for i in range(2)])
    kt_r = Ring("kt", [_gb[:, 4 * i:4 * i + 4, 1296:1424] for i in range(2)])
    v_r = Ring("vg", [A3.alloc([128, 1024], BF16) for _ in range(2)])
    am_r = Ring("am", [_vb[:, c, 1024:1536].rearrange("p (h t) -> p h t", t=128) for c in (6, 7)])
    stmp_r = Ring("stmp", [A3.alloc([128, 256], F32) for _ in range(2)])
    ss3 = A3.alloc([128, 4], F32)
    junk3 = A3.alloc([128, 256], BF16)
    ubb_r = Ring("ubb", [A3.alloc([128, 1024], BF16) for _ in range(2)])
    ring3 = Ring("pb", [banks[i] for i in range(6)])
    oring3 = Ring("ob", [banks[6], banks[7]])

    S.op("dve", [], ["Sf%d" % h for h in range(4)], lambda: nc.vector.memset(Sf, 0.0))
    S.op("dve", [], ["Sb%d" % h for h in range(4)], lambda: nc.vector.memset(Sb_, 0.0))
    for a in glr_r.aps:
        S.op("dve", [], ["glrinit"], lambda: nc.vector.memset(a, 1.0))
    load_w(Wvg, KV, w_in[:, C_VG:C_VG + 1024], 1024, 8, scale=ngT)

    pre3 = {}

    pre3a = {}

    def gate_a(j):
        L0g = j * 128
        gb_, gbank = ring3.next()
        for c in range(8):
            S.op("pe", [KG], [gb_], lambda: nc.tensor.matmul(
                gbank[0:16, 0:128], lhsT=Wgl[:, c, :], rhs=hsrc(L0g, 128, 1, c), start=(c == 0), stop=(c == 7)))
        gk_, glr = glr_r.next()
        S.op("act", [gb_, "glrinit"], [gk_], lambda: nc.scalar.activation(out=glr[0:16, :], in_=gbank[0:16, 0:128], func=AF.Copy))
        pre3a[j] = (gk_, glr)

    def gate_b(j):
        gk_, glr = pre3a.pop(j)
        lb, lbank = ring3.next()
        S.op("pe", [gk_, "const"], [lb], lambda: nc.tensor.matmul(lbank, lhsT=glr, rhs=w2aug, start=True, stop=True))
        S.op("act", [lb], ["ef"], lambda: nc.scalar.activation(out=ef, in_=lbank, func=AF.Exp, scale=-1.0))
        S.op("act", ["ef"], ["ef"], lambda: nc.scalar.activation(out=spf, in_=ef, func=AF.Ln, bias=1.0))
        hk, lgh = lgh_r.next()
        lk, lgl = lgl_r.next()
        S.op("dve", ["ef"], [hk], lambda: nc.vector.tensor_scalar(
            out=lgh, in0=spf, scalar1=-1.0 / 16, scalar2=None, op0=ALU.mult))
        S.op("dve", ["ef", hk], [lk], lambda: nc.vector.scalar_tensor_tensor(
            out=lgl, in0=spf, scalar=-1.0 / 16, in1=lgh, op0=ALU.mult, op1=ALU.subtract))
        pre3[j] = (hk, lgh, lk, lgl)

    def stageA3(j):
        own = j >= 16
        info = {}
        L0 = j * 128
        J = j - 16
        hs = [hsrc(L0, 128, 1, c) for c in range(8)]
        hk, lgh, lk, lgl = pre3.pop(j)
        if j + 1 < 32:
            gate_a(j + 1)
        kb, kbank = ring3.next()
        for h in range(4):
            for c in range(8):
                S.op("pe", [KG], [kb], lambda: nc.tensor.matmul(
                    kbank[:, h * 128:(h + 1) * 128], lhsT=Wkg[:, c, h * 128:(h + 1) * 128], rhs=hs[c],
                    start=(c == 0), stop=(c == 7)))
        if own:
            qb, qbank = ring3.next()
            for h in range(4):
                for c in range(8):
                    S.op("pe", [KG], [qb], lambda: nc.tensor.matmul(
                        qbank[:, h * 128:(h + 1) * 128], lhsT=Wqg[:, c, h * 128:(h + 1) * 128], rhs=hs[c],
                        start=(c == 0), stop=(c == 7)))
        cb_, cbank = ring3.next()
        for h in range(4):
            reg = cbank[:, h * 128:(h + 1) * 128]
            S.op("pe", [hk, "const"], [cb_], lambda: nc.tensor.matmul(
                reg, lhsT=lgh[:, h * 128:(h + 1) * 128], rhs=tri, start=True, stop=False))
            S.op("pe", [lk, "const"], [cb_], lambda: nc.tensor.matmul(
                reg, lhsT=lgl[:, h * 128:(h + 1) * 128], rhs=tri, start=False, stop=True))
        ekk, ek = ek_r.next()
        eqk, eq = eq_r.next()
        c4 = cbank.rearrange("p (h t) -> p h t", t=128)
        S.op("act", [cb_], [ekk], lambda: nc.scalar.activation(out=ek, in_=c4, func=AF.Exp, scale=-1.0))
        S.op("act", [cb_], [eqk], lambda: nc.scalar.activation(out=eq, in_=c4, func=AF.Exp))
        vk, vg = v_r.next()
        for half in range(2):
            vb, vbank = ring3.next()
            for c in range(8):
                S.op("pe", [KV], [vb], lambda: nc.tensor.matmul(
                    vbank, lhsT=hs[c], rhs=Wvg[:, c, half * 512:(half + 1) * 512], start=(c == 0), stop=(c == 7)))
            S.op("act", [vb], [vk], lambda: nc.scalar.activation(
                out=vg[:, half * 512:(half + 1) * 512], in_=vbank, func=AF.Copy))
        ktk, ktT = ktT_r.next()
        S.op("dve", [kb, ekk], [ktk], lambda: nc.vector.tensor_tensor(
            out=ktT, in0=kbank.rearrange("p (h t) -> p h t", t=128), in1=ek, op=ALU.mult))
        if own:
            qtk, qtT = qtT_r.next()
            S.op("dve", [qb, eqk], [qtk], lambda: nc.vector.scalar_tensor_tensor(
                out=qtT, in0=qbank.rearrange("p (h t) -> p h t", t=128), scalar=128.0 ** -0.5,
                in1=eq, op0=ALU.mult, op1=ALU.mult))
        tb, tbank = ring3.next()
        tv = bf_view(tbank)[:, 0:512].rearrange("p (h t) -> p h t", t=128)
        for h in range(4):
            S.op("pe", [ktk, "const"], [tb], lambda: nc.tensor.transpose(out=tv[:, h, :], in_=ktT[:, h, :], identity=ident))
        ktok, kt = kt_r.next()
        S.op("dve", [tb], [ktok], lambda: nc.vector.tensor_copy(out=kt, in_=tv))
        if own:
            ab, abank = ring3.next()
            for h in range(4):
                S.op("pe", [ktk, qtk], [ab], lambda: nc.tensor.matmul(
                    abank[:, h * 128:(h + 1) * 128], lhsT=ktT[:, h, :], rhs=qtT[:, h, :], start=True, stop=True))
            amk, am = am_r.next()
            S.op("dve", [ab, "const"], [amk], lambda: nc.vector.tensor_tensor(
                out=am, in0=abank.rearrange("p (h t) -> p h t", t=128),
                in1=trif.unsqueeze(1).to_broadcast([128, 4, 128]), op=ALU.mult))
        if j + 1 < 32:
            gate_b(j + 1)
        info.update(dict(ktk=ktk, ktT=ktT, vk=vk, vg=vg, ktok=ktok, kt=kt, eqk=eqk, eq=eq))
        if own:
            info.update(dict(qtk=qtk, qtT=qtT, amk=amk, am=am))
        return info

    def stageB3(j, info):
        own = j >= 16
        J = j - 16
        ktk, ktT, vk, vg, ktok, kt, eqk, eq = (info[n] for n in ("ktk", "ktT", "vk", "vg", "ktok", "kt", "eqk", "eq"))
        if own:
            qtk, qtT, amk, am = (info[n] for n in ("qtk", "qtT", "amk", "am"))
            obanks = []
            for hp in range(2):
                ob, obank = oring3.next()
                obanks.append((ob, obank))
                for hh in range(2):
                    h = 2 * hp + hh
                    reg = obank[:, hh * 256:(hh + 1) * 256]
                    S.op("pe", [qtk, "Sb%d" % h], [ob], lambda: nc.tensor.matmul(
                        reg, lhsT=qtT[:, h, :], rhs=Sb_[:, h, :], start=True, stop=False))
                    S.op("pe", [amk, vk], [ob], lambda: nc.tensor.matmul(
                        reg, lhsT=am[:, h, :], rhs=vg[:, h * 256:(h + 1) * 256], start=False, stop=True))
        if j < 31:
            for hp in range(2):
                db, dbank = ring3.next()
                for hh in range(2):
                    h = 2 * hp + hh
                    S.op("pe", [ktok, vk], [db], lambda: nc.tensor.matmul(
                        dbank[:, hh * 256:(hh + 1) * 256], lhsT=kt[:, h, :], rhs=vg[:, h * 256:(h + 1) * 256],
                        start=True, stop=True))
                for hh in range(2):
                    h = 2 * hp + hh
                    dec = eq[:, h, 127:128]
                    stk, stmp = stmp_r.next()
                    S.op("dve", [db, "Sf%d" % h], [stk], lambda: nc.vector.tensor_tensor(
                        out=stmp, in0=dbank[:, hh * 256:(hh + 1) * 256], in1=Sf[:, h, :], op=ALU.add))
                    S.op("act", [stk, eqk], ["Sb%d" % h], lambda: nc.scalar.activation(
                        out=Sb_[:, h, :], in_=stmp, func=AF.Copy, scale=dec))
                    S.op("dve", [stk, eqk], ["Sf%d" % h], lambda: nc.vector.tensor_scalar(
                        out=Sf[:, h, :], in0=stmp, scalar1=dec, scalar2=None, op0=ALU.mult))
        if own:
            ubk, ubb = ubb_r.next()
            for hp in range(2):
                ob, obank = obanks[hp]
                for hh in range(2):
                    h = 2 * hp + hh
                    S.op("act", [ob], ["junk3", "ss3"], lambda: nc.scalar.activation(
                        out=junk3, in_=obank[:, hh * 256:(hh + 1) * 256], func=AF.Square, accum_out=ss3[:, h:h + 1]))
            S.op("act", ["ss3"], ["ss3"], lambda: nc.scalar.activation(out=ss3, in_=ss3, func=AF.Ln, scale=1.0 / 256, bias=EPS))
            S.op("act", ["ss3"], ["ss3"], lambda: nc.scalar.activation(out=ss3, in_=ss3, func=AF.Exp, scale=-0.5))
            for hp in range(2):
                ob, obank = obanks[hp]
                for hh in range(2):
                    h = 2 * hp + hh
                    S.op("dve", [ob, "ss3", "const"], [ubk], lambda: nc.vector.scalar_tensor_tensor(
                        out=ubb[:, h * 256:(h + 1) * 256], in0=obank[:, hh * 256:(hh + 1) * 256], scalar=ss3[:, h:h + 1],
                        in1=gnb, op0=ALU.mult, op1=ALU.mult))
            info["ubb"] = (ubk, ubb)

    def stageC3(j, info):
        J = j - 16
        ubk, ubb = info["ubb"]
        ub_, ubank = ring3.next()
        uv = bf_view(ubank).rearrange("p (c t) -> p c t", t=128)
        for c in range(8):
            S.op("pe", [ubk, "const"], [ub_], lambda: nc.tensor.transpose(
                out=uv[:, c, :], in_=ubb[:, c * 128:(c + 1) * 128], identity=ident))
        S.op("act", [ub_], ["ubT"], lambda: nc.scalar.activation(out=ubT[:, :, J * 128:(J + 1) * 128], in_=uv, func=AF.Copy))
    infos3 = {}
    gate_a(0)
    gate_b(0)
    infos3[0] = stageA3(0)
    for j in range(32):
        stageB3(j, infos3[j])
        if j - 1 >= 16:
            stageC3(j - 1, infos3.pop(j - 1))
        if j + 1 < 32:
            infos3[j + 1] = stageA3(j + 1)
    stageC3(31, infos3.pop(31))
    S.barrier()

    if stop_after <= 3:
        return nc, S
    A4 = Arena(XBASE)
    stg4 = Ring("stg4_", [A4.alloc([128, 8, 128], F32) for _ in range(4)])
    wfc_r = Ring("wfc", [A4.alloc([128, 28, 128], BF16) for _ in range(3)])
    sg_r = Ring("sg", [A4.alloc([128, 512], F32) for _ in range(2)])
    y1_r = Ring("y1", [A4.alloc([128, 512], F32) for _ in range(2)])
    assert A4.top <= XW
    A3b = Arena(X3)
    zt_r = Ring("zt", [A3b.alloc([128, 8, 128], BF16) for _ in range(3)])
    sz_r = Ring("sz", [A3b.alloc([128, 512], F32) for _ in range(2)])
    ring3b = Ring("pb", [banks[i] for i in range(8)])

    def z_steps(c0, zkey, zt):
        def dma():
            sk, st = stg4.next()
            S.dma("sp", sk, [], [sk], st, w_in[:, c0:c0 + 128].rearrange("(c p) n -> p c n", p=128))
            return sk, st

        def cast(h):
            sk, st = h
            S.op("dve", [sk, "const"], [zkey], lambda: nc.vector.tensor_tensor(
                out=zt, in0=st, in1=ngT[:, 0:8].to_broadcast([128, 8, 128]), op=ALU.mult))
        return [(dma, cast)]

    def fc_steps(fc, wkey, wt):
        cols = slice(fc * 128, (fc + 1) * 128)
        steps = []
        for (src, nch, off, scaled) in ((w_pa, 4, 0, False), (w_in[:, C_GA:C_GA + 1024], 8, 4, True),
                                        (w_pb, 8, 12, False), (w_in[:, C_GB:C_GB + 1024], 8, 20, True)):
            def dma(src=src, nch=nch):
                sk, st = stg4.next()
                S.dma("sp", sk, [], [sk], st[:, 0:nch, :], src.rearrange("(c p) n -> p c n", p=128)[:, :, cols])
                return sk, st

            def cast(h, nch=nch, off=off, scaled=scaled):
                sk, st = h
                o = wt[:, off:off + nch, :]
                if scaled:
                    S.op("dve", [sk, "const"], [wkey], lambda: nc.vector.tensor_tensor(
                        out=o, in0=st[:, 0:nch, :], in1=ngT[:, 0:nch].to_broadcast([128, nch, 128]), op=ALU.mult))
                else:
                    S.op("act", [sk], [wkey], lambda: nc.scalar.activation(out=o, in_=st[:, 0:nch, :], func=AF.Copy))
            steps.append((dma, cast))
        return steps

    wtiles = [wfc_r.next() for _ in range(8)]
    zjobs = [(C_ZG + fc * 128, ubT, fc) for fc in range(8)] + [(C_ZA + fc * 128, uaT, fc) for fc in range(4)]
    ztiles = [zt_r.next() for _ in zjobs]
    pf.add(z_steps(zjobs[0][0], *ztiles[0]))
    pf.flush()
    for n, (c0, dstT, fc) in enumerate(zjobs):
        if n + 1 < len(zjobs):
            pf.add(z_steps(zjobs[n + 1][0], *ztiles[n + 1]))
        elif True:
            pf.add(fc_steps(0, *wtiles[0]))
        zk, zt = ztiles[n]
        for tg in range(4):
            tok = slice(tg * 512, (tg + 1) * 512)
            zb, zbank = ring3b.next()
            for c in range(8):
                S.op("pe", [zk], [zb], lambda: nc.tensor.matmul(
                    zbank, lhsT=zt[:, c, :], rhs=hTo[:, c, tok], start=(c == 0), stop=(c == 7)))
            szk, sz = sz_r.next()
            S.op("act", [zb], [szk], lambda: nc.scalar.activation(out=sz, in_=zbank, func=AF.Silu))
            S.op("dve", [szk], ["uT"], lambda: nc.vector.tensor_tensor(
                out=dstT[:, fc, tok], in0=dstT[:, fc, tok], in1=sz, op=ALU.mult))
            pf.pump(1)
    pf.flush()
    S.barrier()

    if debug:
        S.dma("sp", "dbg1", [], ["dbg"], dbg["ua"], uaT)
        S.dma("sp", "dbg2", [], ["dbg"], dbg["ub"], ubT)
        S.barrier()

    yT = hTh
    A4h = Arena(X3)
    Wo = A4h.alloc([128, 8, 1024], BF16)
    Wpg = A4h.alloc([128, 8, 1024], BF16)
    Wpl = A4h.alloc([128, 2, 1024], BF16)
    ring4 = Ring("pb", [banks[i] for i in range(8)])

    def big_steps(dst, key, src, nchunk, scale=None):
        steps = []
        for c in range(nchunk):
            def dma(c=c):
                sk, st = stg4.next()
                S.dma("sp", sk, [], [sk], st.rearrange("p a b -> p (a b)"), src[c * 128:(c + 1) * 128, :])
                return sk, st

            def cast(h, c=c):
                sk, st = h
                flat = st.rearrange("p a b -> p (a b)")
                if scale is None:
                    S.op("act", [sk], [key], lambda: nc.scalar.activation(out=dst[:, c, :], in_=flat, func=AF.Copy))
                else:
                    S.op("act", [sk, "const"], [key], lambda: nc.scalar.activation(
                        out=dst[:, c, :], in_=flat, func=AF.Copy, scale=scale[:, c:c + 1]))
            steps.append((dma, cast))
        return steps

    later = big_steps(Wo, "Wo", w_out, 8) + big_steps(Wpg, "Wpg", w_pg, 8, scale=pngT) + big_steps(Wpl, "Wpl", w_ple, 2)
    for fc in range(1, 8):
        pf.add(fc_steps(fc, *wtiles[fc]))
        pf.add(later[:3])
        later = later[3:]
    pf.add(later)
    for fc in range(8):
        wk, wt = wtiles[fc]
        for tg in range(4):
            tok = slice(tg * 512, (tg + 1) * 512)
            res = []
            for (poff, src, nk, goff) in ((0, uaT, 4, 4), (12, ubT, 8, 20)):
                yb, ybank = ring4.next()
                for c in range(nk):
                    S.op("pe", [wk], [yb], lambda: nc.tensor.matmul(
                        ybank, lhsT=wt[:, poff + c, :], rhs=src[:, c, tok], start=(c == 0), stop=(c == nk - 1)))
                gb2, gbank2 = ring4.next()
                for c in range(8):
                    S.op("pe", [wk], [gb2], lambda: nc.tensor.matmul(
                        gbank2, lhsT=wt[:, goff + c, :], rhs=hTo[:, c, tok], start=(c == 0), stop=(c == 7)))
                sk, sg = sg_r.next()
                S.op("act", [gb2], [sk], lambda: nc.scalar.activation(out=sg, in_=gbank2, func=AF.Sigmoid))
                res.append((yb, ybank, sk, sg))
            y1k, y1 = y1_r.next()
            S.op("dve", [res[0][0], res[0][2]], [y1k], lambda: nc.vector.tensor_tensor(
                out=y1, in0=res[0][1], in1=res[0][3], op=ALU.mult))
            S.op("dve", [res[1][0], res[1][2]], [res[1][2]], lambda: nc.vector.tensor_tensor(
                out=res[1][3], in0=res[1][1], in1=res[1][3], op=ALU.mult))
            S.op("dve", [y1k, res[1][2]], ["yT"], lambda: nc.vector.tensor_tensor(
                out=yT[:, fc, tok], in0=y1, in1=res[1][3], op=ALU.add))
            pf.pump(2)
    pf.flush()
    S.barrier()
    if debug:
        S.dma("sp", "dbg3", [], ["dbg"], dbg["y"], yT)
        S.barrier()

    if stop_after <= 4:
        return nc, S
    A5 = Arena(XBASE)
    load_w = make_loader(A5)
    pTb = A5.alloc([128, 2, T], BF16)
    xt5_r = Ring("xt5", [A5.alloc([128, D], F32) for _ in range(3)])
    x1_r = Ring("x1", [A5.alloc([128, D], F32) for _ in range(3)])
    xn5_r = Ring("xn5", [A5.alloc([128, D], BF16) for _ in range(3)])
    xnT_r = Ring("xnT", [A5.alloc([128, 8, 128], BF16) for _ in range(2)])
    sg5_r = Ring("sg5", [A5.alloc([128, D], F32) for _ in range(2)])
    o5_r = Ring("o5", [A5.alloc([128, D], F32) for _ in range(2)])
    junk5 = A5.alloc([128, D], BF16)
    ss5 = A5.alloc([128, NT], F32)
    mhalf5 = A5.alloc([128, 1], F32)
    S.op("pool", [], ["mh5"], lambda: nc.gpsimd.memset(mhalf5, -0.5))
    assert A5.top <= X3
    ring5 = Ring("pb", [banks[i] for i in range(8)])
    load_w(pTb, "pTb", pT_d, T, 2)
    def stageA5(J):
        tok = slice(J * 128, (J + 1) * 128)
        xk, xt = xt5_r.next()
        S.dma("sp", xk, [], [xk], xt, xs[T + J * 128:T + (J + 1) * 128, :])
        x1k, x1 = x1_r.next()
        for half in range(2):
            hc = slice(half * 512, (half + 1) * 512)
            zb, zbank = ring5.next()
            for c in range(8):
                S.op("pe", ["Wo"], [zb], lambda: nc.tensor.matmul(
                    zbank, lhsT=yT[:, c, tok], rhs=Wo[:, c, hc], start=(c == 0), stop=(c == 7)))
            S.op("dve", [zb, xk], [x1k], lambda: nc.vector.tensor_tensor(out=x1[:, hc], in0=zbank, in1=xt[:, hc], op=ALU.add))
        return (x1k, x1)

    def stageA5b(J, st):
        x1k, x1 = st
        S.op("act", [x1k], ["junk5", "ss5_%d" % J], lambda: nc.scalar.activation(
            out=junk5, in_=x1, func=AF.Square, accum_out=ss5[:, J:J + 1]))
        S.op("pool", ["ss5_%d" % J], ["ss5_%d" % J], lambda: nc.gpsimd.tensor_scalar(
            out=ss5[:, J:J + 1], in0=ss5[:, J:J + 1], scalar1=1.0 / D, scalar2=EPS, op0=ALU.mult, op1=ALU.add))
        S.op("pool", ["ss5_%d" % J, "mh5"], ["ss5_%d" % J], lambda: nc.gpsimd.tensor_tensor(
            out=ss5[:, J:J + 1], in0=ss5[:, J:J + 1], in1=mhalf5, op=ALU.pow))
        nk, xn = xn5_r.next()
        S.op("dve", [x1k, "ss5_%d" % J], [nk], lambda: nc.vector.tensor_scalar(
            out=xn, in0=x1, scalar1=ss5[:, J:J + 1], scalar2=None, op0=ALU.mult))
        return (x1k, x1, nk, xn)

    def stageB5(J, st):
        x1k, x1, nk, xn = st
        tok = slice(J * 128, (J + 1) * 128)
        tb, tbank = ring5.next()
        tv = bf_view(tbank).rearrange("p (c t) -> p c t", t=128)
        for c in range(8):
            S.op("pe", [nk, "const"], [tb], lambda: nc.tensor.transpose(
                out=tv[:, c, :], in_=xn[:, c * 128:(c + 1) * 128], identity=ident))
        tk, xnT = xnT_r.next()
        S.op("act", [tb], [tk], lambda: nc.scalar.activation(out=xnT, in_=tv, func=AF.Copy))
        return (x1k, x1, tk, xnT)

    def stageB5b(J, st):
        x1k, x1, tk, xnT = st
        tok = slice(J * 128, (J + 1) * 128)
        sk, sg = sg5_r.next()
        ok, o5 = o5_r.next()
        pbs = []
        for half in range(2):
            hc = slice(half * 512, (half + 1) * 512)
            pb2, pbank2 = ring5.next()
            for c in range(2):
                S.op("pe", ["pTb", "Wpl"], [pb2], lambda: nc.tensor.matmul(
                    pbank2, lhsT=pTb[:, c, tok], rhs=Wpl[:, c, hc], start=(c == 0), stop=(c == 1)))
            pbs.append((pb2, pbank2))
        for half in range(2):
            hc = slice(half * 512, (half + 1) * 512)
            gb2, gbank2 = ring5.next()
            for c in range(8):
                S.op("pe", [tk, "Wpg"], [gb2], lambda: nc.tensor.matmul(
                    gbank2, lhsT=xnT[:, c, :], rhs=Wpg[:, c, hc], start=(c == 0), stop=(c == 7)))
            skh, okh = "%s_%d" % (sk, half), "%s_%d" % (ok, half)
            S.op("act", [gb2], [skh], lambda: nc.scalar.activation(out=sg[:, hc], in_=gbank2, func=AF.Sigmoid))
            pb2, pbank2 = pbs[half]
            S.op("dve", [pb2, skh], [skh], lambda: nc.vector.tensor_tensor(out=sg[:, hc], in0=pbank2, in1=sg[:, hc], op=ALU.mult))
            S.op("dve", [skh, x1k], [okh], lambda: nc.vector.tensor_tensor(out=o5[:, hc], in0=sg[:, hc], in1=x1[:, hc], op=ALU.add))
        S.dma("pool", ok, [ok + "_0", ok + "_1"], [ok + "_0", ok + "_1"], out_d[tok, :], o5)

    st5 = {0: stageA5b(0, stageA5(0))}
    for J in range(NT):
        a_next = stageA5(J + 1) if J + 1 < NT else None
        b_mid = stageB5(J, st5.pop(J))
        if a_next is not None:
            st5[J + 1] = stageA5b(J + 1, a_next)
        stageB5b(J, b_mid)
    S.barrier()
    return nc, S


def _host_consts(hf):
    p = np.arange(128)
    kp, qp = p[:, None], p[None, :]
    ident = np.eye(128, dtype=np.float32)
    m_prev = np.where(kp >= qp, 0.0, NEG).astype(np.float32)
    m_cur = np.where(kp <= qp, 0.0, NEG).astype(np.float32)
    m_halo = m_prev if hf == 1 else np.full((128, 128), NEG, np.float32)
    tri = (kp <= qp).astype(np.float32)
    cbf = np.concatenate([ident, m_prev, m_cur, m_halo, tri, m_prev, m_cur, m_prev, m_cur,
                          m_halo, m_cur, m_halo, m_cur], axis=1).astype(NPBF)
    return cbf, tri


_CACHE = {}


def _prep_inputs(x, p, positions, norm_g, w_in, qk_norm_q, qk_norm_k, gla_gate_w2, gla_gate_b,
                 gla_norm_g, w_att_proj, w_gla_proj, w_out, ple_norm_g, w_ple_gate, w_ple):
    blocks, NB = _blocks()
    half = 8
    inv = np.power(np.float32(500000.0), -np.arange(half, dtype=np.float32) * np.float32(2.0) / np.float32(16)).astype(np.float32)
    invf = np.ascontiguousarray(np.broadcast_to(inv[None, :], (128, 8))).astype(np.float32)
    w2aug = np.zeros((32, 512), np.float32)
    w2aug[0:16] = gla_gate_w2[0]
    w2aug[16] = gla_gate_b[0]
    shared = {
        "invf": invf,
        "ng": np.ascontiguousarray(norm_g[0].reshape(8, 128).T),
        "png": np.ascontiguousarray(ple_norm_g[0].reshape(8, 128).T),
        "gqb": np.ascontiguousarray(np.broadcast_to(qk_norm_q[0][None, :], (128, 64))),
        "gkb": np.ascontiguousarray(np.broadcast_to(qk_norm_k[0][None, :], (128, 64))),
        "gnb": np.ascontiguousarray(np.broadcast_to(gla_norm_g[0][None, :], (128, 256))),
        "w2aug": w2aug,
        "w_in": np.ascontiguousarray(w_in[0]), "w_pa": np.ascontiguousarray(w_att_proj[0]),
        "w_pb": np.ascontiguousarray(w_gla_proj[0]), "w_out": np.ascontiguousarray(w_out[0]),
        "w_pg": np.ascontiguousarray(w_ple_gate[0]), "w_ple": np.ascontiguousarray(w_ple[0]),
    }
    in_maps = []
    for core in range(8):
        b, hf = core // 2, core % 2
        cbf, tri = _host_consts(hf)
        xs = np.zeros((4096, D), np.float32)
        posl = np.zeros((4096,), np.int32)
        if hf == 1:
            xs[:] = x[b]
            posl[:] = positions[b]
        else:
            xs[T:] = x[b, :T]
            posl[T:] = positions[b, :T]
        posb = np.zeros((128, NB), np.int32)
        pp = np.arange(128)
        for d, gi in GROUPS:
            for (r, B, halo, bid) in blocks[d]:
                posb[:, bid] = posl[r + d * (128 * B + pp)]
        m = dict(shared)
        m.update({"xs": xs, "posb": posb, "pT": np.ascontiguousarray(p[0, b, hf * T:(hf + 1) * T, :].T),
                  "cbf": cbf, "trif": tri})
        in_maps.append(m)
    return in_maps


def kernel(**inputs):
    inputs = {k: np.asarray(v) for k, v in inputs.items()}
    in_maps = _prep_inputs(**inputs)
    if "nc" not in _CACHE:
        _CACHE["nc"] = build_program(False)[0]
    res = run_bass_kernel_spmd(_CACHE["nc"], in_maps, core_ids=list(range(8)))
    out = np.zeros((4, 4096, D), np.float32)
    for core in range(8):
        b, hf = core // 2, core % 2
        out[b, hf * T:(hf + 1) * T] = res.results[core]["out"]
    return out
```
